# Optimizing a Trainium2 kernel written in Bass

```python
import jax
import jax.numpy as jnp
from jax import lax
import numpy as np

D_MODEL = 1024
BATCH = 4
SEQ = 8192
DEPTH = 2

CTX_LEN = 256
GRID_W = 64
N_BRANCH = 3
BRANCH_W = 512
MLA_HEADS = 8
MLA_NOPE = 64
MLA_ROPE = 32
MLA_QK = MLA_NOPE + MLA_ROPE
MLA_V = BRANCH_W // MLA_HEADS
MLA_Q_LORA = 256
MLA_KV_LORA = 128
GLA_HEADS = 4
GLA_DK = (D_MODEL // 2) // GLA_HEADS
GLA_DV = BRANCH_W // GLA_HEADS
GLA_GATE_RANK = 16
GLA_GATE_NORMALIZER = 16.0
RET_HEADS = 4
RET_DK = BRANCH_W // RET_HEADS
RET_DV = BRANCH_W // RET_HEADS
D_FF = 2816
CONV_W = 3
CHUNK = 64
Q_BLOCK = 128
ROPE_THETA = 10000.0
RET_THETA = 10000.0
EPS = 1e-6
F32 = jnp.float32

IN_LAYOUT = (
    ('mla_q', MLA_Q_LORA),
    ('mla_kv', MLA_KV_LORA),
    ('mla_kr', MLA_ROPE),
    ('gla_q', GLA_HEADS * GLA_DK),
    ('gla_k', GLA_HEADS * GLA_DK),
    ('gla_v', GLA_HEADS * GLA_DV),
    ('gla_g', GLA_HEADS * GLA_DV),
    ('gla_rf', GLA_GATE_RANK),
    ('gla_rb', GLA_GATE_RANK),
    ('ret_q', RET_HEADS * RET_DK),
    ('ret_k', RET_HEADS * RET_DK),
    ('ret_v', RET_HEADS * RET_DV),
    ('ret_g', RET_HEADS * RET_DV),
    ('gate_mla', D_MODEL),
    ('gate_gla', D_MODEL),
    ('gate_ret', D_MODEL),
)
N_IN = sum(width for _, width in IN_LAYOUT)
CTX_SIDE = ('mla_kv', 'mla_kr', 'gla_k', 'gla_v', 'gla_rf', 'gla_rb', 'ret_k', 'ret_v')
QUERY_SIDE = ('mla_q', 'gla_q', 'gla_g', 'ret_q', 'ret_g', 'gate_mla', 'gate_gla', 'gate_ret')

kernel_name = 'hybrid_mla_gla_retention_dit_block'


def rms_norm(x, w=None):
    x32 = x.astype(F32)
    y = x32 * lax.rsqrt(jnp.mean(x32 * x32, axis=-1, keepdims=True) + EPS)
    if w is not None:
        y = y * w.astype(F32)
    return y.astype(x.dtype)


def modulate(x, shift, scale):
    return x * (1 + scale) + shift


def split_heads(t, n_heads):
    b, s, _ = t.shape
    return t.reshape(b, s, n_heads, -1).transpose(0, 2, 1, 3)


def merge_heads(t):
    b, h, s, d = t.shape
    return t.transpose(0, 2, 1, 3).reshape(b, s, h * d)


def in_proj(a, w, names):
    out, start = {}, 0
    for name, width in IN_LAYOUT:
        if name in names:
            out[name] = a @ w[:, start:start + width]
        start += width
    return out


def rope_tables(pos, dim, theta):
    inv = theta ** (-jnp.arange(dim // 2, dtype=F32) * 2.0 / dim)
    ang = pos.astype(F32)[:, None] * inv[None, :]
    return jnp.cos(ang), jnp.sin(ang)


def retention_tables(pos):
    inv = 1.0 / (RET_THETA ** jnp.linspace(0.0, 1.0, RET_DK // 2, dtype=F32))
    ang = pos.astype(F32)[:, None] * inv[None, :]
    return jnp.cos(ang), jnp.sin(ang)


def rotate_half(x, cos, sin):
    n = x.shape[-1] // 2
    x1, x2 = x[..., :n], x[..., n:]
    return jnp.concatenate([x1 * cos - x2 * sin, x1 * sin + x2 * cos], axis=-1).astype(x.dtype)


def axial_rope(x, tabs):
    cos_r, sin_r, cos_c, sin_c = tabs
    half = x.shape[-1] // 2
    return jnp.concatenate([rotate_half(x[..., :half], cos_r, sin_r),
                            rotate_half(x[..., half:], cos_c, sin_c)], axis=-1)


def mla_rope(t, tabs):
    if tabs is None:
        return t
    return jnp.concatenate([t[..., :MLA_NOPE], axial_rope(t[..., MLA_NOPE:], tabs)], axis=-1)


def mla_queries(cq, q_norm_a, w_qb, q_norm, tabs):
    q = split_heads(rms_norm(cq, q_norm_a) @ w_qb, MLA_HEADS)
    return mla_rope(rms_norm(q, q_norm), tabs)


def mla_keys_values(ckv, kr, kv_norm_a, w_kvb, k_norm, tabs):
    kv = split_heads(rms_norm(ckv, kv_norm_a) @ w_kvb, MLA_HEADS)
    b, h, t, _ = kv.shape
    k_rope = jnp.broadcast_to(kr[:, None], (b, h, t, MLA_ROPE))
    k = rms_norm(jnp.concatenate([kv[..., :MLA_NOPE], k_rope], axis=-1), k_norm)
    return mla_rope(k, tabs), kv[..., MLA_NOPE:]


def attend(q, k, v):
    s = jnp.einsum('bhqd,bhkd->bhqk', q, k, preferred_element_type=F32) * (MLA_QK ** -0.5)
    p = jax.nn.softmax(s, axis=-1).astype(v.dtype)
    return jnp.einsum('bhqk,bhkd->bhqd', p, v)


def blocked_attend(q, k, v):
    b, h, s, d = q.shape
    qb = jnp.moveaxis(q.reshape(b, h, s // Q_BLOCK, Q_BLOCK, d), 2, 0)
    ob = lax.map(lambda qi: attend(qi, k, v), qb)
    return jnp.moveaxis(ob, 0, 2).reshape(b, h, s, v.shape[-1])


def chunk_mask(inclusive):
    idx = jnp.arange(CHUNK)
    return idx[:, None] >= idx[None, :] if inclusive else idx[:, None] > idx[None, :]


def gla_chunk_scan(q, k, v, log_a, s0, inclusive):
    b_, h, t, dk = k.shape
    dv = v.shape[-1]
    n = t // CHUNK
    kc = k.reshape(b_, h, n, CHUNK, dk).astype(F32)
    vc = v.reshape(b_, h, n, CHUNK, dv).astype(F32)
    cum = jnp.cumsum(log_a.reshape(b_, h, n, CHUNK, dk).astype(F32), axis=3)
    cum_last = cum[:, :, :, -1]
    inc = jnp.einsum('bhnjd,bhnjv->bhndv', kc * jnp.exp(cum_last[:, :, :, None] - cum), vc)

    def step(s, xs):
        decay, u = xs
        return decay[..., None] * s + u, s

    s_final, s_start = lax.scan(step, s0, (jnp.moveaxis(jnp.exp(cum_last), 2, 0), jnp.moveaxis(inc, 2, 0)))
    if q is None:
        return None, s_final
    s_start = jnp.moveaxis(s_start, 0, 2)
    q_dec = q.reshape(b_, h, n, CHUNK, dk).astype(F32) * jnp.exp(cum)
    att = jnp.einsum('bhnid,bhnjd->bhnij', q_dec, kc * jnp.exp(-cum))
    att = jnp.where(chunk_mask(inclusive), att, 0.0)
    o = jnp.einsum('bhnij,bhnjv->bhniv', att, vc) + jnp.einsum('bhnid,bhndv->bhniv', q_dec, s_start)
    return o.reshape(b_, h, t, dv).astype(v.dtype), s_final


def ret_chunk_scan(q, k, v, log_g, s0, inclusive):
    b_, h, t, dk = k.shape
    dv = v.shape[-1]
    n = t // CHUNK
    idx = jnp.arange(CHUNK, dtype=F32)
    lg = log_g.astype(F32)
    kc = k.reshape(b_, h, n, CHUNK, dk).astype(F32)
    vc = v.reshape(b_, h, n, CHUNK, dv).astype(F32)
    zeta = jnp.exp((CHUNK - 1 - idx)[None, :] * lg[:, None])
    inc = jnp.einsum('bhnjd,bhnjv->bhndv', kc * zeta[None, :, None, :, None], vc)
    g_chunk = jnp.exp(CHUNK * lg)[None, :, None, None]

    def step(s, u):
        return g_chunk * s + u, s

    s_final, s_start = lax.scan(step, s0, jnp.moveaxis(inc, 2, 0))
    if q is None:
        return None, s_final
    s_start = jnp.moveaxis(s_start, 0, 2)
    mask = chunk_mask(inclusive)
    rel = jnp.where(mask, idx[:, None] - idx[None, :], 0.0)
    dmat = jnp.where(mask[None], jnp.exp(rel[None] * lg[:, None, None]), 0.0)
    xi = jnp.exp((idx + 1.0)[None, :] * lg[:, None])
    qc = q.reshape(b_, h, n, CHUNK, dk).astype(F32)
    att = jnp.einsum('bhnid,bhnjd->bhnij', qc, kc) * dmat[None, :, None]
    o = (jnp.einsum('bhnij,bhnjv->bhniv', att, vc)
         + jnp.einsum('bhnid,bhndv->bhniv', qc, s_start) * xi[None, :, None, :, None])
    return o.reshape(b_, h, t, dv).astype(v.dtype), s_final


def scan_both_directions(chunk_fn, q, k, v, dec_f, dec_b, s_f, s_b, per_token_decay):
    rev = lambda t: None if t is None else jnp.flip(t, axis=2)
    o_f, s_f = chunk_fn(q, k, v, dec_f, s_f, True)
    o_b, s_b = chunk_fn(rev(q), rev(k), rev(v), rev(dec_b) if per_token_decay else dec_b, s_b, False)
    o = None if q is None else o_f + rev(o_b)
    return o, s_f, s_b


def gla_log_decay(r, w2, b):
    return jax.nn.log_sigmoid((r @ w2 + b).astype(F32)) / GLA_GATE_NORMALIZER


def gated_head_norm(o, g, w):
    return merge_heads(rms_norm(o, w)).astype(g.dtype) * jax.nn.silu(g)


def gated_merge(ys, gs, b_gate, w_branch, w_out):
    out = None
    for n in range(N_BRANCH):
        term = jax.nn.sigmoid(gs[n] + b_gate[n]) * (ys[n] @ w_branch[n])
        out = term if out is None else out + term
    return out @ w_out


def token_mixers(a, ac, need_ctx, lat_tabs, ret_lat_tabs, ret_ctx_tabs, w_in, b_gate,
                 mla_q_norm_a, mla_w_qb, mla_kv_norm_a, mla_w_kvb, mla_q_norm, mla_k_norm,
                 gla_w_gk2, gla_b_gk, gla_o_norm, ret_decay, w_branch, w_out):
    bsz = a.shape[0]
    p = in_proj(a, w_in, CTX_SIDE + QUERY_SIDE)
    pc = in_proj(ac, w_in, CTX_SIDE + QUERY_SIDE if need_ctx else CTX_SIDE)

    k_c, v_c = mla_keys_values(pc['mla_kv'], pc['mla_kr'], mla_kv_norm_a, mla_w_kvb, mla_k_norm, None)
    k_l, v_l = mla_keys_values(p['mla_kv'], p['mla_kr'], mla_kv_norm_a, mla_w_kvb, mla_k_norm, lat_tabs)
    q_l = mla_queries(p['mla_q'], mla_q_norm_a, mla_w_qb, mla_q_norm, lat_tabs)
    y_mla = merge_heads(blocked_attend(q_l, jnp.concatenate([k_c, k_l], axis=2),
                                       jnp.concatenate([v_c, v_l], axis=2)))

    def gla_inputs(z, with_q):
        q = split_heads(z['gla_q'], GLA_HEADS) * (GLA_DK ** -0.5) if with_q else None
        k = split_heads(z['gla_k'], GLA_HEADS)
        v = split_heads(z['gla_v'], GLA_HEADS)
        la_f = split_heads(gla_log_decay(z['gla_rf'], gla_w_gk2[0], gla_b_gk[0]), GLA_HEADS)
        la_b = split_heads(gla_log_decay(z['gla_rb'], gla_w_gk2[1], gla_b_gk[1]), GLA_HEADS)
        return q, k, v, la_f, la_b

    zg = jnp.zeros((bsz, GLA_HEADS, GLA_DK, GLA_DV), F32)
    o_gc, sg_f, sg_b = scan_both_directions(gla_chunk_scan, *gla_inputs(pc, need_ctx), zg, zg, True)
    o_gl, _, _ = scan_both_directions(gla_chunk_scan, *gla_inputs(p, True), sg_f, sg_b, True)
    y_gla = gated_head_norm(o_gl, p['gla_g'], gla_o_norm)

    log_g = -jnp.exp(ret_decay.astype(F32))

    def ret_inputs(z, tabs, with_q):
        q = rotate_half(split_heads(z['ret_q'], RET_HEADS), *tabs) if with_q else None
        k = rotate_half(split_heads(z['ret_k'], RET_HEADS), *tabs) * (RET_DK ** -0.5)
        v = split_heads(z['ret_v'], RET_HEADS)
        return q, k, v

    zr = jnp.zeros((bsz, RET_HEADS, RET_DK, RET_DV), F32)
    o_rc, sr_f, sr_b = scan_both_directions(ret_chunk_scan, *ret_inputs(pc, ret_ctx_tabs, need_ctx),
                                            log_g[0], log_g[1], zr, zr, False)
    o_rl, _, _ = scan_both_directions(ret_chunk_scan, *ret_inputs(p, ret_lat_tabs, True),
                                      log_g[0], log_g[1], sr_f, sr_b, False)
    y_ret = gated_head_norm(o_rl, p['ret_g'], None)

    y = gated_merge((y_mla, y_gla, y_ret), (p['gate_mla'], p['gate_gla'], p['gate_ret']),
                    b_gate, w_branch, w_out)
    if not need_ctx:
        return y, None
    q_c = mla_queries(pc['mla_q'], mla_q_norm_a, mla_w_qb, mla_q_norm, None)
    y_c = gated_merge((merge_heads(attend(q_c, k_c, v_c)),
                       gated_head_norm(o_gc, pc['gla_g'], gla_o_norm),
                       gated_head_norm(o_rc, pc['ret_g'], None)),
                      (pc['gate_mla'], pc['gate_gla'], pc['gate_ret']), b_gate, w_branch, w_out)
    return y, y_c


def conv_ffn(a, w_in, w_dw, b_dw, w_out):
    gate = a @ w_in[:, :D_FF]
    up = a @ w_in[:, D_FF:]
    gate = lax.conv_general_dilated(gate, w_dw[:, None, :], window_strides=(1,),
                                    padding=[(CONV_W // 2, CONV_W // 2)],
                                    dimension_numbers=('NWC', 'WIO', 'NWC'),
                                    feature_group_count=D_FF) + b_dw
    return (jax.nn.gelu(gate) * up) @ w_out


def setup_inputs(seed: int = 0) -> dict:
    key = jax.random.key(seed)
    ks = iter(jax.random.split(key, 32))

    def nrm(shape, scale):
        return scale * jax.random.normal(next(ks), shape, F32)

    def gain(shape):
        return 1.0 + nrm(shape, 0.02)

    L, D = DEPTH, D_MODEL
    ret_base = jnp.log(-jnp.log1p(-(2.0 ** (-5.0 - jnp.arange(RET_HEADS, dtype=F32)))))
    return {
        'x': nrm((BATCH, SEQ, D), 1.0),
        'c': nrm((BATCH, D), 1.0),
        'ctx': nrm((BATCH, CTX_LEN, D), 1.0),
        'c_ctx': nrm((D,), 1.0),
        'w_ada': nrm((L, D, 6 * D), 0.5 * D ** -0.5),
        'b_ada': nrm((L, 6 * D), 0.02),
        'norm1_w': gain((L, D)),
        'norm2_w': gain((L, D)),
        'w_in': nrm((L, D, N_IN), D ** -0.5),
        'b_gate': nrm((L, N_BRANCH, D), 0.02),
        'mla_q_norm_a': gain((L, MLA_Q_LORA)),
        'mla_w_qb': nrm((L, MLA_Q_LORA, MLA_HEADS * MLA_QK), MLA_Q_LORA ** -0.5),
        'mla_kv_norm_a': gain((L, MLA_KV_LORA)),
        'mla_w_kvb': nrm((L, MLA_KV_LORA, MLA_HEADS * (MLA_NOPE + MLA_V)), MLA_KV_LORA ** -0.5),
        'mla_q_norm': gain((L, MLA_QK)),
        'mla_k_norm': gain((L, MLA_QK)),
        'gla_w_gk2': nrm((L, 2, GLA_GATE_RANK, GLA_HEADS * GLA_DK), GLA_GATE_RANK ** -0.5),
        'gla_b_gk': nrm((L, 2, GLA_HEADS * GLA_DK), 0.1),
        'gla_o_norm': gain((L, GLA_DV)),
        'ret_decay': ret_base + nrm((L, 2, RET_HEADS), 0.01),
        'w_branch': nrm((L, N_BRANCH, BRANCH_W, D), BRANCH_W ** -0.5),
        'w_out': nrm((L, D, D), D ** -0.5),
        'w_ffn_in': nrm((L, D, 2 * D_FF), D ** -0.5),
        'w_dw': nrm((L, CONV_W, D_FF), CONV_W ** -0.5),
        'b_dw': nrm((L, D_FF), 0.02),
        'w_ffn_out': nrm((L, D_FF, D), D_FF ** -0.5),
    }


def reference(x, c, ctx, c_ctx, w_ada, b_ada, norm1_w, norm2_w, w_in, b_gate,
              mla_q_norm_a, mla_w_qb, mla_kv_norm_a, mla_w_kvb, mla_q_norm, mla_k_norm,
              gla_w_gk2, gla_b_gk, gla_o_norm, ret_decay, w_branch, w_out,
              w_ffn_in, w_dw, b_dw, w_ffn_out):
    seq = x.shape[1]
    ctx_len = ctx.shape[1]
    rows = seq // GRID_W
    row_pos = jnp.repeat(jnp.arange(rows), GRID_W)
    col_pos = jnp.tile(jnp.arange(GRID_W), rows)
    cos_r, sin_r = rope_tables(row_pos, MLA_ROPE // 2, ROPE_THETA)
    cos_c, sin_c = rope_tables(col_pos, MLA_ROPE // 2, ROPE_THETA)
    lat_tabs = (cos_r, sin_r, cos_c, sin_c)
    ret_ctx_tabs = retention_tables(jnp.arange(ctx_len))
    ret_lat_tabs = retention_tables(ctx_len + jnp.arange(seq))
    cond = jax.nn.silu(c)
    cond_c = jax.nn.silu(c_ctx)
    h, hc = x, ctx
    for l in range(DEPTH):
        need_ctx = l < DEPTH - 1
        mod = jnp.split((cond @ w_ada[l] + b_ada[l])[:, None, :], 6, axis=-1)
        mod_c = jnp.split(cond_c @ w_ada[l] + b_ada[l], 6, axis=-1)
        a = modulate(rms_norm(h, norm1_w[l]), mod[0], mod[1])
        ac = modulate(rms_norm(hc, norm1_w[l]), mod_c[0], mod_c[1])
        y, y_c = token_mixers(a, ac, need_ctx, lat_tabs, ret_lat_tabs, ret_ctx_tabs, w_in[l], b_gate[l],
                              mla_q_norm_a[l], mla_w_qb[l], mla_kv_norm_a[l], mla_w_kvb[l],
                              mla_q_norm[l], mla_k_norm[l], gla_w_gk2[l], gla_b_gk[l], gla_o_norm[l],
                              ret_decay[l], w_branch[l], w_out[l])
        h = h + mod[2] * y
        h = h + mod[5] * conv_ffn(modulate(rms_norm(h, norm2_w[l]), mod[3], mod[4]),
                                  w_ffn_in[l], w_dw[l], b_dw[l], w_ffn_out[l])
        if need_ctx:
            hc = hc + mod_c[2] * y_c
            hc = hc + mod_c[5] * conv_ffn(modulate(rms_norm(hc, norm2_w[l]), mod_c[3], mod_c[4]),
                                          w_ffn_in[l], w_dw[l], b_dw[l], w_ffn_out[l])
    return h
```

```python
import concourse.bass as bass
import concourse.mybir as mybir

SEM_LIMIT = 1000000000
DMA_K = 12


class Res:
    __slots__ = ("name", "lw", "rd_c", "rd_d", "excl")

    def __init__(self, name, excl=False):
        self.name = name
        self.excl = excl
        self.lw = None
        self.rd_c = {}
        self.rd_d = []


class Op:
    __slots__ = ("eng", "fn", "reads", "writes", "dma", "acc", "deps", "sig", "ev", "waits", "barrier")

    def __init__(self, eng, fn, reads, writes, dma, acc):
        self.eng = eng
        self.fn = fn
        self.reads = reads
        self.writes = writes
        self.dma = dma
        self.acc = acc
        self.deps = set()
        self.sig = False
        self.ev = None
        self.waits = []
        self.barrier = False


class Sched:
    ENGS = ("pe", "act", "dve", "pool", "sp")

    def __init__(self, nc):
        self.nc = nc
        self.ops = []

    def op(self, eng, fn, reads=(), writes=(), acc=False):
        self.ops.append(Op(eng, fn, list(reads), list(writes), False, acc))

    def dma(self, q, out, in_, reads=(), writes=()):
        self.ops.append(Op(q, lambda e, o=out, i=in_: e.dma_start(out=o, in_=i), list(reads), list(writes), True, False))

    def barrier(self):
        o = Op(None, None, [], [], False, False)
        o.barrier = True
        self.ops.append(o)

    def _analyse(self):
        ops = self.ops
        last_c = {}
        last_d = {e: [] for e in self.ENGS}
        pend = {e: set() for e in self.ENGS}
        for i, op in enumerate(ops):
            if op.barrier:
                deps = set(last_c.values())
                for q in self.ENGS:
                    deps.update(last_d[q][-DMA_K:])
                for e in self.ENGS:
                    pend[e] |= deps
                continue
            d = op.deps
            if pend[op.eng]:
                d |= pend[op.eng]
                pend[op.eng] = set()
            for r in op.reads:
                if r.lw is not None:
                    d.add(r.lw)
                if r.excl:
                    for e2, j in r.rd_c.items():
                        if e2 != op.eng:
                            d.add(j)
            for w in op.writes:
                if w.lw is not None:
                    lwop = ops[w.lw]
                    if not (op.acc and op.eng == "pe" and lwop.eng == "pe" and not lwop.dma):
                        d.add(w.lw)
                d.update(w.rd_c.values())
                d.update(w.rd_d)
            d.discard(i)
            for r in op.reads:
                if op.dma:
                    r.rd_d.append(i)
                else:
                    r.rd_c[op.eng] = i
            for w in op.writes:
                w.lw = i
                w.rd_c = {}
                w.rd_d = []
            if op.dma:
                last_d[op.eng].append(i)
            else:
                last_c[op.eng] = i
            for j in d:
                ops[j].sig = True
        self.final_deps = set(last_c.values())
        for q in self.ENGS:
            self.final_deps.update(last_d[q][-DMA_K:])
        for j in self.final_deps:
            ops[j].sig = True

    def _assign(self):
        nc = self.nc
        ops = self.ops
        csem = {}
        ccnt = {}
        dsem = {e: [None] * DMA_K for e in self.ENGS}
        dcnt = {e: [0] * DMA_K for e in self.ENGS}
        dprev = {e: [None] * DMA_K for e in self.ENGS}
        dn = {e: 0 for e in self.ENGS}
        waited = {e: {} for e in self.ENGS}
        self.nsem = 0

        def newsem(tag):
            self.nsem += 1
            return nc.alloc_semaphore(name=f"s_{tag}_{self.nsem}")

        def add_wait(op, ev):
            sem, val = ev
            w = waited[op.eng]
            k = id(sem)
            if w.get(k, 0) >= val:
                return
            w[k] = val
            op.waits.append((sem, val))

        for i, op in enumerate(ops):
            if op.barrier:
                continue
            e = op.eng
            for j in sorted(op.deps):
                add_wait(op, ops[j].ev)
            if op.dma:
                k = dn[e] % DMA_K
                dn[e] += 1
                if dprev[e][k] is not None:
                    add_wait(op, dprev[e][k])
                if dsem[e][k] is None or dcnt[e][k] + 16 > SEM_LIMIT:
                    dsem[e][k] = newsem("d" + e)
                    dcnt[e][k] = 0
                dcnt[e][k] += 16
                op.ev = (dsem[e][k], dcnt[e][k])
                dprev[e][k] = op.ev
                op.sig = True
            elif op.sig:
                if e not in csem or ccnt[e] + 1 > SEM_LIMIT:
                    csem[e] = newsem("c" + e)
                    ccnt[e] = 0
                ccnt[e] += 1
                op.ev = (csem[e], ccnt[e])
        print("[sched] ccnt", ccnt, "dcnt max", {e: max(v) for e, v in dcnt.items()}, flush=True)
        self.final_waits = []
        fw = {}
        for j in sorted(self.final_deps):
            sem, val = ops[j].ev
            k = id(sem)
            if fw.get(k, (None, 0))[1] < val:
                fw[k] = (sem, val)
        self.final_waits = list(fw.values())

    def emit(self):
        self._analyse()
        self._assign()
        nc = self.nc
        ops = self.ops

        def run(e):
            def body(eng):
                for op in ops:
                    if op.barrier or op.eng != e:
                        continue
                    for sem, val in op.waits:
                        eng.wait_ge(sem, val)
                    ins = op.fn(eng)
                    if op.sig:
                        ins.then_inc(op.ev[0], 16 if op.dma else 1)
                if e == "sp":
                    for sem, val in self.final_waits:
                        eng.wait_ge(sem, val)
            return body

        with nc.Block() as block:
            block.tensor(run("pe"))
            block.scalar(run("act"))
            block.vector(run("dve"))
            block.gpsimd(run("pool"))
            block.sync(run("sp"))
        n = sum(1 for o in ops if not o.barrier)
        nw = sum(len(o.waits) for o in ops if not o.barrier)
        print(f"[sched] ops={n} waits={nw} sems={self.nsem}", flush=True)


class SbufAlloc:
    def __init__(self, nc, base=16640, limit=229376):
        self.nc = nc
        self.base = base
        self.off = base
        self.limit = limit
        self.n = 0
        self.peak = 0

    def reset(self, to=None):
        self.off = self.base if to is None else to

    def mark(self):
        return self.off

    def tile(self, shape, dtype, name="t"):
        esz = {mybir.dt.float32: 4, mybir.dt.bfloat16: 2}[dtype]
        nb = esz
        for s in shape[1:]:
            nb *= s
        nb = (nb + 63) // 64 * 64
        assert self.off + nb <= self.limit, f"SBUF overflow {name}: {self.off}+{nb}>{self.limit}"
        self.n += 1
        h = self.nc.alloc_sbuf_tensor_at(f"{name}_{self.n}", list(shape), dtype, offset=self.off)
        self.off += nb
        self.peak = max(self.peak, self.off)
        r = Res(f"{name}_{self.n}")
        return h, r


import os
import numpy as np
DBG = os.environ.get('KDBG', '')
from concourse.bass_utils import run_bass_kernel_spmd

F32 = mybir.dt.float32
BF16 = mybir.dt.bfloat16
ALU = mybir.AluOpType
AF = mybir.ActivationFunctionType

D = 1024
KC = 8
NIN = 7616
DFF = 2816
NJ = 22
EPS = 1e-6
C_MQ, C_MKV, C_MKR, C_GQ, C_GK, C_GV, C_GG, C_RF, C_RB, C_RQ, C_RK, C_RV, C_RG, C_G0 = (
    0, 256, 384, 416, 928, 1440, 1952, 2464, 2480, 2496, 3008, 3520, 4032, 4544)

PP = {}
_o = 0
for _n, _w in (("bada", 48), ("n1w", 8), ("n2w", 8), ("bgate", 24), ("qna", 2), ("kvna", 1), ("qn", 1), ("kn", 1),
               ("gon", 1), ("wdw", 66), ("bdw", 22), ("retdec", 8), ("bgk", 1024)):
    PP[_n] = _o
    _o += _w
NPP = _o
CS = {}
_o = 0
for _n, _w in (("ones", 128), ("ident", 128), ("p128", 128), ("maskF", 128), ("maskB", 128), ("uF", 128), ("uB", 128),
               ("idxF", 128), ("idxB", 128), ("p96", 96), ("esel", 96), ("sel65", 64)):
    CS[_n] = _o
    _o += _w
NCST = _o


def build(S_, CTX, L, debug=False, nphase=999):
    NT = CTX + S_
    NTP = NT + 4
    nlt = S_ // 512
    tiles = [(0, CTX, True, 1)] + [(CTX + 512 * i, 512, False, CTX + 3 + 512 * i) for i in range(nlt)]
    ftiles = [(0, 256, True, 1)] if CTX == 256 else [(i * 256, 256, True, 1 + i * 256) for i in range(CTX // 256)]
    ftiles = ftiles + [(CTX + 256 * i, 256, False, CTX + 3 + 256 * i) for i in range(S_ // 256)]
    NKT = NT // 128

    nc = bass.Bass("TRN2", target_bir_lowering=False)
    sc = Sched(nc)
    A = SbufAlloc(nc)

    def din(name, shape, dt=F32):
        return nc.dram_tensor(name, list(shape), dt, kind="ExternalInput").ap()

    skind = "ExternalOutput" if debug else "Internal"

    def dscr(name, shape, dt):
        return nc.dram_tensor(name, list(shape), dt, kind=skind).ap()

    xT = din("xT", [D, S_])
    ctxT = din("ctxT", [D, CTX])
    cvec_d = din("cvec", [128, 16])
    pp_d = din("pp", [L, 128, NPP])
    cst_d = din("cst", [128, NCST])
    mlaC_d = din("mlaC", [96, NT])
    mlaS_d = din("mlaS", [96, NT])
    retC_d = din("retC", [128, NT])
    retS_d = din("retS", [128, NT])
    w_ada_d = din("w_ada", [L, D, 6 * D])
    w_in_d = din("w_in", [L, D, NIN])
    w_qb_d = din("mla_w_qb", [L, 256, 768])
    w_kvb_d = din("mla_w_kvb", [L, 128, 1024])
    w_gk2_d = din("gla_w_gk2", [L, 2, 16, 512])
    w_br_d = din("w_branch", [L, 3, 512, D])
    w_out_d = din("w_out", [L, D, D])
    w_fi_d = din("w_ffn_in", [L, D, 2 * DFF])
    w_fo_d = din("w_ffn_out", [L, DFF, D])
    outT = nc.dram_tensor("outT", [D, S_], F32, kind="ExternalOutput").ap()

    HT = dscr("HT", [D, NT], F32)
    AT = dscr("AT", [D, NTP], BF16)
    QT = dscr("QT", [8, 96, NT], BF16)
    KT = dscr("KT", [8, 96, NT], BF16)
    VA = dscr("VA", [8, NT, 65], BF16)
    GQ = dscr("GQ", [512, NT], BF16)
    GK = dscr("GK", [512, NT], BF16)
    GG = dscr("GG", [512, NT], BF16)
    RQ = dscr("RQ", [512, NT], BF16)
    RK = dscr("RK", [512, NT], BF16)
    RG = dscr("RG", [512, NT], BF16)
    GV = dscr("GV", [NT, 512], BF16)
    RV = dscr("RV", [NT, 512], BF16)
    SPL = dscr("SPL", [2, NT, 512], F32)
    GT = dscr("GT", [3, D, NT], BF16)
    YM = dscr("YM", [512, NT], BF16)
    YG = dscr("YG", [512, NT], BF16)
    YR = dscr("YR", [512, NT], BF16)
    OF = dscr("OF", [512, NT], F32)
    of_res = {}

    PS = [nc.alloc_psum_tensor(f"ps{i}", [128, 512], F32) for i in range(7)]
    PR = [Res(f"ps{i}", True) for i in range(7)]
    PSB = nc.alloc_psum_tensor("psb", [128, 1024], BF16)
    _pb = Res("psb", True)
    PBR = [_pb, _pb]

    def mm(out, lhsT, rhs, start, stop, reads, wres):
        sc.op("pe", lambda e: e.matmul(out, lhsT=lhsT, rhs=rhs, start=start, stop=stop), reads=reads, writes=[wres], acc=True)

    def act(out, in_, func, reads, wres, bias=None, scale=None, eng="act"):
        kw = {}
        if bias is not None:
            kw["bias"] = bias
        if scale is not None:
            kw["scale"] = scale
        sc.op("act", lambda e: e.activation(out=out, in_=in_, func=func, **kw), reads=reads, writes=[wres])

    def cp(eng, out, in_, reads, wres):
        if eng == "act":
            sc.op("act", lambda e: e.copy(out=out, in_=in_), reads=reads, writes=[wres])
        else:
            sc.op(eng, lambda e: e.tensor_copy(out=out, in_=in_), reads=reads, writes=[wres])

    def tt(eng, out, in0, in1, op, reads, wres):
        sc.op(eng, lambda e: e.tensor_tensor(out=out, in0=in0, in1=in1, op=op), reads=reads, writes=[wres])

    def stt(eng, out, in0, scalar, in1, op0, op1, reads, wres):
        sc.op(eng, lambda e: e.scalar_tensor_tensor(out=out, in0=in0, scalar=scalar, in1=in1, op0=op0, op1=op1),
              reads=reads, writes=[wres])

    def ts(eng, out, in0, s1, s2, op0, op1, reads, wres):
        sc.op(eng, lambda e: e.tensor_scalar(out=out, in0=in0, scalar1=s1, scalar2=s2, op0=op0, op1=op1),
              reads=reads, writes=[wres])

    def ts1(eng, out, in0, s1, op0, reads, wres):
        sc.op(eng, lambda e: e.tensor_single_scalar(out=out, in_=in0, scalar=s1, op=op0), reads=reads, writes=[wres])

    def memset(eng, ap, val, wres):
        sc.op(eng, lambda e: e.memset(ap, val), writes=[wres])

    def load(out, in_, wres, reads=()):
        sc.dma("sp", out, in_, reads=reads, writes=[wres])

    def store(out, in_, rres, writes=()):
        sc.dma("pool", out, in_, reads=[rres], writes=writes)

    class Rot:
        def __init__(self, items):
            self.items = items
            self.i = 0

        def next(self):
            it = self.items[self.i % len(self.items)]
            self.i += 1
            return it

    def rot(n, shape, dt, name):
        return Rot([A.tile(shape, dt, name) for _ in range(n)])

    cast_i = [0]

    def load_w(dst, dst_res, src, kc, n, stage):
        rows = src.shape[0] // kc
        for k in range(kc):
            for n0 in range(0, n, 2048):
                w = min(2048, n - n0)
                st, sr = stage.next()
                load(st[:rows, :w], src[k * rows:(k + 1) * rows, n0:n0 + w], sr)
                eng = ("dve", "pool", "act")[cast_i[0] % 3]
                cast_i[0] += 1
                cp(eng, dst[:rows, k, n0:n0 + w], st[:rows, :w], [sr], dst_res)

    cst, cst_r = A.tile([128, NCST], F32, "cst")
    load(cst[:], cst_d, cst_r)
    cbf, cbf_r = A.tile([128, NCST], BF16, "cbf")
    cp("dve", cbf[:], cst[:], [cst_r], cbf_r)
    epst, eps_r = A.tile([128, 1], F32, "eps")
    memset("dve", epst[:], EPS, eps_r)
    onec, onec_r = A.tile([128, 1], F32, "onec")
    memset("dve", onec[:], 1.0, onec_r)
    ppt, pp_r = A.tile([128, NPP], F32, "pp")
    cvt, cv_r = A.tile([128, 16], F32, "cvec")
    cond2, cond_r = A.tile([128, 8, 2], F32, "cond2")
    modt, mod_r = A.tile([128, 48, 2], F32, "mod")
    g1t, g1_r = A.tile([128, 8, 2], F32, "g1")
    g2t, g2_r = A.tile([128, 8, 2], F32, "g2")
    lgt, lg_r = A.tile([128, 8], F32, "lg")
    nlgt, nlg_r = A.tile([128, 8], F32, "nlg")
    A.base = A.off

    def C_(name, rows=128, w=None, bf=True):
        o = CS[name]
        w = w if w is not None else (128 if name not in ("p96", "esel", "sel65") else (96 if name != "sel65" else 64))
        t = cbf if bf else cst
        return t[0:rows, o:o + w]

    ones_bf = C_("ones")
    ident_bf = C_("ident")
    CRES = [cst_r, cbf_r]

    def ppc(name, col, rows=128):
        return ppt[0:rows, PP[name] + col:PP[name] + col + 1]

    def partnorm(src_list, srcres, rows, nfeat, T, sqt, sq_r, ss_bank, lnt, ln_r, rst, rs_r):
        n = len(src_list)
        for i, s in enumerate(src_list):
            act(sqt[0:rows, i, :T], s, AF.Square, srcres, sq_r)
        for i in range(n):
            mm(PS[ss_bank][0:rows, :T], cbf[0:rows, CS["ones"]:CS["ones"] + rows], sqt[0:rows, i, :T], i == 0, i == n - 1,
               [sq_r, cbf_r], PR[ss_bank])
        act(lnt[0:rows, :T], PS[ss_bank][0:rows, :T], AF.Ln, [PR[ss_bank], eps_r], ln_r, bias=epst[0:rows, 0:1], scale=1.0 / nfeat)
        act(rst[0:rows, :T], lnt[0:rows, :T], AF.Exp, [ln_r], rs_r, scale=-0.5)

    sc.dma("sp", HT[:, 0:CTX], ctxT, reads=[], writes=[])
    for i in range(0, S_, 2048):
        w = min(2048, S_ - i)
        sc.dma("sp", HT[:, CTX + i:CTX + i + w], xT[:, i:i + w], reads=[], writes=[])
    load(cvt[:], cvec_d, cv_r)
    act(cond2[:, :, 0], cvt[:, 0:8], AF.Silu, [cv_r], cond_r)
    act(cond2[:, :, 1], cvt[:, 8:16], AF.Silu, [cv_r], cond_r)
    sc.barrier()

    def norm_phase(l, which):
        A.reset()
        hts = rot(2, [128, 8, 512], F32, "h")
        sqs = rot(2, [128, 8, 512], BF16, "sq")
        lns = rot(2, [128, 512], F32, "ln")
        rss = rot(2, [128, 512], F32, "rs")
        tms = rot(2, [128, 8, 512], F32, "tm")
        abs_ = rot(2, [128, 8, 514], BF16, "a")
        for ab_, abr_ in abs_.items:
            memset("pool", ab_[:, :, 0:1], 0.0, abr_)
        gt = g1t if which == 1 else g2t
        gr = g1_r if which == 1 else g2_r
        shv = 0 if which == 1 else 3
        HTv = HT.rearrange("(c p) t -> p c t", p=128)
        ATv = AT.rearrange("(c p) t -> p c t", p=128)
        for ti, (t0, T, isc, ac0) in enumerate(tiles):
            if isc and l == L - 1 and which == 2:
                continue
            col = 1 if isc else 0
            h, hr = hts.next()
            load(h[:, :, :T], HTv[:, :, t0:t0 + T], hr)
            sq, sqr = sqs.next()
            lnv, lnr = lns.next()
            rs, rsr = rss.next()
            bank = ti % 2
            partnorm([h[:, c, :T] for c in range(8)], [hr], 128, D, T, sq, sqr, bank, lnv, lnr, rs, rsr)
            tm, tmr = tms.next()
            tt("dve", tm[:, :, :T], h[:, :, :T], rs[:, :T].unsqueeze(1).to_broadcast([128, 8, T]), ALU.mult, [hr, rsr], tmr)
            ab, abr = abs_.next()
            first = (t0 == 0) or (t0 == CTX)
            lastt = (t0 + T == CTX) or (t0 + T == NT)
            for c in range(8):
                if c % 2 == 0:
                    act(ab[:, c, 1:1 + T], tm[:, c, :T], AF.Identity, [tmr, gr, mod_r], abr,
                        bias=modt[:, shv * 8 + c, col:col + 1], scale=gt[:, c, col:col + 1])
                else:
                    ts("pool", ab[:, c, 1:1 + T], tm[:, c, :T], gt[:, c, col:col + 1], modt[:, shv * 8 + c, col:col + 1],
                       ALU.mult, ALU.add, [tmr, gr, mod_r], abr)
            if lastt:
                memset("pool", ab[:, :, T + 1:T + 2], 0.0, abr)
            lo = 0 if first else 1
            hi = T + 2 if lastt else T + 1
            store(ATv[:, :, ac0 - 1 + lo:ac0 - 1 + hi], ab[:, :, lo:hi], abr)
        sc.barrier()

    def mod_phase(l):
        A.reset()
        load(ppt[:], pp_d[l], pp_r)
        ws = rot(2, [128, 8, 1024], F32, "wada")
        wv = w_ada_d[l].rearrange("(c p) n -> p c n", p=128)
        for g in range(6):
            w, wr = ws.next()
            for k in range(0, 8, 2):
                load(w[:, k:k + 2, :], wv[:, k:k + 2, g * 1024:(g + 1) * 1024], wr)
            for nn in range(8):
                j = g * 8 + nn
                for k in range(8):
                    mm(PS[0][:, 2 * j:2 * j + 2], w[:, k, nn * 128:(nn + 1) * 128], cond2[:, k, :], k == 0, k == 7,
                       [wr, cond_r], PR[0])
        psv = PS[0][:, 0:96].rearrange("p (j t) -> p j t", t=2)
        for col in range(2):
            tt("dve", modt[:, :, col], psv[:, :, col], ppt[:, PP["bada"]:PP["bada"] + 48], ALU.add, [PR[0], pp_r], mod_r)
        for col in range(2):
            stt("dve", g1t[:, :, col], modt[:, 8:16, col], 1.0, ppt[:, PP["n1w"]:PP["n1w"] + 8], ALU.add, ALU.mult,
                [mod_r, pp_r], g1_r)
            stt("dve", g2t[:, :, col], modt[:, 32:40, col], 1.0, ppt[:, PP["n2w"]:PP["n2w"] + 8], ALU.add, ALU.mult,
                [mod_r, pp_r], g2_r)
        act(nlgt[:], ppt[:, PP["retdec"]:PP["retdec"] + 8], AF.Exp, [pp_r], nlg_r)
        ts1("dve", lgt[:], nlgt[:], -1.0, ALU.mult, [nlg_r], lg_r)
        sc.barrier()

    def qk_finish(ps_bank, T, normcol, tabC, tabS, tab_r, out_ap, tmp):
        sq, sqr, lnv, lnr, rs, rsr, xn, xnr, t1, t1r, t2, t2r, ob, obr, ssb, swb = tmp
        partnorm([PS[ps_bank][0:96, :T]], [PR[ps_bank]], 96, 96, T, sq, sqr, ssb, lnv, lnr, rs, rsr)
        stt("dve", xn[0:96, :T], PS[ps_bank][0:96, :T], normcol, rs[0:96, :T], ALU.mult, ALU.mult, [PR[ps_bank], rsr, pp_r], xnr)
        mm(PS[swb][0:96, :T], cbf[0:96, CS["p96"]:CS["p96"] + 96], xn[0:96, :T], True, True, [xnr, cbf_r], PR[swb])
        tt("pool", t1[0:96, :T], xn[0:96, :T], tabC, ALU.mult, [xnr, tab_r], t1r)
        tt("dve", t2[0:96, :T], PS[swb][0:96, :T], tabS, ALU.mult, [PR[swb], tab_r], t2r)
        tt("pool", ob[0:96, :T], t1[0:96, :T], t2[0:96, :T], ALU.add, [t1r, t2r], obr)
        store(out_ap, ob[0:96, :T], obr)

    def inproj_mla(l):
        A.reset()
        stage = rot(2, [128, 2048], F32, "stg")
        win, win_r = A.tile([128, 8, 416], BF16, "winA")
        load_w(win, win_r, w_in_d[l][:, 0:416], 8, 416, stage)
        wqb, wqb_r = A.tile([128, 2, 768], BF16, "wqb")
        load_w(wqb, wqb_r, w_qb_d[l], 2, 768, stage)
        wkf, wkf_r = A.tile([128, 1024], F32, "wkf")
        load(wkf[:], w_kvb_d[l], wkf_r)
        wkn, wkn_r = A.tile([128, 8, 96], BF16, "wkn")
        memset("dve", wkn[:], 0.0, wkn_r)
        wkfv = wkf[:].rearrange("p (h e) -> p h e", e=128)
        cp("dve", wkn[:, :, 0:64], wkfv[:, :, 0:64], [wkf_r], wkn_r)
        wkv, wkv_r = A.tile([128, 8, 64], BF16, "wkv")
        cp("dve", wkv[:], wkfv[:, :, 64:128], [wkf_r], wkv_r)
        ats = rot(2, [128, 8, 512], BF16, "a")
        cqs = rot(2, [128, 2, 512], F32, "cq")
        sq2 = rot(2, [128, 2, 512], BF16, "sq2")
        lns = rot(2, [128, 512], F32, "ln")
        rss = rot(2, [128, 512], F32, "rs")
        cqn = rot(2, [128, 2, 512], BF16, "cqn")
        ckv = rot(2, [128, 512], F32, "ckv")
        ckn = rot(2, [128, 512], BF16, "ckn")
        krs = rot(2, [32, 512], BF16, "kr")
        tCs = rot(2, [96, 512], F32, "tC")
        tSs = rot(2, [96, 512], F32, "tS")
        sqh = rot(3, [96, 1, 512], BF16, "sqh")
        lnh = rot(3, [96, 512], F32, "lnh")
        rsh = rot(3, [96, 512], F32, "rsh")
        xnh = rot(3, [96, 512], BF16, "xnh")
        t1h = rot(3, [96, 512], F32, "t1h")
        t2h = rot(3, [96, 512], F32, "t2h")
        obh = rot(3, [96, 512], BF16, "obh")
        vas = rot(2, [128, 8, 65], BF16, "va")
        for v, vr in vas.items:
            memset("pool", v[:], 1.0, vr)
        ATv = AT.rearrange("(c p) t -> p c t", p=128)
        VAv = VA.rearrange("h t e -> t h e")
        hcount = [0]

        def tmpset():
            i = hcount[0]
            hcount[0] += 1
            sq, sqr = sqh.next(); lnv, lnr = lnh.next(); rs, rsr = rsh.next(); xn, xnr = xnh.next()
            t1, t1r = t1h.next(); t2, t2r = t2h.next(); ob, obr = obh.next()
            return (sq, sqr, lnv, lnr, rs, rsr, xn, xnr, t1, t1r, t2, t2r, ob, obr, 3 + (i % 2), 5 + (i % 2))

        for ti, (t0, T, isc, ac0) in enumerate(tiles):
            a, ar = ats.next()
            load(a[:, :, :T], ATv[:, :, ac0:ac0 + T], ar)
            tC, tCr = tCs.next()
            tS, tSr = tSs.next()
            load(tC[:, :T], mlaC_d[:, t0:t0 + T], tCr)
            load(tS[:, :T], mlaS_d[:, t0:t0 + T], tCr)
            cq, cqr = cqs.next()
            for c2 in range(2):
                for k in range(8):
                    mm(PS[c2][:, :T], win[:, k, c2 * 128:(c2 + 1) * 128], a[:, k, :T], k == 0, k == 7, [win_r, ar], PR[c2])
                cp("act", cq[:, c2, :T], PS[c2][:, :T], [PR[c2]], cqr)
            sq, sqr = sq2.next(); lnv, lnr = lns.next(); rs, rsr = rss.next()
            partnorm([cq[:, 0, :T], cq[:, 1, :T]], [cqr], 128, 256, T, sq, sqr, 2, lnv, lnr, rs, rsr)
            cn, cnr = cqn.next()
            for c2 in range(2):
                stt("dve", cn[:, c2, :T], cq[:, c2, :T], ppc("qna", c2), rs[:, :T], ALU.mult, ALU.mult, [cqr, rsr, pp_r], cnr)
            for k in range(8):
                mm(PS[0][:, :T], win[:, k, 256:384], a[:, k, :T], k == 0, k == 7, [win_r, ar], PR[0])
            kv, kvr = ckv.next()
            cp("act", kv[:, :T], PS[0][:, :T], [PR[0]], kvr)
            sq, sqr = sq2.next(); lnv, lnr = lns.next(); rs, rsr = rss.next()
            partnorm([kv[:, :T]], [kvr], 128, 128, T, sq, sqr, 2, lnv, lnr, rs, rsr)
            kn, knr = ckn.next()
            stt("dve", kn[:, :T], kv[:, :T], ppc("kvna", 0), rs[:, :T], ALU.mult, ALU.mult, [kvr, rsr, pp_r], knr)
            for k in range(8):
                mm(PS[1][0:32, :T], win[:, k, 384:416], a[:, k, :T], k == 0, k == 7, [win_r, ar], PR[1])
            kr, krr = krs.next()
            cp("act", kr[0:32, :T], PS[1][0:32, :T], [PR[1]], krr)
            for h in range(8):
                b = h % 3
                for c2 in range(2):
                    mm(PS[b][0:96, :T], wqb[:, c2, h * 96:(h + 1) * 96], cn[:, c2, :T], c2 == 0, c2 == 1, [wqb_r, cnr], PR[b])
                qk_finish(b, T, ppc("qn", 0, 96), tC[0:96, :T], tS[0:96, :T], tCr, QT[h][:, t0:t0 + T], tmpset())
            for h in range(8):
                b = h % 3
                mm(PS[b][0:96, :T], wkn[:, h, :], kn[:, :T], True, False, [wkn_r, knr], PR[b])
                mm(PS[b][0:96, :T], cbf[0:32, CS["esel"]:CS["esel"] + 96], kr[0:32, :T], False, True, [cbf_r, krr], PR[b])
                qk_finish(b, T, ppc("kn", 0, 96), tC[0:96, :T], tS[0:96, :T], tCr, KT[h][:, t0:t0 + T], tmpset())
            for j in range(T // 128):
                b = j % 2
                mm(PS[b][:, 0:512], kn[:, j * 128:(j + 1) * 128], wkv[:].rearrange("p h e -> p (h e)"), True, True, [knr, wkv_r], PR[b])
                va, var_ = vas.next()
                cp("act", va[:, :, 0:64], PS[b][:, 0:512].rearrange("p (h e) -> p h e", e=64), [PR[b]], var_)
                store(VAv[t0 + j * 128:t0 + (j + 1) * 128], va[:], var_)
        sc.barrier()

    def inproj_lin(l, kind):
        A.reset()
        stage = rot(2, [128, 2048], F32, "stg")
        c0 = C_GQ if kind == 0 else C_RQ
        ncols = 2080 if kind == 0 else 2048
        win, win_r = A.tile([128, 8, ncols], BF16, "winB")
        load_w(win, win_r, w_in_d[l][:, c0:c0 + ncols], 8, ncols, stage)
        if kind == 0:
            w2, w2_r = A.tile([16, 2, 512], BF16, "w2")
            for d in range(2):
                st, sr = stage.next()
                load(st[0:16, 0:512], w_gk2_d[l, d], sr)
                cp("dve", w2[0:16, d, :], st[0:16, 0:512], [sr], w2_r)
        ats = rot(2, [128, 8, 512], BF16, "a")
        stq = rot(2, [128, 4, 512], BF16, "stq")
        stv = rot(2, [128, 4, 512], BF16, "stv")
        if kind == 0:
            rfs = rot(2, [16, 512], BF16, "rf")
            rbs = rot(2, [16, 512], BF16, "rb")
            xbs = rot(2, [128, 512], F32, "xb")
            ees = rot(2, [128, 512], F32, "ee")
            sps = rot(2, [128, 4, 512], F32, "sp")
        else:
            tCs = rot(2, [128, 512], F32, "tC")
            tSs = rot(2, [128, 512], F32, "tS")
            xbf = rot(3, [128, 512], BF16, "xbf")
            t1s = rot(3, [128, 512], F32, "t1")
            t2s = rot(3, [128, 512], F32, "t2")
        ATv = AT.rearrange("(c p) t -> p c t", p=128)
        Qd, Kd, Gd, Vd = (GQ, GK, GG, GV) if kind == 0 else (RQ, RK, RG, RV)
        pbank = [0]

        def nb():
            pbank[0] += 1
            return pbank[0] % 4

        cpi = [0]
        for ti, (t0, T, isc, ac0) in enumerate(tiles):
            a, ar = ats.next()
            load(a[:, :, :T], ATv[:, :, ac0:ac0 + T], ar)
            if kind == 1 and 'notab' not in DBG:
                tC, tCr = tCs.next()
                tS, tSr = tSs.next()
                load(tC[:, :T], retC_d[:, t0:t0 + T], tCr)
                load(tS[:, :T], retS_d[:, t0:t0 + T], tSr)
            for wi, (dst, coff) in enumerate(((Qd, 0), (Kd, 512), (Gd, 1536))):
                st, sr = stq.next()
                for h in range(4):
                    b = nb()
                    for k in range(8):
                        mm(PS[b][:, :T], win[:, k, coff + h * 128:coff + (h + 1) * 128], a[:, k, :T], k == 0, k == 7, [win_r, ar], PR[b])
                    if wi == 2:
                        act(st[:, h, :T], PS[b][:, :T], AF.Silu, [PR[b]], sr)
                    elif kind == 0 or 'norope' in DBG:
                        cpi[0] += 1
                        cp("act" if cpi[0] % 2 else "dve", st[:, h, :T], PS[b][:, :T], [PR[b]], sr)
                    else:
                        xb_, xbr = xbf.next()
                        cp("act", xb_[:, :T], PS[b][:, :T], [PR[b]], xbr)
                        b2 = 4 + (h % 2)
                        if 'nomm' in DBG:
                            b2 = b
                        else:
                            mm(PS[b2][:, :T], C_("p128"), xb_[:, :T], True, True, [xbr, cbf_r], PR[b2])
                        t1, t1r = t1s.next()
                        t2, t2r = t2s.next()
                        tt("dve", t1[:, :T], PS[b][:, :T], tC[:, :T], ALU.mult, [PR[b], tCr], t1r)
                        tt("dve", t2[:, :T], PS[b2][:, :T], tS[:, :T], ALU.mult, [PR[b2], tSr], t2r)
                        tt("dve" if 'dveadd' in DBG else "pool", st[:, h, :T], t1[:, :T], t2[:, :T], ALU.add, [t1r, t2r], sr)
                store(dst.rearrange("(h p) t -> p h t", p=128)[:, :, t0:t0 + T], st[:, :, :T], sr)
            sv, svr = stv.next()
            nbk = T // 128
            for j in range(nbk):
                b = nb()
                for k in range(8):
                    mm(PS[b][:, 0:512], a[:, k, j * 128:(j + 1) * 128], win[:, k, 1024:1536], k == 0, k == 7, [win_r, ar], PR[b])
                cpi[0] += 1
                cp("act" if cpi[0] % 2 else "dve", sv[:, j, :], PS[b][:, 0:512], [PR[b]], svr)
            store(Vd[t0:t0 + T, :].rearrange("(j p) n -> p j n", p=128), sv[:, 0:nbk, :], svr)
            if kind == 0:
                rf, rfr = rfs.next()
                rb, rbr = rbs.next()
                for (rt, rr, cc) in ((rf, rfr, 2048), (rb, rbr, 2064)):
                    b = nb()
                    for k in range(8):
                        mm(PS[b][0:16, :T], win[:, k, cc:cc + 16], a[:, k, :T], k == 0, k == 7, [win_r, ar], PR[b])
                    cp("act", rt[0:16, :T], PS[b][0:16, :T], [PR[b]], rr)
                for d, (rt, rr) in enumerate(((rf, rfr), (rb, rbr))):
                    sp_, spr = sps.next()
                    for j in range(nbk):
                        b = nb()
                        mm(PS[b][:, 0:512], rt[0:16, j * 128:(j + 1) * 128], w2[0:16, d, :], True, True, [rr, w2_r], PR[b])
                        xb_, xbr = xbs.next()
                        tt("dve", xb_[:], PS[b][:, 0:512], ppt[:, PP["bgk"] + d * 512:PP["bgk"] + (d + 1) * 512], ALU.add, [PR[b], pp_r], xbr)
                        ee, eer = ees.next()
                        act(ee[:], xb_[:], AF.Exp, [xbr], eer, scale=-1.0)
                        act(sp_[:, j, :], ee[:], AF.Ln, [eer], spr, bias=1.0)
                    store(SPL[d, t0:t0 + T, :].rearrange("(j p) n -> p j n", p=128), sp_[:, 0:nbk, :], spr)
        sc.barrier()

    def inproj_gates(l):
        A.reset()
        stage = rot(2, [128, 2048], F32, "stg")
        win, win_r = A.tile([128, 8, 3072], BF16, "winD")
        load_w(win, win_r, w_in_d[l][:, C_G0:C_G0 + 3072], 8, 3072, stage)
        ats = rot(2, [128, 8, 512], BF16, "a")
        sts = rot(3, [128, 8, 512], BF16, "stg8")
        ATv = AT.rearrange("(c p) t -> p c t", p=128)
        bi = 0
        for ti, (t0, T, isc, ac0) in enumerate(tiles):
            if isc and l == L - 1:
                continue
            a, ar = ats.next()
            load(a[:, :, :T], ATv[:, :, ac0:ac0 + T], ar)
            for n in range(3):
                st, sr = sts.next()
                for c in range(8):
                    b = bi % 4
                    bi += 1
                    for k in range(8):
                        mm(PS[b][:, :T], win[:, k, n * 1024 + c * 128:n * 1024 + (c + 1) * 128], a[:, k, :T], k == 0, k == 7, [win_r, ar], PR[b])
                    act(st[:, c, :T], PS[b][:, :T], AF.Sigmoid, [PR[b], pp_r], sr, bias=ppc("bgate", n * 8 + c))
                store(GT[n].rearrange("(c p) t -> p c t", p=128)[:, :, t0:t0 + T], st[:, :, :T], sr)
        sc.barrier()

    def attention(l):
        A.reset()
        kts = rot(2, [96, NT], BF16, "kt")
        vas = rot(2, [128, NKT, 65], BF16, "va")
        qts = rot(3, [96, 512], BF16, "qt")
        pts = rot(4, [128, 512], BF16, "pt")
        osb = rot(2, [65, 512], F32, "osb")
        rvs = rot(2, [64, 512], F32, "rinv")
        ybs = rot(2, [64, 512], BF16, "yb")
        scale = 96.0 ** -0.5
        qi = 0
        for h in range(8):
            kt, ktr = kts.next()
            va, var_ = vas.next()
            load(kt[:, :], KT[h], ktr)
            VAh = VA[h].rearrange("(k p) e -> p k e", p=128)
            for k0 in range(0, NKT, 8):
                k1 = min(NKT, k0 + 8)
                load(va[:, k0:k1, :], VAh[:, k0:k1, :], var_)
            for ti, (t0, T, isc, ac0) in enumerate(tiles):
                if isc and l == L - 1:
                    continue
                nk = CTX // 128 if isc else NKT
                qt, qtr = qts.next()
                load(qt[:, :T], QT[h][:, t0:t0 + T], qtr)
                ob = 3 + (qi % 2)
                qi += 1

                def qk(i):
                    mm(PS[i % 3][:, :T], kt[:, i * 128:(i + 1) * 128], qt[:, :T], True, True, [ktr, qtr], PR[i % 3])

                qk(0)
                for i in range(nk):
                    if i + 1 < nk:
                        qk(i + 1)
                    pt, ptr = pts.next()
                    act(pt[:, :T], PS[i % 3][:, :T], AF.Exp, [PR[i % 3]], ptr, scale=scale)
                    mm(PS[ob][0:65, :T], va[:, i, :], pt[:, :T], i == 0, i == nk - 1, [var_, ptr], PR[ob])
                o, orr = osb.next()
                cp("dve", o[0:65, :T], PS[ob][0:65, :T], [PR[ob]], orr)
                mm(PS[5][0:64, :T], cst[0:65, CS["sel65"]:CS["sel65"] + 64], o[0:65, :T], True, True, [cst_r, orr], PR[5])
                rv, rvr = rvs.next()
                sc.op("dve", lambda e, rv=rv, T=T: e.reciprocal(out=rv[0:64, :T], in_=PS[5][0:64, :T]), reads=[PR[5]], writes=[rvr])
                yb, ybr = ybs.next()
                tt("dve", yb[0:64, :T], o[0:64, :T], rv[0:64, :T], ALU.mult, [orr, rvr], ybr)
                store(YM[h * 64:(h + 1) * 64, t0:t0 + T], yb[0:64, :T], ybr)
        sc.barrier()

    def sweep(l, kind):
        A.reset()
        Qd, Kd, Gd, Vd, Yd = (GQ, GK, GG, GV, YG) if kind == 0 else (RQ, RK, RG, RV, YR)
        qs = 128.0 ** -0.5 if kind == 0 else 1.0
        ks = 1.0 if kind == 0 else 128.0 ** -0.5
        NH = 4
        St = [A.tile([128, 128], F32, "S") for _ in range(NH)]
        Sb = [A.tile([128, 128], BF16, "Sb") for _ in range(NH)]
        qts = [rot(2, [128, 512], BF16, "q") for _ in range(NH)]
        kts_ = [rot(2, [128, 512], BF16, "k") for _ in range(NH)]
        vts = [rot(2, [128, 4, 128], BF16, "v") for _ in range(NH)]
        qds = [rot(2, [128, 512], BF16, "qd") for _ in range(NH)]
        kis = [rot(2, [128, 512], BF16, "ki") for _ in range(NH)]
        if kind == 0:
            spt = [rot(2, [128, 4, 128], F32, "spt") for _ in range(NH)]
            Es = [rot(2, [128, 512], F32, "E") for _ in range(NH)]
            Eis = [rot(2, [128, 512], F32, "Ei") for _ in range(NH)]
        else:
            Etab = [[A.tile([128, 128], F32, "Et") for _ in range(2)] for _ in range(NH)]
            Eitab = [[A.tile([128, 128], F32, "Eit") for _ in range(2)] for _ in range(NH)]
            for h in range(NH):
                for d in range(2):
                    idx = cst[:, CS["idxF" if d == 0 else "idxB"]:CS["idxF" if d == 0 else "idxB"] + 128]
                    act(Etab[h][d][0][:], idx, AF.Exp, [cst_r, lg_r], Etab[h][d][1], scale=lgt[:, d * 4 + h:d * 4 + h + 1])
                    act(Eitab[h][d][0][:], idx, AF.Exp, [cst_r, nlg_r], Eitab[h][d][1], scale=nlgt[:, d * 4 + h:d * 4 + h + 1])
        kes = rot(4, [128, 128], BF16, "ke")
        kets = rot(4, [128, 128], BF16, "ket")
        ams = rot(4, [128, 128], BF16, "am")
        ofs = [rot(2, [128, 512], F32, "of") for _ in range(NH)]
        sgs = [rot(2, [128, 512], BF16, "sg") for _ in range(NH)]
        ots = rot(8, [128, 512], F32, "ot")
        sqs = rot(2, [128, 1, 512], BF16, "sq")
        lns = rot(2, [128, 512], F32, "ln")
        rss = rot(2, [128, 512], F32, "rs")
        y1s = rot(2, [128, 512], F32, "y1")
        ybs = rot(2, [128, 512], BF16, "yb")
        cnt = [0]
        for d in range(2):
            for h in range(NH):
                memset("dve", St[h][0][:], 0.0, St[h][1])
                memset("pool", Sb[h][0][:], 0.0, Sb[h][1])
            order = [0] + (list(range(1, len(tiles))) if d == 0 else list(range(len(tiles) - 1, 0, -1)))
            mask = cst[:, CS["maskF" if d == 0 else "maskB"]:CS["maskF" if d == 0 else "maskB"] + 128]
            um = cst[:, CS["uF" if d == 0 else "uB"]:CS["uF" if d == 0 else "uB"] + 128]
            for ti in order:
                t0, T, isc, ac0 = tiles[ti]
                skip_out = isc and l == L - 1
                nbk = T // 128
                cur = []
                for h in range(NH):
                    q, qr = qts[h].next(); k, kr = kts_[h].next(); v, vr = vts[h].next()
                    load(q[:, :T], Qd[h * 128:(h + 1) * 128, t0:t0 + T], qr)
                    load(k[:, :T], Kd[h * 128:(h + 1) * 128, t0:t0 + T], kr)
                    load(v[:, 0:nbk, :], Vd[t0:t0 + T, h * 128:(h + 1) * 128].rearrange("(j p) n -> p j n", p=128), vr)
                    qd, qdr = qds[h].next(); ki, kir = kis[h].next()
                    if kind == 0:
                        sp_, spr = spt[h].next()
                        load(sp_[:, 0:nbk, :], SPL[d, t0:t0 + T, h * 128:(h + 1) * 128].rearrange("(j p) n -> p j n", p=128), spr)
                        cb = h % 2
                        for j in range(nbk):
                            mm(PS[cb][:, j * 128:(j + 1) * 128], sp_[:, j, :], um, True, True, [spr, cst_r], PR[cb])
                        E, Er = Es[h].next(); Ei, Eir = Eis[h].next()
                        act(E[:, :T], PS[cb][:, :T], AF.Exp, [PR[cb]], Er)
                        act(Ei[:, :T], PS[cb][:, :T], AF.Exp, [PR[cb]], Eir, scale=-1.0)
                        stt("dve", qd[:, :T], q[:, :T], qs, E[:, :T], ALU.mult, ALU.mult, [qr, Er], qdr)
                        stt("dve", ki[:, :T], k[:, :T], ks, Ei[:, :T], ALU.mult, ALU.mult, [kr, Eir], kir)
                        gsrc = (E, Er)
                    else:
                        E, Er = Etab[h][d]; Ei, Eir = Eitab[h][d]
                        stt("dve", qd[:, :T].rearrange("p (j n) -> p j n", n=128), q[:, :T].rearrange("p (j n) -> p j n", n=128), qs,
                            E[:].unsqueeze(1).to_broadcast([128, nbk, 128]), ALU.mult, ALU.mult, [qr, Er], qdr)
                        stt("dve", ki[:, :T].rearrange("p (j n) -> p j n", n=128), k[:, :T].rearrange("p (j n) -> p j n", n=128), ks,
                            Ei[:].unsqueeze(1).to_broadcast([128, nbk, 128]), ALU.mult, ALU.mult, [kr, Eir], kir)
                        gsrc = (E, Er)
                    cur.append((q, qr, k, kr, v, vr, qd, qdr, ki, kir, gsrc))
                ob = [None] * NH
                blocks = list(range(nbk)) if d == 0 else list(range(nbk - 1, -1, -1))
                for j in blocks:
                    c0 = j * 128
                    for h in range(NH):
                        q, qr, k, kr, v, vr, qd, qdr, ki, kir, (E, Er) = cur[h]
                        if kind == 0:
                            gcol = E[:, c0 + 127:c0 + 128] if d == 0 else E[:, c0:c0 + 1]
                        else:
                            gcol = E[:, 127:128] if d == 0 else E[:, 0:1]
                        ke, ker = kes.next()
                        ts1("pool", ke[:], ki[:, c0:c0 + 128], gcol, ALU.mult, [kir, Er], ker)
                        pb = cnt[0] % 2
                        cnt[0] += 1
                        sc.op("pe", lambda e, pb=pb, ke=ke: e.transpose(PSB[:, pb * 128:(pb + 1) * 128], ke[:], ident_bf),
                              reads=[ker, cbf_r], writes=[PBR[pb]])
                        ket, ketr = kets.next()
                        cp("act", ket[:], PSB[:, pb * 128:(pb + 1) * 128], [PBR[pb]], ketr)
                        ab = 4 + pb
                        mm(PS[ab][:, 0:128], ki[:, c0:c0 + 128], qd[:, c0:c0 + 128], True, True, [kir, qdr], PR[ab])
                        am, amr = ams.next()
                        tt("dve", am[:], PS[ab][:, 0:128], mask, ALU.mult, [PR[ab], cst_r], amr)
                        obk = 2 + (h % 2)
                        mm(PS[obk][:, 0:128], v[:, j, :], am[:], True, False, [vr, amr], PR[obk])
                        mm(PS[obk][:, 0:128], Sb[h][0][:], qd[:, c0:c0 + 128], False, True, [Sb[h][1], qdr], PR[obk])
                        if not skip_out:
                            if ob[h] is None:
                                ob[h] = ots.next()
                            ot, otr = ob[h]
                            cp("act", ot[:, c0:c0 + 128], PS[obk][:, 0:128], [PR[obk]], otr)
                        mm(PS[6][:, 0:128], ket[:], v[:, j, :], True, True, [ketr, vr], PR[6])
                        stt("dve", St[h][0][:], St[h][0][:], gcol, PS[6][:, 0:128], ALU.mult, ALU.add, [St[h][1], Er, PR[6]], St[h][1])
                        cp("pool", Sb[h][0][:], St[h][0][:], [St[h][1]], Sb[h][1])
                if skip_out:
                    continue
                for h in range(NH):
                    ot, otr = ob[h]
                    key = (kind, h, ti)
                    if d == 0:
                        r_ = of_res.setdefault(key, Res(f"of{key}"))
                        store(OF[h * 128:(h + 1) * 128, t0:t0 + T], ot[:, :T], otr, writes=[r_])
                    else:
                        r_ = of_res[key]
                        of_, ofr = ofs[h].next()
                        load(of_[:, :T], OF[h * 128:(h + 1) * 128, t0:t0 + T], ofr, reads=[r_])
                        sg, sgr = sgs[h].next()
                        load(sg[:, :T], Gd[h * 128:(h + 1) * 128, t0:t0 + T], sgr)
                        tt("dve", ot[:, :T], ot[:, :T], of_[:, :T], ALU.add, [otr, ofr], otr)
                        sq, sqr = sqs.next(); lnv, lnr = lns.next(); rs, rsr = rss.next()
                        partnorm([ot[:, :T]], [otr], 128, 128, T, sq, sqr, h % 2, lnv, lnr, rs, rsr)
                        y1, y1r = y1s.next()
                        ncol = ppc("gon", 0) if kind == 0 else onec[:, 0:1]
                        stt("dve", y1[:, :T], ot[:, :T], ncol, rs[:, :T], ALU.mult, ALU.mult, [otr, rsr, pp_r, onec_r], y1r)
                        yb, ybr = ybs.next()
                        tt("pool", yb[:, :T], y1[:, :T], sg[:, :T], ALU.mult, [y1r, sgr], ybr)
                        store(Yd[h * 128:(h + 1) * 128, t0:t0 + T], yb[:, :T], ybr)
            sc.barrier()

    def merge_phase(l):
        A.reset()
        stage = rot(2, [128, 2048], F32, "stg")
        wbr, wbr_r = A.tile([128, 12, 1024], BF16, "wbr")
        load_w(wbr, wbr_r, w_br_d[l].rearrange("n k m -> (n k) m"), 12, 1024, stage)
        wo, wo_r = A.tile([128, 8, 1024], BF16, "wo")
        load_w(wo, wo_r, w_out_d[l], 8, 1024, stage)
        ys = [rot(2, [128, 4, 512], BF16, f"y{n}") for n in range(3)]
        gs = [rot(2, [128, 8, 512], BF16, f"g{n}") for n in range(3)]
        hts = rot(2, [128, 8, 512], F32, "h")
        m32 = rot(2, [128, 512], F32, "m32")
        tms = rot(3, [128, 512], F32, "tm")
        mbs = rot(2, [128, 8, 512], BF16, "mb")
        HTv = HT.rearrange("(c p) t -> p c t", p=128)
        bi = 0
        for ti, (t0, T, isc, ac0) in enumerate(tiles):
            if isc and l == L - 1:
                continue
            col = 1 if isc else 0
            yy = []
            gg = []
            for n, Yd in enumerate((YM, YG, YR)):
                y, yr = ys[n].next()
                load(y[:, :, :T], Yd.rearrange("(c p) t -> p c t", p=128)[:, :, t0:t0 + T], yr)
                g, gr = gs[n].next()
                load(g[:, :, :T], GT[n].rearrange("(c p) t -> p c t", p=128)[:, :, t0:t0 + T], gr)
                yy.append((y, yr))
                gg.append((g, gr))
            h, hr = hts.next()
            load(h[:, :, :T], HTv[:, :, t0:t0 + T], hr)
            mb, mbr = mbs.next()
            for c in range(8):
                m, mr = m32.next()
                for n in range(3):
                    b = bi % 3
                    bi += 1
                    for k in range(4):
                        mm(PS[b][:, :T], wbr[:, n * 4 + k, c * 128:(c + 1) * 128], yy[n][0][:, k, :T], k == 0, k == 3, [wbr_r, yy[n][1]], PR[b])
                    if n == 0:
                        tt("dve", m[:, :T], PS[b][:, :T], gg[0][0][:, c, :T], ALU.mult, [PR[b], gg[0][1]], mr)
                    else:
                        tm, tmr = tms.next()
                        tt("dve", tm[:, :T], PS[b][:, :T], gg[n][0][:, c, :T], ALU.mult, [PR[b], gg[n][1]], tmr)
                        if n == 1:
                            tt("pool", m[:, :T], m[:, :T], tm[:, :T], ALU.add, [mr, tmr], mr)
                        else:
                            tt("pool", mb[:, c, :T], m[:, :T], tm[:, :T], ALU.add, [mr, tmr], mbr)
            for c in range(8):
                b = 3 + (c % 2)
                for k in range(8):
                    mm(PS[b][:, :T], wo[:, k, c * 128:(c + 1) * 128], mb[:, k, :T], k == 0, k == 7, [wo_r, mbr], PR[b])
                stt("dve", h[:, c, :T], PS[b][:, :T], modt[:, 16 + c, col:col + 1], h[:, c, :T], ALU.mult, ALU.add, [PR[b], mod_r, hr], hr)
            store(HTv[:, :, t0:t0 + T], h[:, :, :T], hr)
        sc.barrier()

    def ffn_phase(l):
        A.reset()
        stage = rot(2, [128, 2048], F32, "stg")
        wfi, wfi_r = A.tile([128, 8, 2 * DFF], BF16, "wfi")
        wfo, wfo_r = A.tile([128, NJ, 1024], BF16, "wfo")
        load_w(wfi, wfi_r, w_fi_d[l], 8, 2 * DFF, stage)
        load_w(wfo, wfo_r, w_fo_d[l], NJ, 1024, stage)
        FT = 256
        ats = rot(2, [128, 8, FT + 2], BF16, "a2")
        hts = rot(2, [128, 8, FT], F32, "h")
        hid = rot(1, [128, NJ, FT], BF16, "hid")
        gsb = rot(2, [128, FT + 2], F32, "gsb")
        acc = rot(2, [128, FT], F32, "acc")
        gel = rot(2, [128, FT], F32, "gel")
        ATv = AT.rearrange("(c p) t -> p c t", p=128)
        HTv = HT.rearrange("(c p) t -> p c t", p=128)
        OTv = outT.rearrange("(c p) t -> p c t", p=128)
        last = (l == L - 1)
        nlat = S_ // FT
        for fi, (t0, T, isc, ac0) in enumerate(ftiles):
            if isc and last:
                continue
            col = 1 if isc else 0
            a, ar = ats.next()
            load(a[:, :, :], ATv[:, :, ac0 - 1:ac0 + T + 1], ar)
            first = (t0 == 0) or (t0 == CTX)
            lastt = (t0 + T == CTX) or (t0 + T == NT)
            if first:
                memset("pool", a[:, :, 0:1], 0.0, ar)
            if lastt:
                memset("pool", a[:, :, T + 1:T + 2], 0.0, ar)
            h, hr = hts.next()
            load(h[:, :, :], HTv[:, :, t0:t0 + T], hr)
            hd, hdr = hid.next()
            for j in range(NJ):
                gb = j % 2
                ub = 2 + (j % 2)
                for k in range(8):
                    mm(PS[gb][:, 0:T], wfi[:, k, j * 128:(j + 1) * 128], a[:, k, 1:T + 1], k == 0, k == 7, [wfi_r, ar], PR[gb])
                for k in range(8):
                    mm(PS[4][:, 0:2], wfi[:, k, j * 128:(j + 1) * 128], a[:, k, 0:T + 2:T + 1], k == 0, k == 7, [wfi_r, ar], PR[4])
                for k in range(8):
                    mm(PS[ub][:, 0:T], wfi[:, k, DFF + j * 128:DFF + (j + 1) * 128], a[:, k, 1:T + 1], k == 0, k == 7, [wfi_r, ar], PR[ub])
                g, gr = gsb.next()
                cp("act", g[:, 1:T + 1], PS[gb][:, 0:T], [PR[gb]], gr)
                cp("act", g[:, 0:T + 2:T + 1], PS[4][:, 0:2], [PR[4]], gr)
                ac, acr = acc.next()
                ts("dve", ac[:], g[:, 0:T], ppc("wdw", j), ppc("bdw", j), ALU.mult, ALU.add, [gr, pp_r], acr)
                stt("dve", ac[:], g[:, 1:T + 1], ppc("wdw", NJ + j), ac[:], ALU.mult, ALU.add, [gr, pp_r, acr], acr)
                stt("dve", ac[:], g[:, 2:T + 2], ppc("wdw", 2 * NJ + j), ac[:], ALU.mult, ALU.add, [gr, pp_r, acr], acr)
                ge, ger = gel.next()
                act(ge[:], ac[:], AF.Gelu_apprx_tanh, [acr], ger)
                tt("dve", hd[:, j, :], PS[ub][:, 0:T], ge[:], ALU.mult, [PR[ub], ger], hdr)
            for c in range(8):
                b = 5 + (c % 2)
                for j in range(NJ):
                    mm(PS[b][:, 0:T], wfo[:, j, c * 128:(c + 1) * 128], hd[:, j, :], j == 0, j == NJ - 1, [wfo_r, hdr], PR[b])
                stt("dve", h[:, c, :], PS[b][:, 0:T], modt[:, 40 + c, col:col + 1], h[:, c, :], ALU.mult, ALU.add, [PR[b], mod_r, hr], hr)
            if last:
                store(OTv[:, :, t0 - CTX:t0 - CTX + T], h[:, :, :], hr)
            else:
                store(HTv[:, :, t0:t0 + T], h[:, :, :], hr)
        sc.barrier()

    plist = []
    for l in range(L):
        plist += [(mod_phase, (l,)), (norm_phase, (l, 1)), (inproj_mla, (l,)), (inproj_lin, (l, 0)), (inproj_lin, (l, 1)),
                  (inproj_gates, (l,)), (attention, (l,)), (sweep, (l, 0)), (sweep, (l, 1)), (merge_phase, (l,)),
                  (norm_phase, (l, 2)), (ffn_phase, (l,))]
    for f, a in plist[:nphase]:
        f(*a)
    print(f"[build] sbuf peak {A.peak}", flush=True)
    sc.emit()
    return nc


def _pc(v, n):
    return np.ascontiguousarray(np.asarray(v, np.float32).reshape(n, 128).T)


def _consts():
    c = np.zeros((128, NCST), np.float32)
    c[:, CS["ones"]:CS["ones"] + 128] = 1.0
    c[:, CS["ident"]:CS["ident"] + 128] = np.eye(128, dtype=np.float32)
    p = np.zeros((128, 128), np.float32)
    for i in range(64):
        p[64 + i, i] = -1.0
        p[i, 64 + i] = 1.0
    c[:, CS["p128"]:CS["p128"] + 128] = p
    j = np.arange(128)[:, None]
    i = np.arange(128)[None, :]
    c[:, CS["maskF"]:CS["maskF"] + 128] = (j <= i)
    c[:, CS["maskB"]:CS["maskB"] + 128] = (j > i)
    c[:, CS["uF"]:CS["uF"] + 128] = (j <= i) * (-1.0 / 16.0)
    c[:, CS["uB"]:CS["uB"] + 128] = (j >= i) * (-1.0 / 16.0)
    c[:, CS["idxF"]:CS["idxF"] + 128] = (i + 1.0) * np.ones((128, 1))
    c[:, CS["idxB"]:CS["idxB"] + 128] = (128.0 - i) * np.ones((128, 1))
    p96 = np.zeros((128, 96), np.float32)
    for base in (64, 80):
        for t in range(8):
            p96[base + 8 + t, base + t] = -1.0
            p96[base + t, base + 8 + t] = 1.0
    c[:, CS["p96"]:CS["p96"] + 96] = p96
    es = np.zeros((128, 96), np.float32)
    for t in range(32):
        es[t, 64 + t] = 1.0
    c[:, CS["esel"]:CS["esel"] + 96] = es
    s65 = np.zeros((128, 64), np.float32)
    s65[64, :] = 1.0
    c[:, CS["sel65"]:CS["sel65"] + 64] = s65
    return c


def _tables(S_, CTX):
    NT = CTX + S_
    f32 = np.float32
    t = np.arange(S_)
    row = (t // 64).astype(f32)
    colp = (t % 64).astype(f32)
    inv = (f32(10000.0) ** (-(np.arange(8, dtype=f32) * f32(2.0) / f32(16)))).astype(f32)
    mc = np.ones((96, NT), f32)
    ms = np.zeros((96, NT), f32)
    ar = (row[:, None] * inv[None, :]).astype(f32)
    ac = (colp[:, None] * inv[None, :]).astype(f32)
    for i in range(8):
        for base, ang in ((64, ar), (80, ac)):
            mc[base + i, CTX:] = np.cos(ang[:, i]); mc[base + 8 + i, CTX:] = np.cos(ang[:, i])
            ms[base + i, CTX:] = np.sin(ang[:, i]); ms[base + 8 + i, CTX:] = np.sin(ang[:, i])
    pos = np.arange(NT).astype(f32)
    rinv = (f32(1.0) / (f32(10000.0) ** np.linspace(0.0, 1.0, 64, dtype=f32))).astype(f32)
    ang = (pos[:, None] * rinv[None, :]).astype(f32)
    rc = np.concatenate([np.cos(ang), np.cos(ang)], 1).T.astype(f32)
    rs_ = np.concatenate([np.sin(ang), np.sin(ang)], 1).T.astype(f32)
    return np.ascontiguousarray(mc), np.ascontiguousarray(ms), np.ascontiguousarray(rc), np.ascontiguousarray(rs_)


def _pack_pp(inp, L):
    pp = np.zeros((L, 128, NPP), np.float32)
    for l in range(L):
        def put(name, arr):
            arr = np.asarray(arr, np.float32)
            pp[l, :arr.shape[0], PP[name]:PP[name] + arr.shape[1]] = arr
        put("bada", _pc(inp["b_ada"][l], 48))
        put("n1w", _pc(inp["norm1_w"][l], 8))
        put("n2w", _pc(inp["norm2_w"][l], 8))
        put("bgate", _pc(np.asarray(inp["b_gate"][l]).reshape(-1), 24))
        put("qna", _pc(inp["mla_q_norm_a"][l], 2))
        put("kvna", _pc(inp["mla_kv_norm_a"][l], 1))
        put("qn", np.asarray(inp["mla_q_norm"][l]).reshape(96, 1))
        put("kn", np.asarray(inp["mla_k_norm"][l]).reshape(96, 1))
        put("gon", _pc(inp["gla_o_norm"][l], 1))
        put("wdw", _pc(np.asarray(inp["w_dw"][l]).reshape(-1), 66))
        put("bdw", _pc(inp["b_dw"][l], 22))
        put("retdec", np.broadcast_to(np.asarray(inp["ret_decay"][l]).reshape(1, 8), (128, 8)))
        put("bgk", np.broadcast_to(np.asarray(inp["gla_b_gk"][l]).reshape(1, 1024), (128, 1024)))
    return pp


_NC_CACHE = {}


def _run(inp, S_, CTX, L, ncores, debug=False, nphase=999):
    key = (S_, CTX, L, debug, nphase)
    if key not in _NC_CACHE:
        _NC_CACHE[key] = build(S_, CTX, L, debug, nphase)
    nc = _NC_CACHE[key]
    B = np.asarray(inp["x"]).shape[0]
    mc, ms, rc, rs_ = _tables(S_, CTX)
    shared = {
        "pp": _pack_pp(inp, L), "cst": _consts(), "mlaC": mc, "mlaS": ms, "retC": rc, "retS": rs_,
    }
    for k in ("w_ada", "w_in", "mla_w_qb", "mla_w_kvb", "gla_w_gk2", "w_branch", "w_out", "w_ffn_in", "w_ffn_out"):
        shared[k] = np.ascontiguousarray(np.asarray(inp[k], np.float32))
    cc = _pc(inp["c_ctx"], 8)
    in_maps = []
    for core in range(ncores):
        b = core % B
        m = dict(shared)
        m["xT"] = np.ascontiguousarray(np.asarray(inp["x"][b], np.float32).T)
        m["ctxT"] = np.ascontiguousarray(np.asarray(inp["ctx"][b], np.float32).T)
        m["cvec"] = np.ascontiguousarray(np.concatenate([_pc(inp["c"][b], 8), cc], 1))
        in_maps.append(m)
    res = run_bass_kernel_spmd(nc, in_maps, core_ids=list(range(ncores)))
    return res


def kernel(**inputs):
    x = np.asarray(inputs["x"])
    B, S_, _ = x.shape
    CTX = np.asarray(inputs["ctx"]).shape[1]
    L = np.asarray(inputs["w_ada"]).shape[0]
    res = _run(inputs, S_, CTX, L, 8)
    out = np.empty((B, S_, D), np.float32)
    for b in range(B):
        out[b] = res.results[b]["outT"].T
    return out
```

```python
import concourse.bass as bass
import concourse.mybir as mybir

SEM_LIMIT = 1000000000
DMA_K = 12


class Res:
    __slots__ = ("name", "lw", "rd_c", "rd_d", "excl")

    def __init__(self, name, excl=False):
        self.name = name
        self.excl = excl
        self.lw = None
        self.rd_c = {}
        self.rd_d = []


class Op:
    __slots__ = ("eng", "fn", "reads", "writes", "dma", "acc", "deps", "sig", "ev", "waits", "barrier")

    def __init__(self, eng, fn, reads, writes, dma, acc):
        self.eng = eng
        self.fn = fn
        self.reads = reads
        self.writes = writes
        self.dma = dma
        self.acc = acc
        self.deps = set()
        self.sig = False
        self.ev = None
        self.waits = []
        self.barrier = False


class Sched:
    ENGS = ("pe", "act", "dve", "pool", "sp")

    def __init__(self, nc):
        self.nc = nc
        self.ops = []

    def op(self, eng, fn, reads=(), writes=(), acc=False):
        self.ops.append(Op(eng, fn, list(reads), list(writes), False, acc))

    def dma(self, q, out, in_, reads=(), writes=()):
        self.ops.append(Op(q, lambda e, o=out, i=in_: e.dma_start(out=o, in_=i), list(reads), list(writes), True, False))

    def custom_dma(self, q, fn, reads=(), writes=()):
        self.ops.append(Op(q, fn, list(reads), list(writes), True, False))

    def barrier(self):
        o = Op(None, None, [], [], False, False)
        o.barrier = True
        self.ops.append(o)

    def _analyse(self):
        ops = self.ops
        last_c = {}
        last_d = {e: [] for e in self.ENGS}
        pend = {e: set() for e in self.ENGS}
        for i, op in enumerate(ops):
            if op.barrier:
                deps = set(last_c.values())
                for q in self.ENGS:
                    deps.update(last_d[q][-DMA_K:])
                for e in self.ENGS:
                    pend[e] |= deps
                continue
            d = op.deps
            if pend[op.eng]:
                d |= pend[op.eng]
                pend[op.eng] = set()
            for r in op.reads:
                if r.lw is not None:
                    d.add(r.lw)
                if r.excl:
                    for e2, j in r.rd_c.items():
                        if e2 != op.eng:
                            d.add(j)
            for w in op.writes:
                if w.lw is not None:
                    lwop = ops[w.lw]
                    if not (op.acc and op.eng == "pe" and lwop.eng == "pe" and not lwop.dma):
                        d.add(w.lw)
                d.update(w.rd_c.values())
                d.update(w.rd_d)
            d.discard(i)
            for r in op.reads:
                if op.dma:
                    r.rd_d.append(i)
                else:
                    r.rd_c[op.eng] = i
            for w in op.writes:
                w.lw = i
                w.rd_c = {}
                w.rd_d = []
            if op.dma:
                last_d[op.eng].append(i)
            else:
                last_c[op.eng] = i
            for j in d:
                ops[j].sig = True
        self.final_deps = set(last_c.values())
        for q in self.ENGS:
            self.final_deps.update(last_d[q][-DMA_K:])
        for j in self.final_deps:
            ops[j].sig = True

    def _assign(self):
        nc = self.nc
        ops = self.ops
        csem = {}
        ccnt = {}
        dsem = {e: [None] * DMA_K for e in self.ENGS}
        dcnt = {e: [0] * DMA_K for e in self.ENGS}
        dprev = {e: [None] * DMA_K for e in self.ENGS}
        dn = {e: 0 for e in self.ENGS}
        waited = {e: {} for e in self.ENGS}
        self.nsem = 0

        def newsem(tag):
            self.nsem += 1
            return nc.alloc_semaphore(name=f"s_{tag}_{self.nsem}")

        def add_wait(op, ev):
            sem, val = ev
            w = waited[op.eng]
            k = id(sem)
            if w.get(k, 0) >= val:
                return
            w[k] = val
            op.waits.append((sem, val))

        for i, op in enumerate(ops):
            if op.barrier:
                continue
            e = op.eng
            for j in sorted(op.deps):
                add_wait(op, ops[j].ev)
            if op.dma:
                k = dn[e] % DMA_K
                dn[e] += 1
                if dprev[e][k] is not None:
                    add_wait(op, dprev[e][k])
                if dsem[e][k] is None or dcnt[e][k] + 16 > SEM_LIMIT:
                    dsem[e][k] = newsem("d" + e)
                    dcnt[e][k] = 0
                dcnt[e][k] += 16
                op.ev = (dsem[e][k], dcnt[e][k])
                dprev[e][k] = op.ev
                op.sig = True
            elif op.sig:
                if e not in csem or ccnt[e] + 1 > SEM_LIMIT:
                    csem[e] = newsem("c" + e)
                    ccnt[e] = 0
                ccnt[e] += 1
                op.ev = (csem[e], ccnt[e])
        print("[sched] ccnt", ccnt, "dcnt max", {e: max(v) for e, v in dcnt.items()}, flush=True)
        self.final_waits = []
        fw = {}
        for j in sorted(self.final_deps):
            sem, val = ops[j].ev
            k = id(sem)
            if fw.get(k, (None, 0))[1] < val:
                fw[k] = (sem, val)
        self.final_waits = list(fw.values())

    def emit(self):
        self._analyse()
        self._assign()
        nc = self.nc
        ops = self.ops

        def run(e):
            def body(eng):
                for op in ops:
                    if op.barrier or op.eng != e:
                        continue
                    for sem, val in op.waits:
                        eng.wait_ge(sem, val)
                    ins = op.fn(eng)
                    if op.sig:
                        ins.then_inc(op.ev[0], 16 if op.dma else 1)
                if e == "sp":
                    for sem, val in self.final_waits:
                        eng.wait_ge(sem, val)
            return body

        with nc.Block() as block:
            block.tensor(run("pe"))
            block.scalar(run("act"))
            block.vector(run("dve"))
            block.gpsimd(run("pool"))
            block.sync(run("sp"))
        n = sum(1 for o in ops if not o.barrier)
        nw = sum(len(o.waits) for o in ops if not o.barrier)
        print(f"[sched] ops={n} waits={nw} sems={self.nsem}", flush=True)


class SbufAlloc:
    def __init__(self, nc, base=16640, limit=229376):
        self.nc = nc
        self.base = base
        self.off = base
        self.limit = limit
        self.n = 0
        self.peak = 0

    def reset(self, to=None):
        self.off = self.base if to is None else to

    def mark(self):
        return self.off

    def tile(self, shape, dtype, name="t"):
        esz = {mybir.dt.float32: 4, mybir.dt.bfloat16: 2}[dtype]
        nb = esz
        for s in shape[1:]:
            nb *= s
        nb = (nb + 63) // 64 * 64
        assert self.off + nb <= self.limit, f"SBUF overflow {name}: {self.off}+{nb}>{self.limit}"
        self.n += 1
        h = self.nc.alloc_sbuf_tensor_at(f"{name}_{self.n}", list(shape), dtype, offset=self.off)
        self.off += nb
        self.peak = max(self.peak, self.off)
        r = Res(f"{name}_{self.n}")
        return h, r


import os
import numpy as np
DBG = os.environ.get('KDBG', '')
from concourse.bass_utils import run_bass_kernel_spmd

F32 = mybir.dt.float32
BF16 = mybir.dt.bfloat16
ALU = mybir.AluOpType
AF = mybir.ActivationFunctionType

D = 1024
KC = 8
NIN = 7616
DFF = 2816
NJ = 22
EPS = 1e-6
C_MQ, C_MKV, C_MKR, C_GQ, C_GK, C_GV, C_GG, C_RF, C_RB, C_RQ, C_RK, C_RV, C_RG, C_G0 = (
    0, 256, 384, 416, 928, 1440, 1952, 2464, 2480, 2496, 3008, 3520, 4032, 4544)

PP = {}
_o = 0
for _n, _w in (("bada", 48), ("n1w", 8), ("n2w", 8), ("bgate", 24), ("qna", 2), ("kvna", 1), ("qn", 1), ("kn", 1),
               ("gon", 1), ("wdw", 66), ("bdw", 22), ("retdec", 8), ("bgk", 1024)):
    PP[_n] = _o
    _o += _w
NPP = _o
CS = {}
_o = 0
for _n, _w in (("ones", 128), ("ident", 128), ("p128", 128), ("maskF", 128), ("maskB", 128), ("uF", 128), ("uB", 128),
               ("idxF", 128), ("idxB", 128), ("p96", 96), ("esel", 96), ("sel65", 64)):
    CS[_n] = _o
    _o += _w
NCST = _o


def build(S_, CTX, L, debug=False, nphase=999):
    NT = CTX + S_
    NTP = NT + 4
    nlt = S_ // 512
    tiles = [(0, CTX, True, 1)] + [(CTX + 512 * i, 512, False, CTX + 3 + 512 * i) for i in range(nlt)]
    ftiles = [(0, 256, True, 1)] if CTX == 256 else [(i * 256, 256, True, 1 + i * 256) for i in range(CTX // 256)]
    ftiles = ftiles + [(CTX + 256 * i, 256, False, CTX + 3 + 256 * i) for i in range(S_ // 256)]
    NKT = NT // 128

    nc = bass.Bass("TRN2", target_bir_lowering=False)
    sc = Sched(nc)
    A = SbufAlloc(nc)

    def din(name, shape, dt=F32):
        return nc.dram_tensor(name, list(shape), dt, kind="ExternalInput").ap()

    skind = "ExternalOutput" if debug else "Internal"

    def dscr(name, shape, dt):
        return nc.dram_tensor(name, list(shape), dt, kind=skind).ap()

    xT = din("xT", [D, S_])
    ctxT = din("ctxT", [D, CTX])
    cvec_d = din("cvec", [128, 16])
    pp_d = din("pp", [L, 128, NPP])
    cst_d = din("cst", [128, NCST])
    mlaC_d = din("mlaC", [96, NT])
    mlaS_d = din("mlaS", [96, NT])
    retC_d = din("retC", [128, NT])
    retS_d = din("retS", [128, NT])
    w_ada_d = din("w_ada", [L, D, 6 * D])
    w_in_d = din("w_in", [L, D, NIN])
    w_qb_d = din("mla_w_qb", [L, 256, 768])
    w_kvb_d = din("mla_w_kvb", [L, 128, 1024])
    w_gk2_d = din("gla_w_gk2", [L, 2, 16, 512])
    w_br_d = din("w_branch", [L, 3, 512, D])
    w_out_d = din("w_out", [L, D, D])
    w_fi_d = din("w_ffn_in", [L, D, 2 * DFF])
    w_fo_d = din("w_ffn_out", [L, DFF, D])
    outT = nc.dram_tensor("outT", [D, S_], F32, kind="ExternalOutput").ap()

    HT = dscr("HT", [D, NT], F32)
    AT = dscr("AT", [D, NTP], BF16)
    QT = dscr("QT", [8, 96, NT], BF16)
    KT = dscr("KT", [8, 96, NT], BF16)
    VA = dscr("VA", [8, NT, 65], BF16)
    GQ = dscr("GQ", [512, NT], BF16)
    GK = dscr("GK", [512, NT], BF16)
    GG = dscr("GG", [512, NT], BF16)
    RQ = dscr("RQ", [512, NT], BF16)
    RK = dscr("RK", [512, NT], BF16)
    RG = dscr("RG", [512, NT], BF16)
    GV = dscr("GV", [NT, 512], BF16)
    RV = dscr("RV", [NT, 512], BF16)
    SPL = dscr("SPL", [2, NT, 512], F32)
    GT = dscr("GT", [3, D, NT], BF16)
    YM = dscr("YM", [512, NT], BF16)
    YG = dscr("YG", [512, NT], BF16)
    YR = dscr("YR", [512, NT], BF16)
    OF = dscr("OF", [512, NT], F32)
    of_res = {}

    PS = [nc.alloc_psum_tensor(f"ps{i}", [128, 512], F32) for i in range(7)]
    PR = [Res(f"ps{i}", True) for i in range(7)]
    PSB = nc.alloc_psum_tensor("psb", [128, 1024], BF16)
    _pb = Res("psb", True)
    PBR = [_pb, _pb]

    def mm(out, lhsT, rhs, start, stop, reads, wres):
        sc.op("pe", lambda e: e.matmul(out, lhsT=lhsT, rhs=rhs, start=start, stop=stop), reads=reads, writes=[wres], acc=True)

    def act(out, in_, func, reads, wres, bias=None, scale=None, eng="act"):
        kw = {}
        if bias is not None:
            kw["bias"] = bias
        if scale is not None:
            kw["scale"] = scale
        sc.op("act", lambda e: e.activation(out=out, in_=in_, func=func, **kw), reads=reads, writes=[wres])

    def cp(eng, out, in_, reads, wres):
        if eng == "act":
            sc.op("act", lambda e: e.copy(out=out, in_=in_), reads=reads, writes=[wres])
        else:
            sc.op(eng, lambda e: e.tensor_copy(out=out, in_=in_), reads=reads, writes=[wres])

    def tt(eng, out, in0, in1, op, reads, wres):
        sc.op(eng, lambda e: e.tensor_tensor(out=out, in0=in0, in1=in1, op=op), reads=reads, writes=[wres])

    def stt(eng, out, in0, scalar, in1, op0, op1, reads, wres):
        sc.op(eng, lambda e: e.scalar_tensor_tensor(out=out, in0=in0, scalar=scalar, in1=in1, op0=op0, op1=op1),
              reads=reads, writes=[wres])

    def ts(eng, out, in0, s1, s2, op0, op1, reads, wres):
        sc.op(eng, lambda e: e.tensor_scalar(out=out, in0=in0, scalar1=s1, scalar2=s2, op0=op0, op1=op1),
              reads=reads, writes=[wres])

    def ts1(eng, out, in0, s1, op0, reads, wres):
        sc.op(eng, lambda e: e.tensor_single_scalar(out=out, in_=in0, scalar=s1, op=op0), reads=reads, writes=[wres])

    def memset(eng, ap, val, wres):
        sc.op(eng, lambda e: e.memset(ap, val), writes=[wres])

    def load(out, in_, wres, reads=()):
        sc.dma("sp", out, in_, reads=reads, writes=[wres])

    def store(out, in_, rres, writes=()):
        sc.dma("pool", out, in_, reads=[rres], writes=writes)

    class Rot:
        def __init__(self, items):
            self.items = items
            self.i = 0

        def next(self):
            it = self.items[self.i % len(self.items)]
            self.i += 1
            return it

    def rot(n, shape, dt, name):
        return Rot([A.tile(shape, dt, name) for _ in range(n)])

    cast_i = [0]

    def load_w(dst, dst_res, src, kc, n, stage):
        rows = src.shape[0] // kc
        for k in range(kc):
            for n0 in range(0, n, 2048):
                w = min(2048, n - n0)
                st, sr = stage.next()
                load(st[:rows, :w], src[k * rows:(k + 1) * rows, n0:n0 + w], sr)
                eng = ("dve", "pool", "act")[cast_i[0] % 3]
                cast_i[0] += 1
                cp(eng, dst[:rows, k, n0:n0 + w], st[:rows, :w], [sr], dst_res)

    cst, cst_r = A.tile([128, NCST], F32, "cst")
    load(cst[:], cst_d, cst_r)
    cbf, cbf_r = A.tile([128, NCST], BF16, "cbf")
    cp("dve", cbf[:], cst[:], [cst_r], cbf_r)
    epst, eps_r = A.tile([128, 1], F32, "eps")
    memset("dve", epst[:], EPS, eps_r)
    onec, onec_r = A.tile([128, 1], F32, "onec")
    memset("dve", onec[:], 1.0, onec_r)
    ppt, pp_r = A.tile([128, NPP], F32, "pp")
    cvt, cv_r = A.tile([128, 16], F32, "cvec")
    cond2, cond_r = A.tile([128, 8, 2], F32, "cond2")
    modt, mod_r = A.tile([128, 48, 2], F32, "mod")
    g1t, g1_r = A.tile([128, 8, 2], F32, "g1")
    g2t, g2_r = A.tile([128, 8, 2], F32, "g2")
    lgt, lg_r = A.tile([128, 8], F32, "lg")
    nlgt, nlg_r = A.tile([128, 8], F32, "nlg")
    A.base = A.off

    def C_(name, rows=128, w=None, bf=True):
        o = CS[name]
        w = w if w is not None else (128 if name not in ("p96", "esel", "sel65") else (96 if name != "sel65" else 64))
        t = cbf if bf else cst
        return t[0:rows, o:o + w]

    ones_bf = C_("ones")
    ident_bf = C_("ident")
    CRES = [cst_r, cbf_r]

    def ppc(name, col, rows=128):
        return ppt[0:rows, PP[name] + col:PP[name] + col + 1]

    def partnorm(src_list, srcres, rows, nfeat, T, sqt, sq_r, ss_bank, lnt, ln_r, rst, rs_r):
        n = len(src_list)
        for i, s in enumerate(src_list):
            act(sqt[0:rows, i, :T], s, AF.Square, srcres, sq_r)
        for i in range(n):
            mm(PS[ss_bank][0:rows, :T], cbf[0:rows, CS["ones"]:CS["ones"] + rows], sqt[0:rows, i, :T], i == 0, i == n - 1,
               [sq_r, cbf_r], PR[ss_bank])
        act(lnt[0:rows, :T], PS[ss_bank][0:rows, :T], AF.Ln, [PR[ss_bank], eps_r], ln_r, bias=epst[0:rows, 0:1], scale=1.0 / nfeat)
        act(rst[0:rows, :T], lnt[0:rows, :T], AF.Exp, [ln_r], rs_r, scale=-0.5)

    sc.dma("sp", HT[:, 0:CTX], ctxT, reads=[], writes=[])
    for i in range(0, S_, 2048):
        w = min(2048, S_ - i)
        sc.dma("sp", HT[:, CTX + i:CTX + i + w], xT[:, i:i + w], reads=[], writes=[])
    load(cvt[:], cvec_d, cv_r)
    act(cond2[:, :, 0], cvt[:, 0:8], AF.Silu, [cv_r], cond_r)
    act(cond2[:, :, 1], cvt[:, 8:16], AF.Silu, [cv_r], cond_r)
    sc.barrier()

    def norm_phase(l, which):
        A.reset()
        hts = rot(2, [128, 8, 512], F32, "h")
        sqs = rot(2, [128, 8, 512], BF16, "sq")
        lns = rot(2, [128, 512], F32, "ln")
        rss = rot(2, [128, 512], F32, "rs")
        tms = rot(2, [128, 8, 512], F32, "tm")
        abs_ = rot(2, [128, 8, 514], BF16, "a")
        for ab_, abr_ in abs_.items:
            memset("pool", ab_[:, :, 0:1], 0.0, abr_)
        gt = g1t if which == 1 else g2t
        gr = g1_r if which == 1 else g2_r
        shv = 0 if which == 1 else 3
        HTv = HT.rearrange("(c p) t -> p c t", p=128)
        ATv = AT.rearrange("(c p) t -> p c t", p=128)
        for ti, (t0, T, isc, ac0) in enumerate(tiles):
            if isc and l == L - 1 and which == 2:
                continue
            col = 1 if isc else 0
            h, hr = hts.next()
            load(h[:, :, :T], HTv[:, :, t0:t0 + T], hr)
            sq, sqr = sqs.next()
            lnv, lnr = lns.next()
            rs, rsr = rss.next()
            bank = ti % 2
            partnorm([h[:, c, :T] for c in range(8)], [hr], 128, D, T, sq, sqr, bank, lnv, lnr, rs, rsr)
            tm, tmr = tms.next()
            tt("dve", tm[:, :, :T], h[:, :, :T], rs[:, :T].unsqueeze(1).to_broadcast([128, 8, T]), ALU.mult, [hr, rsr], tmr)
            ab, abr = abs_.next()
            first = (t0 == 0) or (t0 == CTX)
            lastt = (t0 + T == CTX) or (t0 + T == NT)
            for c in range(8):
                if c % 2 == 0:
                    act(ab[:, c, 1:1 + T], tm[:, c, :T], AF.Identity, [tmr, gr, mod_r], abr,
                        bias=modt[:, shv * 8 + c, col:col + 1], scale=gt[:, c, col:col + 1])
                else:
                    ts("pool", ab[:, c, 1:1 + T], tm[:, c, :T], gt[:, c, col:col + 1], modt[:, shv * 8 + c, col:col + 1],
                       ALU.mult, ALU.add, [tmr, gr, mod_r], abr)
            if lastt:
                memset("pool", ab[:, :, T + 1:T + 2], 0.0, abr)
            lo = 0 if first else 1
            hi = T + 2 if lastt else T + 1
            store(ATv[:, :, ac0 - 1 + lo:ac0 - 1 + hi], ab[:, :, lo:hi], abr)
        sc.barrier()

    def mod_phase(l):
        A.reset()
        load(ppt[:], pp_d[l], pp_r)
        ws = rot(2, [128, 8, 1024], F32, "wada")
        wv = w_ada_d[l].rearrange("(c p) n -> p c n", p=128)
        for g in range(6):
            w, wr = ws.next()
            for k in range(0, 8, 2):
                load(w[:, k:k + 2, :], wv[:, k:k + 2, g * 1024:(g + 1) * 1024], wr)
            for nn in range(8):
                j = g * 8 + nn
                for k in range(8):
                    mm(PS[0][:, 2 * j:2 * j + 2], w[:, k, nn * 128:(nn + 1) * 128], cond2[:, k, :], k == 0, k == 7,
                       [wr, cond_r], PR[0])
        psv = PS[0][:, 0:96].rearrange("p (j t) -> p j t", t=2)
        for col in range(2):
            tt("dve", modt[:, :, col], psv[:, :, col], ppt[:, PP["bada"]:PP["bada"] + 48], ALU.add, [PR[0], pp_r], mod_r)
        for col in range(2):
            stt("dve", g1t[:, :, col], modt[:, 8:16, col], 1.0, ppt[:, PP["n1w"]:PP["n1w"] + 8], ALU.add, ALU.mult,
                [mod_r, pp_r], g1_r)
            stt("dve", g2t[:, :, col], modt[:, 32:40, col], 1.0, ppt[:, PP["n2w"]:PP["n2w"] + 8], ALU.add, ALU.mult,
                [mod_r, pp_r], g2_r)
        act(nlgt[:], ppt[:, PP["retdec"]:PP["retdec"] + 8], AF.Exp, [pp_r], nlg_r)
        ts1("dve", lgt[:], nlgt[:], -1.0, ALU.mult, [nlg_r], lg_r)
        sc.barrier()

    def qk_finish(ps_bank, T, normcol, tabC, tabS, tab_r, out_ap, tmp):
        sq, sqr, lnv, lnr, rs, rsr, xn, xnr, t1, t1r, t2, t2r, ob, obr, ssb, swb = tmp
        partnorm([PS[ps_bank][0:96, :T]], [PR[ps_bank]], 96, 96, T, sq, sqr, ssb, lnv, lnr, rs, rsr)
        stt("dve", xn[0:96, :T], PS[ps_bank][0:96, :T], normcol, rs[0:96, :T], ALU.mult, ALU.mult, [PR[ps_bank], rsr, pp_r], xnr)
        mm(PS[swb][0:96, :T], cbf[0:96, CS["p96"]:CS["p96"] + 96], xn[0:96, :T], True, True, [xnr, cbf_r], PR[swb])
        tt("pool", t1[0:96, :T], xn[0:96, :T], tabC, ALU.mult, [xnr, tab_r], t1r)
        tt("dve", t2[0:96, :T], PS[swb][0:96, :T], tabS, ALU.mult, [PR[swb], tab_r], t2r)
        tt("pool", ob[0:96, :T], t1[0:96, :T], t2[0:96, :T], ALU.add, [t1r, t2r], obr)
        store(out_ap, ob[0:96, :T], obr)

    def inproj_mla(l):
        A.reset()
        stage = rot(2, [128, 2048], F32, "stg")
        win, win_r = A.tile([128, 8, 416], BF16, "winA")
        load_w(win, win_r, w_in_d[l][:, 0:416], 8, 416, stage)
        wqb, wqb_r = A.tile([128, 2, 768], BF16, "wqb")
        load_w(wqb, wqb_r, w_qb_d[l], 2, 768, stage)
        wkf, wkf_r = A.tile([128, 1024], F32, "wkf")
        load(wkf[:], w_kvb_d[l], wkf_r)
        wkn, wkn_r = A.tile([128, 8, 96], BF16, "wkn")
        memset("dve", wkn[:], 0.0, wkn_r)
        wkfv = wkf[:].rearrange("p (h e) -> p h e", e=128)
        cp("dve", wkn[:, :, 0:64], wkfv[:, :, 0:64], [wkf_r], wkn_r)
        wkv, wkv_r = A.tile([128, 8, 64], BF16, "wkv")
        cp("dve", wkv[:], wkfv[:, :, 64:128], [wkf_r], wkv_r)
        ats = rot(2, [128, 8, 512], BF16, "a")
        cqs = rot(2, [128, 2, 512], F32, "cq")
        sq2 = rot(2, [128, 2, 512], BF16, "sq2")
        lns = rot(2, [128, 512], F32, "ln")
        rss = rot(2, [128, 512], F32, "rs")
        cqn = rot(2, [128, 2, 512], BF16, "cqn")
        ckv = rot(2, [128, 512], F32, "ckv")
        ckn = rot(2, [128, 512], BF16, "ckn")
        krs = rot(2, [32, 512], BF16, "kr")
        tCs = rot(2, [96, 512], F32, "tC")
        tSs = rot(2, [96, 512], F32, "tS")
        sqh = rot(3, [96, 1, 512], BF16, "sqh")
        lnh = rot(3, [96, 512], F32, "lnh")
        rsh = rot(3, [96, 512], F32, "rsh")
        xnh = rot(3, [96, 512], BF16, "xnh")
        t1h = rot(3, [96, 512], F32, "t1h")
        t2h = rot(3, [96, 512], F32, "t2h")
        obh = rot(3, [96, 512], BF16, "obh")
        vas = rot(2, [128, 8, 65], BF16, "va")
        for v, vr in vas.items:
            memset("pool", v[:], 1.0, vr)
        ATv = AT.rearrange("(c p) t -> p c t", p=128)
        VAv = VA.rearrange("h t e -> t h e")
        hcount = [0]

        def tmpset():
            i = hcount[0]
            hcount[0] += 1
            sq, sqr = sqh.next(); lnv, lnr = lnh.next(); rs, rsr = rsh.next(); xn, xnr = xnh.next()
            t1, t1r = t1h.next(); t2, t2r = t2h.next(); ob, obr = obh.next()
            return (sq, sqr, lnv, lnr, rs, rsr, xn, xnr, t1, t1r, t2, t2r, ob, obr, 3 + (i % 2), 5 + (i % 2))

        for ti, (t0, T, isc, ac0) in enumerate(tiles):
            a, ar = ats.next()
            load(a[:, :, :T], ATv[:, :, ac0:ac0 + T], ar)
            tC, tCr = tCs.next()
            tS, tSr = tSs.next()
            load(tC[:, :T], mlaC_d[:, t0:t0 + T], tCr)
            load(tS[:, :T], mlaS_d[:, t0:t0 + T], tCr)
            cq, cqr = cqs.next()
            for c2 in range(2):
                for k in range(8):
                    mm(PS[c2][:, :T], win[:, k, c2 * 128:(c2 + 1) * 128], a[:, k, :T], k == 0, k == 7, [win_r, ar], PR[c2])
                cp("act", cq[:, c2, :T], PS[c2][:, :T], [PR[c2]], cqr)
            sq, sqr = sq2.next(); lnv, lnr = lns.next(); rs, rsr = rss.next()
            partnorm([cq[:, 0, :T], cq[:, 1, :T]], [cqr], 128, 256, T, sq, sqr, 2, lnv, lnr, rs, rsr)
            cn, cnr = cqn.next()
            for c2 in range(2):
                stt("dve", cn[:, c2, :T], cq[:, c2, :T], ppc("qna", c2), rs[:, :T], ALU.mult, ALU.mult, [cqr, rsr, pp_r], cnr)
            for k in range(8):
                mm(PS[0][:, :T], win[:, k, 256:384], a[:, k, :T], k == 0, k == 7, [win_r, ar], PR[0])
            kv, kvr = ckv.next()
            cp("act", kv[:, :T], PS[0][:, :T], [PR[0]], kvr)
            sq, sqr = sq2.next(); lnv, lnr = lns.next(); rs, rsr = rss.next()
            partnorm([kv[:, :T]], [kvr], 128, 128, T, sq, sqr, 2, lnv, lnr, rs, rsr)
            kn, knr = ckn.next()
            stt("dve", kn[:, :T], kv[:, :T], ppc("kvna", 0), rs[:, :T], ALU.mult, ALU.mult, [kvr, rsr, pp_r], knr)
            for k in range(8):
                mm(PS[1][0:32, :T], win[:, k, 384:416], a[:, k, :T], k == 0, k == 7, [win_r, ar], PR[1])
            kr, krr = krs.next()
            cp("act", kr[0:32, :T], PS[1][0:32, :T], [PR[1]], krr)
            for h in range(8):
                b = h % 3
                for c2 in range(2):
                    mm(PS[b][0:96, :T], wqb[:, c2, h * 96:(h + 1) * 96], cn[:, c2, :T], c2 == 0, c2 == 1, [wqb_r, cnr], PR[b])
                qk_finish(b, T, ppc("qn", 0, 96), tC[0:96, :T], tS[0:96, :T], tCr, QT[h][:, t0:t0 + T], tmpset())
            for h in range(8):
                b = h % 3
                mm(PS[b][0:96, :T], wkn[:, h, :], kn[:, :T], True, False, [wkn_r, knr], PR[b])
                mm(PS[b][0:96, :T], cbf[0:32, CS["esel"]:CS["esel"] + 96], kr[0:32, :T], False, True, [cbf_r, krr], PR[b])
                qk_finish(b, T, ppc("kn", 0, 96), tC[0:96, :T], tS[0:96, :T], tCr, KT[h][:, t0:t0 + T], tmpset())
            for j in range(T // 128):
                b = j % 2
                mm(PS[b][:, 0:512], kn[:, j * 128:(j + 1) * 128], wkv[:].rearrange("p h e -> p (h e)"), True, True, [knr, wkv_r], PR[b])
                va, var_ = vas.next()
                cp("act", va[:, :, 0:64], PS[b][:, 0:512].rearrange("p (h e) -> p h e", e=64), [PR[b]], var_)
                store(VAv[t0 + j * 128:t0 + (j + 1) * 128], va[:], var_)
        sc.barrier()

    def inproj_lin(l, kind):
        A.reset()
        stage = rot(2, [128, 2048], F32, "stg")
        c0 = C_GQ if kind == 0 else C_RQ
        ncols = 2080 if kind == 0 else 2048
        win, win_r = A.tile([128, 8, ncols], BF16, "winB")
        load_w(win, win_r, w_in_d[l][:, c0:c0 + ncols], 8, ncols, stage)
        if kind == 0:
            w2, w2_r = A.tile([16, 2, 512], BF16, "w2")
            for d in range(2):
                st, sr = stage.next()
                load(st[0:16, 0:512], w_gk2_d[l, d], sr)
                cp("dve", w2[0:16, d, :], st[0:16, 0:512], [sr], w2_r)
        ats = rot(2, [128, 8, 512], BF16, "a")
        stq = rot(2, [128, 4, 512], BF16, "stq")
        stv = rot(2, [128, 4, 512], BF16, "stv")
        if kind == 0:
            rfs = rot(2, [16, 512], BF16, "rf")
            rbs = rot(2, [16, 512], BF16, "rb")
            xbs = rot(2, [128, 512], F32, "xb")
            ees = rot(2, [128, 512], F32, "ee")
            sps = rot(2, [128, 4, 512], F32, "sp")
        else:
            tCs = rot(2, [128, 512], F32, "tC")
            tSs = rot(2, [128, 512], F32, "tS")
            xbf = rot(3, [128, 512], BF16, "xbf")
            t1s = rot(3, [128, 512], F32, "t1")
            t2s = rot(3, [128, 512], F32, "t2")
        ATv = AT.rearrange("(c p) t -> p c t", p=128)
        Qd, Kd, Gd, Vd = (GQ, GK, GG, GV) if kind == 0 else (RQ, RK, RG, RV)
        pbank = [0]

        def nb():
            pbank[0] += 1
            return pbank[0] % 4

        cpi = [0]
        for ti, (t0, T, isc, ac0) in enumerate(tiles):
            a, ar = ats.next()
            load(a[:, :, :T], ATv[:, :, ac0:ac0 + T], ar)
            if kind == 1 and 'notab' not in DBG:
                tC, tCr = tCs.next()
                tS, tSr = tSs.next()
                load(tC[:, :T], retC_d[:, t0:t0 + T], tCr)
                load(tS[:, :T], retS_d[:, t0:t0 + T], tSr)
            for wi, (dst, coff) in enumerate(((Qd, 0), (Kd, 512), (Gd, 1536))):
                st, sr = stq.next()
                for h in range(4):
                    b = nb()
                    for k in range(8):
                        mm(PS[b][:, :T], win[:, k, coff + h * 128:coff + (h + 1) * 128], a[:, k, :T], k == 0, k == 7, [win_r, ar], PR[b])
                    if wi == 2:
                        act(st[:, h, :T], PS[b][:, :T], AF.Silu, [PR[b]], sr)
                    elif kind == 0 or 'norope' in DBG:
                        cpi[0] += 1
                        cp("act" if cpi[0] % 2 else "dve", st[:, h, :T], PS[b][:, :T], [PR[b]], sr)
                    else:
                        xb_, xbr = xbf.next()
                        cp("act", xb_[:, :T], PS[b][:, :T], [PR[b]], xbr)
                        b2 = 4 + (h % 2)
                        if 'nomm' in DBG:
                            b2 = b
                        else:
                            mm(PS[b2][:, :T], C_("p128"), xb_[:, :T], True, True, [xbr, cbf_r], PR[b2])
                        t1, t1r = t1s.next()
                        t2, t2r = t2s.next()
                        tt("dve", t1[:, :T], PS[b][:, :T], tC[:, :T], ALU.mult, [PR[b], tCr], t1r)
                        tt("dve", t2[:, :T], PS[b2][:, :T], tS[:, :T], ALU.mult, [PR[b2], tSr], t2r)
                        tt("dve" if 'dveadd' in DBG else "pool", st[:, h, :T], t1[:, :T], t2[:, :T], ALU.add, [t1r, t2r], sr)
                store(dst.rearrange("(h p) t -> p h t", p=128)[:, :, t0:t0 + T], st[:, :, :T], sr)
            sv, svr = stv.next()
            nbk = T // 128
            for j in range(nbk):
                b = nb()
                for k in range(8):
                    mm(PS[b][:, 0:512], a[:, k, j * 128:(j + 1) * 128], win[:, k, 1024:1536], k == 0, k == 7, [win_r, ar], PR[b])
                cpi[0] += 1
                cp("act" if cpi[0] % 2 else "dve", sv[:, j, :], PS[b][:, 0:512], [PR[b]], svr)
            store(Vd[t0:t0 + T, :].rearrange("(j p) n -> p j n", p=128), sv[:, 0:nbk, :], svr)
            if kind == 0:
                rf, rfr = rfs.next()
                rb, rbr = rbs.next()
                for (rt, rr, cc) in ((rf, rfr, 2048), (rb, rbr, 2064)):
                    b = nb()
                    for k in range(8):
                        mm(PS[b][0:16, :T], win[:, k, cc:cc + 16], a[:, k, :T], k == 0, k == 7, [win_r, ar], PR[b])
                    cp("act", rt[0:16, :T], PS[b][0:16, :T], [PR[b]], rr)
                for d, (rt, rr) in enumerate(((rf, rfr), (rb, rbr))):
                    sp_, spr = sps.next()
                    for j in range(nbk):
                        b = nb()
                        mm(PS[b][:, 0:512], rt[0:16, j * 128:(j + 1) * 128], w2[0:16, d, :], True, True, [rr, w2_r], PR[b])
                        xb_, xbr = xbs.next()
                        tt("dve", xb_[:], PS[b][:, 0:512], ppt[:, PP["bgk"] + d * 512:PP["bgk"] + (d + 1) * 512], ALU.add, [PR[b], pp_r], xbr)
                        ee, eer = ees.next()
                        act(ee[:], xb_[:], AF.Exp, [xbr], eer, scale=-1.0)
                        act(sp_[:, j, :], ee[:], AF.Ln, [eer], spr, bias=1.0)
                    store(SPL[d, t0:t0 + T, :].rearrange("(j p) n -> p j n", p=128), sp_[:, 0:nbk, :], spr)
        sc.barrier()

    def inproj_gates(l):
        A.reset()
        stage = rot(2, [128, 2048], F32, "stg")
        win, win_r = A.tile([128, 8, 3072], BF16, "winD")
        load_w(win, win_r, w_in_d[l][:, C_G0:C_G0 + 3072], 8, 3072, stage)
        ats = rot(2, [128, 8, 512], BF16, "a")
        sts = rot(3, [128, 8, 512], BF16, "stg8")
        ATv = AT.rearrange("(c p) t -> p c t", p=128)
        bi = 0
        for ti, (t0, T, isc, ac0) in enumerate(tiles):
            if isc and l == L - 1:
                continue
            a, ar = ats.next()
            load(a[:, :, :T], ATv[:, :, ac0:ac0 + T], ar)
            for n in range(3):
                st, sr = sts.next()
                for c in range(8):
                    b = bi % 4
                    bi += 1
                    for k in range(8):
                        mm(PS[b][:, :T], win[:, k, n * 1024 + c * 128:n * 1024 + (c + 1) * 128], a[:, k, :T], k == 0, k == 7, [win_r, ar], PR[b])
                    act(st[:, c, :T], PS[b][:, :T], AF.Sigmoid, [PR[b], pp_r], sr, bias=ppc("bgate", n * 8 + c))
                store(GT[n].rearrange("(c p) t -> p c t", p=128)[:, :, t0:t0 + T], st[:, :, :T], sr)
        sc.barrier()

    def attention(l):
        A.reset()
        kts = rot(2, [96, NT], BF16, "kt")
        vas = rot(2, [128, NKT, 65], BF16, "va")
        qts = rot(3, [96, 512], BF16, "qt")
        pts = rot(4, [128, 512], BF16, "pt")
        osb = rot(2, [65, 512], F32, "osb")
        rvs = rot(2, [64, 512], F32, "rinv")
        ybs = rot(2, [64, 512], BF16, "yb")
        scale = 96.0 ** -0.5
        qi = 0
        for h in range(8):
            kt, ktr = kts.next()
            va, var_ = vas.next()
            load(kt[:, :], KT[h], ktr)
            VAh = VA[h].rearrange("(k p) e -> p k e", p=128)
            for k0 in range(0, NKT, 8):
                k1 = min(NKT, k0 + 8)
                load(va[:, k0:k1, :], VAh[:, k0:k1, :], var_)
            for ti, (t0, T, isc, ac0) in enumerate(tiles):
                if isc and l == L - 1:
                    continue
                nk = CTX // 128 if isc else NKT
                qt, qtr = qts.next()
                load(qt[:, :T], QT[h][:, t0:t0 + T], qtr)
                ob = 3 + (qi % 2)
                qi += 1

                def qk(i):
                    mm(PS[i % 3][:, :T], kt[:, i * 128:(i + 1) * 128], qt[:, :T], True, True, [ktr, qtr], PR[i % 3])

                qk(0)
                for i in range(nk):
                    if i + 1 < nk:
                        qk(i + 1)
                    pt, ptr = pts.next()
                    act(pt[:, :T], PS[i % 3][:, :T], AF.Exp, [PR[i % 3]], ptr, scale=scale)
                    mm(PS[ob][0:65, :T], va[:, i, :], pt[:, :T], i == 0, i == nk - 1, [var_, ptr], PR[ob])
                o, orr = osb.next()
                cp("dve", o[0:65, :T], PS[ob][0:65, :T], [PR[ob]], orr)
                mm(PS[5][0:64, :T], cst[0:65, CS["sel65"]:CS["sel65"] + 64], o[0:65, :T], True, True, [cst_r, orr], PR[5])
                rv, rvr = rvs.next()
                sc.op("dve", lambda e, rv=rv, T=T: e.reciprocal(out=rv[0:64, :T], in_=PS[5][0:64, :T]), reads=[PR[5]], writes=[rvr])
                yb, ybr = ybs.next()
                tt("dve", yb[0:64, :T], o[0:64, :T], rv[0:64, :T], ALU.mult, [orr, rvr], ybr)
                store(YM[h * 64:(h + 1) * 64, t0:t0 + T], yb[0:64, :T], ybr)
        sc.barrier()

    def sweep(l, kind):
        A.reset()
        Qd, Kd, Gd, Vd, Yd = (GQ, GK, GG, GV, YG) if kind == 0 else (RQ, RK, RG, RV, YR)
        qs = 128.0 ** -0.5 if kind == 0 else 1.0
        ks = 1.0 if kind == 0 else 128.0 ** -0.5
        NH = 4
        St = [A.tile([128, 128], F32, "S") for _ in range(NH)]
        Sb = [A.tile([128, 128], BF16, "Sb") for _ in range(NH)]
        qts = [rot(2, [128, 512], BF16, "q") for _ in range(NH)]
        kts_ = [rot(2, [128, 512], BF16, "k") for _ in range(NH)]
        vts = [rot(2, [128, 4, 128], BF16, "v") for _ in range(NH)]
        qds = [rot(2, [128, 512], BF16, "qd") for _ in range(NH)]
        kis = [rot(2, [128, 512], BF16, "ki") for _ in range(NH)]
        if kind == 0:
            spt = [rot(2, [128, 4, 128], F32, "spt") for _ in range(NH)]
            Es = [rot(2, [128, 512], F32, "E") for _ in range(NH)]
            Eis = [rot(2, [128, 512], F32, "Ei") for _ in range(NH)]
        else:
            Etab = [[A.tile([128, 128], F32, "Et") for _ in range(2)] for _ in range(NH)]
            Eitab = [[A.tile([128, 128], F32, "Eit") for _ in range(2)] for _ in range(NH)]
            for h in range(NH):
                for d in range(2):
                    idx = cst[:, CS["idxF" if d == 0 else "idxB"]:CS["idxF" if d == 0 else "idxB"] + 128]
                    act(Etab[h][d][0][:], idx, AF.Exp, [cst_r, lg_r], Etab[h][d][1], scale=lgt[:, d * 4 + h:d * 4 + h + 1])
                    act(Eitab[h][d][0][:], idx, AF.Exp, [cst_r, nlg_r], Eitab[h][d][1], scale=nlgt[:, d * 4 + h:d * 4 + h + 1])
        ktas = [rot(2, [128, 512], BF16, "kta") for _ in range(NH)]
        stmp = [A.tile([128, 128], F32, "stmp") for _ in range(NH)]
        ams = rot(4, [128, 128], BF16, "am")
        ofs = [rot(2, [128, 512], F32, "of") for _ in range(NH)]
        sgs = [rot(2, [128, 512], BF16, "sg") for _ in range(NH)]
        ots = rot(8, [128, 512], F32, "ot")
        sqs = rot(2, [128, 1, 512], BF16, "sq")
        lns = rot(2, [128, 512], F32, "ln")
        rss = rot(2, [128, 512], F32, "rs")
        y1s = rot(2, [128, 512], F32, "y1")
        ybs = rot(2, [128, 512], BF16, "yb")
        cnt = [0]
        for d in range(2):
            for h in range(NH):
                memset("dve", St[h][0][:], 0.0, St[h][1])
                memset("pool", Sb[h][0][:], 0.0, Sb[h][1])
            order = [0] + (list(range(1, len(tiles))) if d == 0 else list(range(len(tiles) - 1, 0, -1)))
            mask = cst[:, CS["maskF" if d == 0 else "maskB"]:CS["maskF" if d == 0 else "maskB"] + 128]
            um = cst[:, CS["uF" if d == 0 else "uB"]:CS["uF" if d == 0 else "uB"] + 128]
            for ti in order:
                t0, T, isc, ac0 = tiles[ti]
                skip_out = isc and l == L - 1
                nbk = T // 128
                cur = []
                for h in range(NH):
                    q, qr = qts[h].next(); k, kr = kts_[h].next(); v, vr = vts[h].next()
                    load(q[:, :T], Qd[h * 128:(h + 1) * 128, t0:t0 + T], qr)
                    load(k[:, :T], Kd[h * 128:(h + 1) * 128, t0:t0 + T], kr)
                    load(v[:, 0:nbk, :], Vd[t0:t0 + T, h * 128:(h + 1) * 128].rearrange("(j p) n -> p j n", p=128), vr)
                    qd, qdr = qds[h].next(); ki, kir = kis[h].next()
                    if kind == 0:
                        sp_, spr = spt[h].next()
                        load(sp_[:, 0:nbk, :], SPL[d, t0:t0 + T, h * 128:(h + 1) * 128].rearrange("(j p) n -> p j n", p=128), spr)
                        cb = 0
                        for j in range(nbk):
                            mm(PS[cb][:, j * 128:(j + 1) * 128], sp_[:, j, :], um, True, True, [spr, cst_r], PR[cb])
                        E, Er = Es[h].next(); Ei, Eir = Eis[h].next()
                        act(E[:, :T], PS[cb][:, :T], AF.Exp, [PR[cb]], Er)
                        act(Ei[:, :T], PS[cb][:, :T], AF.Exp, [PR[cb]], Eir, scale=-1.0)
                        stt("dve", qd[:, :T], q[:, :T], qs, E[:, :T], ALU.mult, ALU.mult, [qr, Er], qdr)
                        stt("dve", ki[:, :T], k[:, :T], ks, Ei[:, :T], ALU.mult, ALU.mult, [kr, Eir], kir)
                        gsrc = (E, Er)
                    else:
                        E, Er = Etab[h][d]; Ei, Eir = Eitab[h][d]
                        stt("dve", qd[:, :T].rearrange("p (j n) -> p j n", n=128), q[:, :T].rearrange("p (j n) -> p j n", n=128), qs,
                            E[:].unsqueeze(1).to_broadcast([128, nbk, 128]), ALU.mult, ALU.mult, [qr, Er], qdr)
                        stt("dve", ki[:, :T].rearrange("p (j n) -> p j n", n=128), k[:, :T].rearrange("p (j n) -> p j n", n=128), ks,
                            Ei[:].unsqueeze(1).to_broadcast([128, nbk, 128]), ALU.mult, ALU.mult, [kr, Eir], kir)
                        gsrc = (E, Er)
                    for j in range(nbk):
                        sc.op("pe", lambda e, j=j, ki=ki: e.transpose(PSB[:, j * 128:(j + 1) * 128], ki[:, j * 128:(j + 1) * 128], ident_bf),
                              reads=[kir, cbf_r], writes=[PBR[0]])
                    kta, ktar = ktas[h].next()
                    cp("act", kta[:, :T], PSB[:, 0:T], [PBR[0]], ktar)
                    cur.append((q, qr, k, kr, v, vr, qd, qdr, ki, kir, gsrc, kta, ktar))
                ob = [None] * NH
                blocks = list(range(nbk)) if d == 0 else list(range(nbk - 1, -1, -1))
                for j in blocks:
                    c0 = j * 128
                    for h in range(NH):
                        q, qr, k, kr, v, vr, qd, qdr, ki, kir, (E, Er), kta, ktar = cur[h]
                        if kind == 0:
                            gcol = E[:, c0 + 127:c0 + 128] if d == 0 else E[:, c0:c0 + 1]
                        else:
                            gcol = E[:, 127:128] if d == 0 else E[:, 0:1]
                        pb = cnt[0] % 2
                        cnt[0] += 1
                        ab = 4 + pb
                        mm(PS[ab][:, 0:128], ki[:, c0:c0 + 128], qd[:, c0:c0 + 128], True, True, [kir, qdr], PR[ab])
                        am, amr = ams.next()
                        tt("dve", am[:], PS[ab][:, 0:128], mask, ALU.mult, [PR[ab], cst_r], amr)
                        obk = 2 + (h % 2)
                        mm(PS[obk][:, 0:128], v[:, j, :], am[:], True, False, [vr, amr], PR[obk])
                        mm(PS[obk][:, 0:128], Sb[h][0][:], qd[:, c0:c0 + 128], False, True, [Sb[h][1], qdr], PR[obk])
                        if not skip_out:
                            if ob[h] is None:
                                ob[h] = ots.next()
                            ot, otr = ob[h]
                            cp("act", ot[:, c0:c0 + 128], PS[obk][:, 0:128], [PR[obk]], otr)
                        ib = 1 if h % 2 == 0 else 6
                        mm(PS[ib][:, 0:128], kta[:, c0:c0 + 128], v[:, j, :], True, True, [ktar, vr], PR[ib])
                        tmp, tmpr = stmp[h]
                        tt("dve", tmp[:], St[h][0][:], PS[ib][:, 0:128], ALU.add, [St[h][1], PR[ib]], tmpr)
                        act(Sb[h][0][:], tmp[:], AF.Copy, [tmpr, Er], Sb[h][1], scale=gcol)
                        ts1("dve", St[h][0][:], tmp[:], gcol, ALU.mult, [tmpr, Er], St[h][1])
                if skip_out:
                    continue
                for h in range(NH):
                    ot, otr = ob[h]
                    key = (kind, h, ti)
                    if d == 0:
                        r_ = of_res.setdefault(key, Res(f"of{key}"))
                        store(OF[h * 128:(h + 1) * 128, t0:t0 + T], ot[:, :T], otr, writes=[r_])
                    else:
                        r_ = of_res[key]
                        of_, ofr = ofs[h].next()
                        load(of_[:, :T], OF[h * 128:(h + 1) * 128, t0:t0 + T], ofr, reads=[r_])
                        sg, sgr = sgs[h].next()
                        load(sg[:, :T], Gd[h * 128:(h + 1) * 128, t0:t0 + T], sgr)
                        tt("dve", ot[:, :T], ot[:, :T], of_[:, :T], ALU.add, [otr, ofr], otr)
                        sq, sqr = sqs.next(); lnv, lnr = lns.next(); rs, rsr = rss.next()
                        partnorm([ot[:, :T]], [otr], 128, 128, T, sq, sqr, 0, lnv, lnr, rs, rsr)
                        y1, y1r = y1s.next()
                        ncol = ppc("gon", 0) if kind == 0 else onec[:, 0:1]
                        stt("dve", y1[:, :T], ot[:, :T], ncol, rs[:, :T], ALU.mult, ALU.mult, [otr, rsr, pp_r, onec_r], y1r)
                        yb, ybr = ybs.next()
                        tt("pool", yb[:, :T], y1[:, :T], sg[:, :T], ALU.mult, [y1r, sgr], ybr)
                        store(Yd[h * 128:(h + 1) * 128, t0:t0 + T], yb[:, :T], ybr)
            sc.barrier()

    def merge_phase(l):
        A.reset()
        stage = rot(2, [128, 2048], F32, "stg")
        wbr, wbr_r = A.tile([128, 12, 1024], BF16, "wbr")
        load_w(wbr, wbr_r, w_br_d[l].rearrange("n k m -> (n k) m"), 12, 1024, stage)
        wo, wo_r = A.tile([128, 8, 1024], BF16, "wo")
        load_w(wo, wo_r, w_out_d[l], 8, 1024, stage)
        ys = [rot(2, [128, 4, 512], BF16, f"y{n}") for n in range(3)]
        gs = [rot(2, [128, 8, 512], BF16, f"g{n}") for n in range(3)]
        hts = rot(2, [128, 8, 512], F32, "h")
        m32 = rot(2, [128, 512], F32, "m32")
        tms = rot(3, [128, 512], F32, "tm")
        mbs = rot(2, [128, 8, 512], BF16, "mb")
        HTv = HT.rearrange("(c p) t -> p c t", p=128)
        bi = 0
        for ti, (t0, T, isc, ac0) in enumerate(tiles):
            if isc and l == L - 1:
                continue
            col = 1 if isc else 0
            yy = []
            gg = []
            for n, Yd in enumerate((YM, YG, YR)):
                y, yr = ys[n].next()
                load(y[:, :, :T], Yd.rearrange("(c p) t -> p c t", p=128)[:, :, t0:t0 + T], yr)
                g, gr = gs[n].next()
                load(g[:, :, :T], GT[n].rearrange("(c p) t -> p c t", p=128)[:, :, t0:t0 + T], gr)
                yy.append((y, yr))
                gg.append((g, gr))
            h, hr = hts.next()
            load(h[:, :, :T], HTv[:, :, t0:t0 + T], hr)
            mb, mbr = mbs.next()
            for c in range(8):
                m, mr = m32.next()
                for n in range(3):
                    b = bi % 3
                    bi += 1
                    for k in range(4):
                        mm(PS[b][:, :T], wbr[:, n * 4 + k, c * 128:(c + 1) * 128], yy[n][0][:, k, :T], k == 0, k == 3, [wbr_r, yy[n][1]], PR[b])
                    if n == 0:
                        tt("dve", m[:, :T], PS[b][:, :T], gg[0][0][:, c, :T], ALU.mult, [PR[b], gg[0][1]], mr)
                    else:
                        tm, tmr = tms.next()
                        tt("dve", tm[:, :T], PS[b][:, :T], gg[n][0][:, c, :T], ALU.mult, [PR[b], gg[n][1]], tmr)
                        if n == 1:
                            tt("pool", m[:, :T], m[:, :T], tm[:, :T], ALU.add, [mr, tmr], mr)
                        else:
                            tt("pool", mb[:, c, :T], m[:, :T], tm[:, :T], ALU.add, [mr, tmr], mbr)
            for c in range(8):
                b = 3 + (c % 2)
                for k in range(8):
                    mm(PS[b][:, :T], wo[:, k, c * 128:(c + 1) * 128], mb[:, k, :T], k == 0, k == 7, [wo_r, mbr], PR[b])
                stt("dve", h[:, c, :T], PS[b][:, :T], modt[:, 16 + c, col:col + 1], h[:, c, :T], ALU.mult, ALU.add, [PR[b], mod_r, hr], hr)
            store(HTv[:, :, t0:t0 + T], h[:, :, :T], hr)
        sc.barrier()

    def ffn_phase(l):
        A.reset()
        stage = rot(2, [128, 2048], F32, "stg")
        wfi, wfi_r = A.tile([128, 8, 2 * DFF], BF16, "wfi")
        wfo, wfo_r = A.tile([128, NJ, 1024], BF16, "wfo")
        load_w(wfi, wfi_r, w_fi_d[l], 8, 2 * DFF, stage)
        load_w(wfo, wfo_r, w_fo_d[l], NJ, 1024, stage)
        FT = 256
        ats = rot(2, [128, 8, FT + 2], BF16, "a2")
        hts = rot(2, [128, 8, FT], F32, "h")
        hid = rot(1, [128, NJ, FT], BF16, "hid")
        gsb = rot(2, [128, FT + 2], F32, "gsb")
        acc = rot(2, [128, FT], F32, "acc")
        gel = rot(2, [128, FT], F32, "gel")
        ATv = AT.rearrange("(c p) t -> p c t", p=128)
        HTv = HT.rearrange("(c p) t -> p c t", p=128)
        OTv = outT.rearrange("(c p) t -> p c t", p=128)
        last = (l == L - 1)
        nlat = S_ // FT
        for fi, (t0, T, isc, ac0) in enumerate(ftiles):
            if isc and last:
                continue
            col = 1 if isc else 0
            a, ar = ats.next()
            load(a[:, :, :], ATv[:, :, ac0 - 1:ac0 + T + 1], ar)
            first = (t0 == 0) or (t0 == CTX)
            lastt = (t0 + T == CTX) or (t0 + T == NT)
            if first:
                memset("pool", a[:, :, 0:1], 0.0, ar)
            if lastt:
                memset("pool", a[:, :, T + 1:T + 2], 0.0, ar)
            h, hr = hts.next()
            load(h[:, :, :], HTv[:, :, t0:t0 + T], hr)
            hd, hdr = hid.next()
            for j in range(NJ):
                gb = j % 2
                ub = 2 + (j % 2)
                for k in range(8):
                    mm(PS[gb][:, 0:T], wfi[:, k, j * 128:(j + 1) * 128], a[:, k, 1:T + 1], k == 0, k == 7, [wfi_r, ar], PR[gb])
                for k in range(8):
                    mm(PS[4][:, 0:2], wfi[:, k, j * 128:(j + 1) * 128], a[:, k, 0:T + 2:T + 1], k == 0, k == 7, [wfi_r, ar], PR[4])
                for k in range(8):
                    mm(PS[ub][:, 0:T], wfi[:, k, DFF + j * 128:DFF + (j + 1) * 128], a[:, k, 1:T + 1], k == 0, k == 7, [wfi_r, ar], PR[ub])
                g, gr = gsb.next()
                cp("act", g[:, 1:T + 1], PS[gb][:, 0:T], [PR[gb]], gr)
                cp("act", g[:, 0:T + 2:T + 1], PS[4][:, 0:2], [PR[4]], gr)
                ac, acr = acc.next()
                ts("dve", ac[:], g[:, 0:T], ppc("wdw", j), ppc("bdw", j), ALU.mult, ALU.add, [gr, pp_r], acr)
                stt("dve", ac[:], g[:, 1:T + 1], ppc("wdw", NJ + j), ac[:], ALU.mult, ALU.add, [gr, pp_r, acr], acr)
                stt("dve", ac[:], g[:, 2:T + 2], ppc("wdw", 2 * NJ + j), ac[:], ALU.mult, ALU.add, [gr, pp_r, acr], acr)
                ge, ger = gel.next()
                act(ge[:], ac[:], AF.Gelu_apprx_tanh, [acr], ger)
                tt("dve", hd[:, j, :], PS[ub][:, 0:T], ge[:], ALU.mult, [PR[ub], ger], hdr)
            for c in range(8):
                b = 5 + (c % 2)
                for j in range(NJ):
                    mm(PS[b][:, 0:T], wfo[:, j, c * 128:(c + 1) * 128], hd[:, j, :], j == 0, j == NJ - 1, [wfo_r, hdr], PR[b])
                stt("dve", h[:, c, :], PS[b][:, 0:T], modt[:, 40 + c, col:col + 1], h[:, c, :], ALU.mult, ALU.add, [PR[b], mod_r, hr], hr)
            if last:
                store(OTv[:, :, t0 - CTX:t0 - CTX + T], h[:, :, :], hr)
            else:
                store(HTv[:, :, t0:t0 + T], h[:, :, :], hr)
        sc.barrier()

    plist = []
    for l in range(L):
        plist += [(mod_phase, (l,)), (norm_phase, (l, 1)), (inproj_mla, (l,)), (inproj_lin, (l, 0)), (inproj_lin, (l, 1)),
                  (inproj_gates, (l,)), (attention, (l,)), (sweep, (l, 0)), (sweep, (l, 1)), (merge_phase, (l,)),
                  (norm_phase, (l, 2)), (ffn_phase, (l,))]
    for f, a in plist[:nphase]:
        f(*a)
    print(f"[build] sbuf peak {A.peak}", flush=True)
    sc.emit()
    return nc


def _pc(v, n):
    return np.ascontiguousarray(np.asarray(v, np.float32).reshape(n, 128).T)


def _consts():
    c = np.zeros((128, NCST), np.float32)
    c[:, CS["ones"]:CS["ones"] + 128] = 1.0
    c[:, CS["ident"]:CS["ident"] + 128] = np.eye(128, dtype=np.float32)
    p = np.zeros((128, 128), np.float32)
    for i in range(64):
        p[64 + i, i] = -1.0
        p[i, 64 + i] = 1.0
    c[:, CS["p128"]:CS["p128"] + 128] = p
    j = np.arange(128)[:, None]
    i = np.arange(128)[None, :]
    c[:, CS["maskF"]:CS["maskF"] + 128] = (j <= i)
    c[:, CS["maskB"]:CS["maskB"] + 128] = (j > i)
    c[:, CS["uF"]:CS["uF"] + 128] = (j <= i) * (-1.0 / 16.0)
    c[:, CS["uB"]:CS["uB"] + 128] = (j >= i) * (-1.0 / 16.0)
    c[:, CS["idxF"]:CS["idxF"] + 128] = (i + 1.0) * np.ones((128, 1))
    c[:, CS["idxB"]:CS["idxB"] + 128] = (128.0 - i) * np.ones((128, 1))
    p96 = np.zeros((128, 96), np.float32)
    for base in (64, 80):
        for t in range(8):
            p96[base + 8 + t, base + t] = -1.0
            p96[base + t, base + 8 + t] = 1.0
    c[:, CS["p96"]:CS["p96"] + 96] = p96
    es = np.zeros((128, 96), np.float32)
    for t in range(32):
        es[t, 64 + t] = 1.0
    c[:, CS["esel"]:CS["esel"] + 96] = es
    s65 = np.zeros((128, 64), np.float32)
    s65[64, :] = 1.0
    c[:, CS["sel65"]:CS["sel65"] + 64] = s65
    return c


def _tables(S_, CTX):
    NT = CTX + S_
    f32 = np.float32
    t = np.arange(S_)
    row = (t // 64).astype(f32)
    colp = (t % 64).astype(f32)
    inv = (f32(10000.0) ** (-(np.arange(8, dtype=f32) * f32(2.0) / f32(16)))).astype(f32)
    mc = np.ones((96, NT), f32)
    ms = np.zeros((96, NT), f32)
    ar = (row[:, None] * inv[None, :]).astype(f32)
    ac = (colp[:, None] * inv[None, :]).astype(f32)
    for i in range(8):
        for base, ang in ((64, ar), (80, ac)):
            mc[base + i, CTX:] = np.cos(ang[:, i]); mc[base + 8 + i, CTX:] = np.cos(ang[:, i])
            ms[base + i, CTX:] = np.sin(ang[:, i]); ms[base + 8 + i, CTX:] = np.sin(ang[:, i])
    pos = np.arange(NT).astype(f32)
    rinv = (f32(1.0) / (f32(10000.0) ** np.linspace(0.0, 1.0, 64, dtype=f32))).astype(f32)
    ang = (pos[:, None] * rinv[None, :]).astype(f32)
    rc = np.concatenate([np.cos(ang), np.cos(ang)], 1).T.astype(f32)
    rs_ = np.concatenate([np.sin(ang), np.sin(ang)], 1).T.astype(f32)
    return np.ascontiguousarray(mc), np.ascontiguousarray(ms), np.ascontiguousarray(rc), np.ascontiguousarray(rs_)


def _pack_pp(inp, L):
    pp = np.zeros((L, 128, NPP), np.float32)
    for l in range(L):
        def put(name, arr):
            arr = np.asarray(arr, np.float32)
            pp[l, :arr.shape[0], PP[name]:PP[name] + arr.shape[1]] = arr
        put("bada", _pc(inp["b_ada"][l], 48))
        put("n1w", _pc(inp["norm1_w"][l], 8))
        put("n2w", _pc(inp["norm2_w"][l], 8))
        put("bgate", _pc(np.asarray(inp["b_gate"][l]).reshape(-1), 24))
        put("qna", _pc(inp["mla_q_norm_a"][l], 2))
        put("kvna", _pc(inp["mla_kv_norm_a"][l], 1))
        put("qn", np.asarray(inp["mla_q_norm"][l]).reshape(96, 1))
        put("kn", np.asarray(inp["mla_k_norm"][l]).reshape(96, 1))
        put("gon", _pc(inp["gla_o_norm"][l], 1))
        put("wdw", _pc(np.asarray(inp["w_dw"][l]).reshape(-1), 66))
        put("bdw", _pc(inp["b_dw"][l], 22))
        put("retdec", np.broadcast_to(np.asarray(inp["ret_decay"][l]).reshape(1, 8), (128, 8)))
        put("bgk", np.broadcast_to(np.asarray(inp["gla_b_gk"][l]).reshape(1, 1024), (128, 1024)))
    return pp


_NC_CACHE = {}


def _run(inp, S_, CTX, L, ncores, debug=False, nphase=999):
    key = (S_, CTX, L, debug, nphase)
    if key not in _NC_CACHE:
        _NC_CACHE[key] = build(S_, CTX, L, debug, nphase)
    nc = _NC_CACHE[key]
    B = np.asarray(inp["x"]).shape[0]
    mc, ms, rc, rs_ = _tables(S_, CTX)
    shared = {
        "pp": _pack_pp(inp, L), "cst": _consts(), "mlaC": mc, "mlaS": ms, "retC": rc, "retS": rs_,
    }
    for k in ("w_ada", "w_in", "mla_w_qb", "mla_w_kvb", "gla_w_gk2", "w_branch", "w_out", "w_ffn_in", "w_ffn_out"):
        shared[k] = np.ascontiguousarray(np.asarray(inp[k], np.float32))
    cc = _pc(inp["c_ctx"], 8)
    in_maps = []
    for core in range(ncores):
        b = core % B
        m = dict(shared)
        m["xT"] = np.ascontiguousarray(np.asarray(inp["x"][b], np.float32).T)
        m["ctxT"] = np.ascontiguousarray(np.asarray(inp["ctx"][b], np.float32).T)
        m["cvec"] = np.ascontiguousarray(np.concatenate([_pc(inp["c"][b], 8), cc], 1))
        in_maps.append(m)
    res = run_bass_kernel_spmd(nc, in_maps, core_ids=list(range(ncores)))
    return res


def kernel(**inputs):
    x = np.asarray(inputs["x"])
    B, S_, _ = x.shape
    CTX = np.asarray(inputs["ctx"]).shape[1]
    L = np.asarray(inputs["w_ada"]).shape[0]
    res = _run(inputs, S_, CTX, L, 8)
    out = np.empty((B, S_, D), np.float32)
    for b in range(B):
        out[b] = res.results[b]["outT"].T
    return out
```

```python
import concourse.bass as bass
import concourse.mybir as mybir

SEM_LIMIT = 1000000000
DMA_K = 12


class Res:
    __slots__ = ("name", "lw", "rd_c", "rd_d", "excl")

    def __init__(self, name, excl=False):
        self.name = name
        self.excl = excl
        self.lw = None
        self.rd_c = {}
        self.rd_d = []


class Op:
    __slots__ = ("eng", "fn", "reads", "writes", "dma", "acc", "deps", "sig", "ev", "waits", "barrier")

    def __init__(self, eng, fn, reads, writes, dma, acc):
        self.eng = eng
        self.fn = fn
        self.reads = reads
        self.writes = writes
        self.dma = dma
        self.acc = acc
        self.deps = set()
        self.sig = False
        self.ev = None
        self.waits = []
        self.barrier = False


class Sched:
    ENGS = ("pe", "act", "dve", "pool", "sp")

    def __init__(self, nc):
        self.nc = nc
        self.ops = []

    def op(self, eng, fn, reads=(), writes=(), acc=False):
        self.ops.append(Op(eng, fn, list(reads), list(writes), False, acc))

    def dma(self, q, out, in_, reads=(), writes=()):
        self.ops.append(Op(q, lambda e, o=out, i=in_: e.dma_start(out=o, in_=i), list(reads), list(writes), True, False))

    def custom_dma(self, q, fn, reads=(), writes=()):
        self.ops.append(Op(q, fn, list(reads), list(writes), True, False))

    def barrier(self):
        o = Op(None, None, [], [], False, False)
        o.barrier = True
        self.ops.append(o)

    def _analyse(self):
        ops = self.ops
        last_c = {}
        last_d = {e: [] for e in self.ENGS}
        pend = {e: set() for e in self.ENGS}
        for i, op in enumerate(ops):
            if op.barrier:
                deps = set(last_c.values())
                for q in self.ENGS:
                    deps.update(last_d[q][-DMA_K:])
                for e in self.ENGS:
                    pend[e] |= deps
                continue
            d = op.deps
            if pend[op.eng]:
                d |= pend[op.eng]
                pend[op.eng] = set()
            for r in op.reads:
                if r.lw is not None:
                    d.add(r.lw)
                if r.excl:
                    for e2, j in r.rd_c.items():
                        if e2 != op.eng:
                            d.add(j)
            for w in op.writes:
                if w.lw is not None:
                    lwop = ops[w.lw]
                    if not (op.acc and op.eng == "pe" and lwop.eng == "pe" and not lwop.dma):
                        d.add(w.lw)
                d.update(w.rd_c.values())
                d.update(w.rd_d)
            d.discard(i)
            for r in op.reads:
                if op.dma:
                    r.rd_d.append(i)
                else:
                    r.rd_c[op.eng] = i
            for w in op.writes:
                w.lw = i
                w.rd_c = {}
                w.rd_d = []
            if op.dma:
                last_d[op.eng].append(i)
            else:
                last_c[op.eng] = i
            for j in d:
                ops[j].sig = True
        self.final_deps = set(last_c.values())
        for q in self.ENGS:
            self.final_deps.update(last_d[q][-DMA_K:])
        for j in self.final_deps:
            ops[j].sig = True

    def _assign(self):
        nc = self.nc
        ops = self.ops
        csem = {}
        ccnt = {}
        dsem = {e: [None] * DMA_K for e in self.ENGS}
        dcnt = {e: [0] * DMA_K for e in self.ENGS}
        dprev = {e: [None] * DMA_K for e in self.ENGS}
        dn = {e: 0 for e in self.ENGS}
        waited = {e: {} for e in self.ENGS}
        self.nsem = 0

        def newsem(tag):
            self.nsem += 1
            return nc.alloc_semaphore(name=f"s_{tag}_{self.nsem}")

        def add_wait(op, ev):
            sem, val = ev
            w = waited[op.eng]
            k = id(sem)
            if w.get(k, 0) >= val:
                return
            w[k] = val
            op.waits.append((sem, val))

        for i, op in enumerate(ops):
            if op.barrier:
                continue
            e = op.eng
            for j in sorted(op.deps):
                add_wait(op, ops[j].ev)
            if op.dma:
                k = dn[e] % DMA_K
                dn[e] += 1
                if dprev[e][k] is not None:
                    add_wait(op, dprev[e][k])
                if dsem[e][k] is None or dcnt[e][k] + 16 > SEM_LIMIT:
                    dsem[e][k] = newsem("d" + e)
                    dcnt[e][k] = 0
                dcnt[e][k] += 16
                op.ev = (dsem[e][k], dcnt[e][k])
                dprev[e][k] = op.ev
                op.sig = True
            elif op.sig:
                if e not in csem or ccnt[e] + 1 > SEM_LIMIT:
                    csem[e] = newsem("c" + e)
                    ccnt[e] = 0
                ccnt[e] += 1
                op.ev = (csem[e], ccnt[e])
        print("[sched] ccnt", ccnt, "dcnt max", {e: max(v) for e, v in dcnt.items()}, flush=True)
        self.final_waits = []
        fw = {}
        for j in sorted(self.final_deps):
            sem, val = ops[j].ev
            k = id(sem)
            if fw.get(k, (None, 0))[1] < val:
                fw[k] = (sem, val)
        self.final_waits = list(fw.values())

    def emit(self):
        self._analyse()
        self._assign()
        nc = self.nc
        ops = self.ops

        def run(e):
            def body(eng):
                for op in ops:
                    if op.barrier or op.eng != e:
                        continue
                    for sem, val in op.waits:
                        eng.wait_ge(sem, val)
                    ins = op.fn(eng)
                    if op.sig:
                        ins.then_inc(op.ev[0], 16 if op.dma else 1)
                if e == "sp":
                    for sem, val in self.final_waits:
                        eng.wait_ge(sem, val)
            return body

        with nc.Block() as block:
            block.tensor(run("pe"))
            block.scalar(run("act"))
            block.vector(run("dve"))
            block.gpsimd(run("pool"))
            block.sync(run("sp"))
        n = sum(1 for o in ops if not o.barrier)
        nw = sum(len(o.waits) for o in ops if not o.barrier)
        print(f"[sched] ops={n} waits={nw} sems={self.nsem}", flush=True)


class SbufAlloc:
    def __init__(self, nc, base=16640, limit=229376):
        self.nc = nc
        self.base = base
        self.off = base
        self.limit = limit
        self.n = 0
        self.peak = 0

    def reset(self, to=None):
        self.off = self.base if to is None else to

    def mark(self):
        return self.off

    def tile(self, shape, dtype, name="t"):
        esz = {mybir.dt.float32: 4, mybir.dt.bfloat16: 2}[dtype]
        nb = esz
        for s in shape[1:]:
            nb *= s
        nb = (nb + 63) // 64 * 64
        assert self.off + nb <= self.limit, f"SBUF overflow {name}: {self.off}+{nb}>{self.limit}"
        self.n += 1
        h = self.nc.alloc_sbuf_tensor_at(f"{name}_{self.n}", list(shape), dtype, offset=self.off)
        self.off += nb
        self.peak = max(self.peak, self.off)
        r = Res(f"{name}_{self.n}")
        return h, r


import os
import numpy as np
DBG = os.environ.get('KDBG', '')
from concourse.bass_utils import run_bass_kernel_spmd

F32 = mybir.dt.float32
BF16 = mybir.dt.bfloat16
ALU = mybir.AluOpType
AF = mybir.ActivationFunctionType

D = 1024
KC = 8
NIN = 7616
DFF = 2816
NJ = 22
EPS = 1e-6
C_MQ, C_MKV, C_MKR, C_GQ, C_GK, C_GV, C_GG, C_RF, C_RB, C_RQ, C_RK, C_RV, C_RG, C_G0 = (
    0, 256, 384, 416, 928, 1440, 1952, 2464, 2480, 2496, 3008, 3520, 4032, 4544)

PP = {}
_o = 0
for _n, _w in (("bada", 48), ("n1w", 8), ("n2w", 8), ("bgate", 24), ("qna", 2), ("kvna", 1), ("qn", 1), ("kn", 1),
               ("gon", 1), ("wdw", 66), ("bdw", 22), ("retdec", 8), ("bgk", 1024)):
    PP[_n] = _o
    _o += _w
NPP = _o
CS = {}
_o = 0
for _n, _w in (("ones", 128), ("ident", 128), ("p128", 128), ("maskF", 128), ("maskB", 128), ("uF", 128), ("uB", 128),
               ("idxF", 128), ("idxB", 128), ("p96", 96), ("esel", 96), ("sel65", 64)):
    CS[_n] = _o
    _o += _w
NCST = _o


def build(S_, CTX, L, debug=False, nphase=999):
    NT = CTX + S_
    NTP = NT + 4
    nlt = S_ // 512
    tiles = [(0, CTX, True, 1)] + [(CTX + 512 * i, 512, False, CTX + 3 + 512 * i) for i in range(nlt)]
    ftiles = [(0, 256, True, 1)] if CTX == 256 else [(i * 256, 256, True, 1 + i * 256) for i in range(CTX // 256)]
    ftiles = ftiles + [(CTX + 256 * i, 256, False, CTX + 3 + 256 * i) for i in range(S_ // 256)]
    NKT = NT // 128

    nc = bass.Bass("TRN2", target_bir_lowering=False)
    sc = Sched(nc)
    A = SbufAlloc(nc)

    def din(name, shape, dt=F32):
        return nc.dram_tensor(name, list(shape), dt, kind="ExternalInput").ap()

    skind = "ExternalOutput" if debug else "Internal"

    def dscr(name, shape, dt):
        return nc.dram_tensor(name, list(shape), dt, kind=skind).ap()

    xT = din("xT", [D, S_])
    ctxT = din("ctxT", [D, CTX])
    cvec_d = din("cvec", [128, 16])
    pp_d = din("pp", [L, 128, NPP])
    cst_d = din("cst", [128, NCST])
    mlaC_d = din("mlaC", [96, NT])
    mlaS_d = din("mlaS", [96, NT])
    retC_d = din("retC", [128, NT])
    retS_d = din("retS", [128, NT])
    w_ada_d = din("w_ada", [L, D, 6 * D])
    w_in_d = din("w_in", [L, D, NIN])
    w_qb_d = din("mla_w_qb", [L, 256, 768])
    w_kvb_d = din("mla_w_kvb", [L, 128, 1024])
    w_gk2_d = din("gla_w_gk2", [L, 2, 16, 512])
    w_br_d = din("w_branch", [L, 3, 512, D])
    w_out_d = din("w_out", [L, D, D])
    w_fi_d = din("w_ffn_in", [L, D, 2 * DFF])
    w_fo_d = din("w_ffn_out", [L, DFF, D])
    outT = nc.dram_tensor("outT", [D, S_], F32, kind="ExternalOutput").ap()

    HT = dscr("HT", [D, NT], F32)
    AT = dscr("AT", [D, NTP], BF16)
    QT = dscr("QT", [8, 96, NT], BF16)
    KT = dscr("KT", [8, 96, NT], BF16)
    VA = dscr("VA", [8, NT, 65], BF16)
    GQ = dscr("GQ", [512, NT], BF16)
    GK = dscr("GK", [512, NT], BF16)
    GG = dscr("GG", [512, NT], BF16)
    RQ = dscr("RQ", [512, NT], BF16)
    RK = dscr("RK", [512, NT], BF16)
    RG = dscr("RG", [512, NT], BF16)
    GV = dscr("GV", [NT, 512], BF16)
    RV = dscr("RV", [NT, 512], BF16)
    SPL = dscr("SPL", [2, NT, 512], F32)
    GT = dscr("GT", [3, D, NT], BF16)
    YM = dscr("YM", [512, NT], BF16)
    YG = dscr("YG", [512, NT], BF16)
    YR = dscr("YR", [512, NT], BF16)
    OF = dscr("OF", [512, NT], F32)
    of_res = {}

    PSALL = nc.alloc_psum_tensor("psall", [128, 7 * 512], F32)
    PS = [PSALL[:, i * 512:(i + 1) * 512] for i in range(7)]
    PR = [Res(f"ps{i}", True) for i in range(7)]
    PSB = nc.alloc_psum_tensor("psb", [128, 1024], BF16)
    _pb = Res("psb", True)
    PBR = [_pb, _pb]

    def mm(out, lhsT, rhs, start, stop, reads, wres):
        sc.op("pe", lambda e: e.matmul(out, lhsT=lhsT, rhs=rhs, start=start, stop=stop), reads=reads, writes=[wres], acc=True)

    def act(out, in_, func, reads, wres, bias=None, scale=None, eng="act"):
        kw = {}
        if bias is not None:
            kw["bias"] = bias
        if scale is not None:
            kw["scale"] = scale
        sc.op("act", lambda e: e.activation(out=out, in_=in_, func=func, **kw), reads=reads, writes=[wres])

    def cp(eng, out, in_, reads, wres):
        if eng == "act":
            sc.op("act", lambda e: e.copy(out=out, in_=in_), reads=reads, writes=[wres])
        else:
            sc.op(eng, lambda e: e.tensor_copy(out=out, in_=in_), reads=reads, writes=[wres])

    def tt(eng, out, in0, in1, op, reads, wres):
        sc.op(eng, lambda e: e.tensor_tensor(out=out, in0=in0, in1=in1, op=op), reads=reads, writes=[wres])

    def stt(eng, out, in0, scalar, in1, op0, op1, reads, wres):
        sc.op(eng, lambda e: e.scalar_tensor_tensor(out=out, in0=in0, scalar=scalar, in1=in1, op0=op0, op1=op1),
              reads=reads, writes=[wres])

    def ts(eng, out, in0, s1, s2, op0, op1, reads, wres):
        sc.op(eng, lambda e: e.tensor_scalar(out=out, in0=in0, scalar1=s1, scalar2=s2, op0=op0, op1=op1),
              reads=reads, writes=[wres])

    def ts1(eng, out, in0, s1, op0, reads, wres):
        sc.op(eng, lambda e: e.tensor_single_scalar(out=out, in_=in0, scalar=s1, op=op0), reads=reads, writes=[wres])

    def memset(eng, ap, val, wres):
        sc.op(eng, lambda e: e.memset(ap, val), writes=[wres])

    def load(out, in_, wres, reads=()):
        sc.dma("sp", out, in_, reads=reads, writes=[wres])

    def store(out, in_, rres, writes=()):
        sc.dma("pool", out, in_, reads=[rres], writes=writes)

    class Rot:
        def __init__(self, items):
            self.items = items
            self.i = 0

        def next(self):
            it = self.items[self.i % len(self.items)]
            self.i += 1
            return it

    def rot(n, shape, dt, name):
        return Rot([A.tile(shape, dt, name) for _ in range(n)])

    cast_i = [0]

    def load_w(dst, dst_res, src, kc, n, stage):
        rows = src.shape[0] // kc
        for k in range(kc):
            for n0 in range(0, n, 2048):
                w = min(2048, n - n0)
                st, sr = stage.next()
                load(st[:rows, :w], src[k * rows:(k + 1) * rows, n0:n0 + w], sr)
                eng = ("dve", "pool", "act")[cast_i[0] % 3]
                cast_i[0] += 1
                cp(eng, dst[:rows, k, n0:n0 + w], st[:rows, :w], [sr], dst_res)

    cst, cst_r = A.tile([128, NCST], F32, "cst")
    load(cst[:], cst_d, cst_r)
    cbf, cbf_r = A.tile([128, NCST], BF16, "cbf")
    cp("dve", cbf[:], cst[:], [cst_r], cbf_r)
    epst, eps_r = A.tile([128, 1], F32, "eps")
    memset("dve", epst[:], EPS, eps_r)
    onec, onec_r = A.tile([128, 1], F32, "onec")
    memset("dve", onec[:], 1.0, onec_r)
    ppt, pp_r = A.tile([128, NPP], F32, "pp")
    cvt, cv_r = A.tile([128, 16], F32, "cvec")
    cond2, cond_r = A.tile([128, 8, 2], F32, "cond2")
    modt, mod_r = A.tile([128, 48, 2], F32, "mod")
    g1t, g1_r = A.tile([128, 8, 2], F32, "g1")
    g2t, g2_r = A.tile([128, 8, 2], F32, "g2")
    lgt, lg_r = A.tile([128, 8], F32, "lg")
    nlgt, nlg_r = A.tile([128, 8], F32, "nlg")
    A.base = A.off

    def C_(name, rows=128, w=None, bf=True):
        o = CS[name]
        w = w if w is not None else (128 if name not in ("p96", "esel", "sel65") else (96 if name != "sel65" else 64))
        t = cbf if bf else cst
        return t[0:rows, o:o + w]

    ones_bf = C_("ones")
    ident_bf = C_("ident")
    CRES = [cst_r, cbf_r]

    def ppc(name, col, rows=128):
        return ppt[0:rows, PP[name] + col:PP[name] + col + 1]

    def partnorm(src_list, srcres, rows, nfeat, T, sqt, sq_r, ss_bank, lnt, ln_r, rst, rs_r):
        n = len(src_list)
        for i, s in enumerate(src_list):
            act(sqt[0:rows, i, :T], s, AF.Square, srcres, sq_r)
        for i in range(n):
            mm(PS[ss_bank][0:rows, :T], cbf[0:rows, CS["ones"]:CS["ones"] + rows], sqt[0:rows, i, :T], i == 0, i == n - 1,
               [sq_r, cbf_r], PR[ss_bank])
        act(lnt[0:rows, :T], PS[ss_bank][0:rows, :T], AF.Ln, [PR[ss_bank], eps_r], ln_r, bias=epst[0:rows, 0:1], scale=1.0 / nfeat)
        act(rst[0:rows, :T], lnt[0:rows, :T], AF.Exp, [ln_r], rs_r, scale=-0.5)

    sc.dma("sp", HT[:, 0:CTX], ctxT, reads=[], writes=[])
    for i in range(0, S_, 2048):
        w = min(2048, S_ - i)
        sc.dma("sp", HT[:, CTX + i:CTX + i + w], xT[:, i:i + w], reads=[], writes=[])
    load(cvt[:], cvec_d, cv_r)
    act(cond2[:, :, 0], cvt[:, 0:8], AF.Silu, [cv_r], cond_r)
    act(cond2[:, :, 1], cvt[:, 8:16], AF.Silu, [cv_r], cond_r)
    sc.barrier()

    def norm_phase(l, which):
        A.reset()
        hts = rot(2, [128, 8, 512], F32, "h")
        sqs = rot(2, [128, 8, 512], BF16, "sq")
        lns = rot(2, [128, 512], F32, "ln")
        rss = rot(2, [128, 512], F32, "rs")
        tms = rot(2, [128, 8, 512], F32, "tm")
        abs_ = rot(2, [128, 8, 514], BF16, "a")
        for ab_, abr_ in abs_.items:
            memset("pool", ab_[:, :, 0:1], 0.0, abr_)
        gt = g1t if which == 1 else g2t
        gr = g1_r if which == 1 else g2_r
        shv = 0 if which == 1 else 3
        HTv = HT.rearrange("(c p) t -> p c t", p=128)
        ATv = AT.rearrange("(c p) t -> p c t", p=128)
        for ti, (t0, T, isc, ac0) in enumerate(tiles):
            if isc and l == L - 1 and which == 2:
                continue
            col = 1 if isc else 0
            h, hr = hts.next()
            load(h[:, :, :T], HTv[:, :, t0:t0 + T], hr)
            sq, sqr = sqs.next()
            lnv, lnr = lns.next()
            rs, rsr = rss.next()
            bank = ti % 2
            partnorm([h[:, c, :T] for c in range(8)], [hr], 128, D, T, sq, sqr, bank, lnv, lnr, rs, rsr)
            tm, tmr = tms.next()
            tt("dve", tm[:, :, :T], h[:, :, :T], rs[:, :T].unsqueeze(1).to_broadcast([128, 8, T]), ALU.mult, [hr, rsr], tmr)
            ab, abr = abs_.next()
            first = (t0 == 0) or (t0 == CTX)
            lastt = (t0 + T == CTX) or (t0 + T == NT)
            for c in range(8):
                if c % 2 == 0:
                    act(ab[:, c, 1:1 + T], tm[:, c, :T], AF.Identity, [tmr, gr, mod_r], abr,
                        bias=modt[:, shv * 8 + c, col:col + 1], scale=gt[:, c, col:col + 1])
                else:
                    ts("pool", ab[:, c, 1:1 + T], tm[:, c, :T], gt[:, c, col:col + 1], modt[:, shv * 8 + c, col:col + 1],
                       ALU.mult, ALU.add, [tmr, gr, mod_r], abr)
            if lastt:
                memset("pool", ab[:, :, T + 1:T + 2], 0.0, abr)
            lo = 0 if first else 1
            hi = T + 2 if lastt else T + 1
            store(ATv[:, :, ac0 - 1 + lo:ac0 - 1 + hi], ab[:, :, lo:hi], abr)
        sc.barrier()

    def mod_phase(l):
        A.reset()
        load(ppt[:], pp_d[l], pp_r)
        ws = rot(2, [128, 8, 1024], F32, "wada")
        wv = w_ada_d[l].rearrange("(c p) n -> p c n", p=128)
        for g in range(6):
            w, wr = ws.next()
            for k in range(0, 8, 2):
                load(w[:, k:k + 2, :], wv[:, k:k + 2, g * 1024:(g + 1) * 1024], wr)
            for nn in range(8):
                j = g * 8 + nn
                for k in range(8):
                    mm(PS[0][:, 2 * j:2 * j + 2], w[:, k, nn * 128:(nn + 1) * 128], cond2[:, k, :], k == 0, k == 7,
                       [wr, cond_r], PR[0])
        psv = PS[0][:, 0:96].rearrange("p (j t) -> p j t", t=2)
        for col in range(2):
            tt("dve", modt[:, :, col], psv[:, :, col], ppt[:, PP["bada"]:PP["bada"] + 48], ALU.add, [PR[0], pp_r], mod_r)
        for col in range(2):
            stt("dve", g1t[:, :, col], modt[:, 8:16, col], 1.0, ppt[:, PP["n1w"]:PP["n1w"] + 8], ALU.add, ALU.mult,
                [mod_r, pp_r], g1_r)
            stt("dve", g2t[:, :, col], modt[:, 32:40, col], 1.0, ppt[:, PP["n2w"]:PP["n2w"] + 8], ALU.add, ALU.mult,
                [mod_r, pp_r], g2_r)
        act(nlgt[:], ppt[:, PP["retdec"]:PP["retdec"] + 8], AF.Exp, [pp_r], nlg_r)
        ts1("dve", lgt[:], nlgt[:], -1.0, ALU.mult, [nlg_r], lg_r)
        sc.barrier()

    def qk_finish(ps_bank, T, normcol, tabC, tabS, tab_r, out_ap, tmp):
        sq, sqr, lnv, lnr, rs, rsr, xn, xnr, t1, t1r, t2, t2r, ob, obr, ssb, swb = tmp
        partnorm([PS[ps_bank][0:96, :T]], [PR[ps_bank]], 96, 96, T, sq, sqr, ssb, lnv, lnr, rs, rsr)
        stt("dve", xn[0:96, :T], PS[ps_bank][0:96, :T], normcol, rs[0:96, :T], ALU.mult, ALU.mult, [PR[ps_bank], rsr, pp_r], xnr)
        mm(PS[swb][0:96, :T], cbf[0:96, CS["p96"]:CS["p96"] + 96], xn[0:96, :T], True, True, [xnr, cbf_r], PR[swb])
        tt("pool", t1[0:96, :T], xn[0:96, :T], tabC, ALU.mult, [xnr, tab_r], t1r)
        tt("dve", t2[0:96, :T], PS[swb][0:96, :T], tabS, ALU.mult, [PR[swb], tab_r], t2r)
        tt("pool", ob[0:96, :T], t1[0:96, :T], t2[0:96, :T], ALU.add, [t1r, t2r], obr)
        store(out_ap, ob[0:96, :T], obr)

    def inproj_mla(l):
        A.reset()
        stage = rot(2, [128, 2048], F32, "stg")
        win, win_r = A.tile([128, 8, 416], BF16, "winA")
        load_w(win, win_r, w_in_d[l][:, 0:416], 8, 416, stage)
        wqb, wqb_r = A.tile([128, 2, 768], BF16, "wqb")
        load_w(wqb, wqb_r, w_qb_d[l], 2, 768, stage)
        wkf, wkf_r = A.tile([128, 1024], F32, "wkf")
        load(wkf[:], w_kvb_d[l], wkf_r)
        wkn, wkn_r = A.tile([128, 8, 96], BF16, "wkn")
        memset("dve", wkn[:], 0.0, wkn_r)
        wkfv = wkf[:].rearrange("p (h e) -> p h e", e=128)
        cp("dve", wkn[:, :, 0:64], wkfv[:, :, 0:64], [wkf_r], wkn_r)
        wkv, wkv_r = A.tile([128, 8, 64], BF16, "wkv")
        cp("dve", wkv[:], wkfv[:, :, 64:128], [wkf_r], wkv_r)
        ats = rot(2, [128, 8, 512], BF16, "a")
        cqs = rot(2, [128, 2, 512], F32, "cq")
        sq2 = rot(2, [128, 2, 512], BF16, "sq2")
        lns = rot(2, [128, 512], F32, "ln")
        rss = rot(2, [128, 512], F32, "rs")
        cqn = rot(2, [128, 2, 512], BF16, "cqn")
        ckv = rot(2, [128, 512], F32, "ckv")
        ckn = rot(2, [128, 512], BF16, "ckn")
        krs = rot(2, [32, 512], BF16, "kr")
        tCs = rot(2, [96, 512], F32, "tC")
        tSs = rot(2, [96, 512], F32, "tS")
        sqh = rot(3, [96, 1, 512], BF16, "sqh")
        lnh = rot(3, [96, 512], F32, "lnh")
        rsh = rot(3, [96, 512], F32, "rsh")
        xnh = rot(3, [96, 512], BF16, "xnh")
        t1h = rot(3, [96, 512], F32, "t1h")
        t2h = rot(3, [96, 512], F32, "t2h")
        obh = rot(3, [96, 512], BF16, "obh")
        vas = rot(2, [128, 8, 65], BF16, "va")
        for v, vr in vas.items:
            memset("pool", v[:], 1.0, vr)
        ATv = AT.rearrange("(c p) t -> p c t", p=128)
        VAv = VA.rearrange("h t e -> t h e")
        hcount = [0]

        def tmpset():
            i = hcount[0]
            hcount[0] += 1
            sq, sqr = sqh.next(); lnv, lnr = lnh.next(); rs, rsr = rsh.next(); xn, xnr = xnh.next()
            t1, t1r = t1h.next(); t2, t2r = t2h.next(); ob, obr = obh.next()
            return (sq, sqr, lnv, lnr, rs, rsr, xn, xnr, t1, t1r, t2, t2r, ob, obr, 3 + (i % 2), 5 + (i % 2))

        for ti, (t0, T, isc, ac0) in enumerate(tiles):
            a, ar = ats.next()
            load(a[:, :, :T], ATv[:, :, ac0:ac0 + T], ar)
            tC, tCr = tCs.next()
            tS, tSr = tSs.next()
            load(tC[:, :T], mlaC_d[:, t0:t0 + T], tCr)
            load(tS[:, :T], mlaS_d[:, t0:t0 + T], tCr)
            cq, cqr = cqs.next()
            for c2 in range(2):
                for k in range(8):
                    mm(PS[c2][:, :T], win[:, k, c2 * 128:(c2 + 1) * 128], a[:, k, :T], k == 0, k == 7, [win_r, ar], PR[c2])
                cp("act", cq[:, c2, :T], PS[c2][:, :T], [PR[c2]], cqr)
            sq, sqr = sq2.next(); lnv, lnr = lns.next(); rs, rsr = rss.next()
            partnorm([cq[:, 0, :T], cq[:, 1, :T]], [cqr], 128, 256, T, sq, sqr, 2, lnv, lnr, rs, rsr)
            cn, cnr = cqn.next()
            for c2 in range(2):
                stt("dve", cn[:, c2, :T], cq[:, c2, :T], ppc("qna", c2), rs[:, :T], ALU.mult, ALU.mult, [cqr, rsr, pp_r], cnr)
            for k in range(8):
                mm(PS[0][:, :T], win[:, k, 256:384], a[:, k, :T], k == 0, k == 7, [win_r, ar], PR[0])
            kv, kvr = ckv.next()
            cp("act", kv[:, :T], PS[0][:, :T], [PR[0]], kvr)
            sq, sqr = sq2.next(); lnv, lnr = lns.next(); rs, rsr = rss.next()
            partnorm([kv[:, :T]], [kvr], 128, 128, T, sq, sqr, 2, lnv, lnr, rs, rsr)
            kn, knr = ckn.next()
            stt("dve", kn[:, :T], kv[:, :T], ppc("kvna", 0), rs[:, :T], ALU.mult, ALU.mult, [kvr, rsr, pp_r], knr)
            for k in range(8):
                mm(PS[1][0:32, :T], win[:, k, 384:416], a[:, k, :T], k == 0, k == 7, [win_r, ar], PR[1])
            kr, krr = krs.next()
            cp("act", kr[0:32, :T], PS[1][0:32, :T], [PR[1]], krr)
            for h in range(8):
                b = h % 3
                for c2 in range(2):
                    mm(PS[b][0:96, :T], wqb[:, c2, h * 96:(h + 1) * 96], cn[:, c2, :T], c2 == 0, c2 == 1, [wqb_r, cnr], PR[b])
                qk_finish(b, T, ppc("qn", 0, 96), tC[0:96, :T], tS[0:96, :T], tCr, QT[h][:, t0:t0 + T], tmpset())
            for h in range(8):
                b = h % 3
                mm(PS[b][0:96, :T], wkn[:, h, :], kn[:, :T], True, False, [wkn_r, knr], PR[b])
                mm(PS[b][0:96, :T], cbf[0:32, CS["esel"]:CS["esel"] + 96], kr[0:32, :T], False, True, [cbf_r, krr], PR[b])
                qk_finish(b, T, ppc("kn", 0, 96), tC[0:96, :T], tS[0:96, :T], tCr, KT[h][:, t0:t0 + T], tmpset())
            for j in range(T // 128):
                b = j % 2
                mm(PS[b][:, 0:512], kn[:, j * 128:(j + 1) * 128], wkv[:].rearrange("p h e -> p (h e)"), True, True, [knr, wkv_r], PR[b])
                va, var_ = vas.next()
                cp("act", va[:, :, 0:64], PS[b][:, 0:512].rearrange("p (h e) -> p h e", e=64), [PR[b]], var_)
                store(VAv[t0 + j * 128:t0 + (j + 1) * 128], va[:], var_)
        sc.barrier()

    def inproj_lin(l, kind):
        A.reset()
        stage = rot(2, [128, 2048], F32, "stg")
        c0 = C_GQ if kind == 0 else C_RQ
        ncols = 2080 if kind == 0 else 2048
        win, win_r = A.tile([128, 8, ncols], BF16, "winB")
        load_w(win, win_r, w_in_d[l][:, c0:c0 + ncols], 8, ncols, stage)
        if kind == 0:
            w2, w2_r = A.tile([16, 2, 512], BF16, "w2")
            for d in range(2):
                st, sr = stage.next()
                load(st[0:16, 0:512], w_gk2_d[l, d], sr)
                cp("dve", w2[0:16, d, :], st[0:16, 0:512], [sr], w2_r)
        ats = rot(2, [128, 8, 512], BF16, "a")
        stq = rot(2, [128, 4, 512], BF16, "stq")
        stv = rot(2, [128, 4, 512], BF16, "stv")
        if kind == 0:
            rfs = rot(2, [16, 512], BF16, "rf")
            rbs = rot(2, [16, 512], BF16, "rb")
            xbs = rot(2, [128, 512], F32, "xb")
            ees = rot(2, [128, 512], F32, "ee")
            sps = rot(2, [128, 4, 512], F32, "sp")
        else:
            tCs = rot(2, [128, 512], F32, "tC")
            tSs = rot(2, [128, 512], F32, "tS")
            xbf = rot(3, [128, 512], BF16, "xbf")
            t1s = rot(3, [128, 512], F32, "t1")
            t2s = rot(3, [128, 512], F32, "t2")
        ATv = AT.rearrange("(c p) t -> p c t", p=128)
        Qd, Kd, Gd, Vd = (GQ, GK, GG, GV) if kind == 0 else (RQ, RK, RG, RV)
        pbank = [0]

        def nb():
            pbank[0] += 1
            return pbank[0] % 4

        cpi = [0]
        for ti, (t0, T, isc, ac0) in enumerate(tiles):
            a, ar = ats.next()
            load(a[:, :, :T], ATv[:, :, ac0:ac0 + T], ar)
            if kind == 1 and 'notab' not in DBG:
                tC, tCr = tCs.next()
                tS, tSr = tSs.next()
                load(tC[:, :T], retC_d[:, t0:t0 + T], tCr)
                load(tS[:, :T], retS_d[:, t0:t0 + T], tSr)
            for wi, (dst, coff) in enumerate(((Qd, 0), (Kd, 512), (Gd, 1536))):
                st, sr = stq.next()
                for h in range(4):
                    b = nb()
                    for k in range(8):
                        mm(PS[b][:, :T], win[:, k, coff + h * 128:coff + (h + 1) * 128], a[:, k, :T], k == 0, k == 7, [win_r, ar], PR[b])
                    if wi == 2:
                        act(st[:, h, :T], PS[b][:, :T], AF.Silu, [PR[b]], sr)
                    elif kind == 0 or 'norope' in DBG:
                        cpi[0] += 1
                        cp("act" if cpi[0] % 2 else "dve", st[:, h, :T], PS[b][:, :T], [PR[b]], sr)
                    else:
                        xb_, xbr = xbf.next()
                        cp("act", xb_[:, :T], PS[b][:, :T], [PR[b]], xbr)
                        b2 = 4 + (h % 2)
                        if 'nomm' in DBG:
                            b2 = b
                        else:
                            mm(PS[b2][:, :T], C_("p128"), xb_[:, :T], True, True, [xbr, cbf_r], PR[b2])
                        t1, t1r = t1s.next()
                        t2, t2r = t2s.next()
                        tt("dve", t1[:, :T], PS[b][:, :T], tC[:, :T], ALU.mult, [PR[b], tCr], t1r)
                        tt("dve", t2[:, :T], PS[b2][:, :T], tS[:, :T], ALU.mult, [PR[b2], tSr], t2r)
                        tt("dve" if 'dveadd' in DBG else "pool", st[:, h, :T], t1[:, :T], t2[:, :T], ALU.add, [t1r, t2r], sr)
                store(dst.rearrange("(h p) t -> p h t", p=128)[:, :, t0:t0 + T], st[:, :, :T], sr)
            sv, svr = stv.next()
            nbk = T // 128
            for j in range(nbk):
                b = nb()
                for k in range(8):
                    mm(PS[b][:, 0:512], a[:, k, j * 128:(j + 1) * 128], win[:, k, 1024:1536], k == 0, k == 7, [win_r, ar], PR[b])
                cpi[0] += 1
                cp("act" if cpi[0] % 2 else "dve", sv[:, j, :], PS[b][:, 0:512], [PR[b]], svr)
            store(Vd[t0:t0 + T, :].rearrange("(j p) n -> p j n", p=128), sv[:, 0:nbk, :], svr)
            if kind == 0:
                rf, rfr = rfs.next()
                rb, rbr = rbs.next()
                for (rt, rr, cc) in ((rf, rfr, 2048), (rb, rbr, 2064)):
                    b = nb()
                    for k in range(8):
                        mm(PS[b][0:16, :T], win[:, k, cc:cc + 16], a[:, k, :T], k == 0, k == 7, [win_r, ar], PR[b])
                    cp("act", rt[0:16, :T], PS[b][0:16, :T], [PR[b]], rr)
                for d, (rt, rr) in enumerate(((rf, rfr), (rb, rbr))):
                    sp_, spr = sps.next()
                    for j in range(nbk):
                        b = nb()
                        mm(PS[b][:, 0:512], rt[0:16, j * 128:(j + 1) * 128], w2[0:16, d, :], True, True, [rr, w2_r], PR[b])
                        xb_, xbr = xbs.next()
                        tt("dve", xb_[:], PS[b][:, 0:512], ppt[:, PP["bgk"] + d * 512:PP["bgk"] + (d + 1) * 512], ALU.add, [PR[b], pp_r], xbr)
                        ee, eer = ees.next()
                        act(ee[:], xb_[:], AF.Exp, [xbr], eer, scale=-1.0)
                        act(sp_[:, j, :], ee[:], AF.Ln, [eer], spr, bias=1.0)
                    store(SPL[d, t0:t0 + T, :].rearrange("(j p) n -> p j n", p=128), sp_[:, 0:nbk, :], spr)
        sc.barrier()

    def inproj_gates(l):
        A.reset()
        stage = rot(2, [128, 2048], F32, "stg")
        win, win_r = A.tile([128, 8, 3072], BF16, "winD")
        load_w(win, win_r, w_in_d[l][:, C_G0:C_G0 + 3072], 8, 3072, stage)
        ats = rot(2, [128, 8, 512], BF16, "a")
        sts = rot(3, [128, 8, 512], BF16, "stg8")
        ATv = AT.rearrange("(c p) t -> p c t", p=128)
        bi = 0
        for ti, (t0, T, isc, ac0) in enumerate(tiles):
            if isc and l == L - 1:
                continue
            a, ar = ats.next()
            load(a[:, :, :T], ATv[:, :, ac0:ac0 + T], ar)
            for n in range(3):
                st, sr = sts.next()
                for c in range(8):
                    b = bi % 4
                    bi += 1
                    for k in range(8):
                        mm(PS[b][:, :T], win[:, k, n * 1024 + c * 128:n * 1024 + (c + 1) * 128], a[:, k, :T], k == 0, k == 7, [win_r, ar], PR[b])
                    act(st[:, c, :T], PS[b][:, :T], AF.Sigmoid, [PR[b], pp_r], sr, bias=ppc("bgate", n * 8 + c))
                store(GT[n].rearrange("(c p) t -> p c t", p=128)[:, :, t0:t0 + T], st[:, :, :T], sr)
        sc.barrier()

    def attention(l):
        A.reset()
        kts = rot(2, [96, NT], BF16, "kt")
        vas = rot(2, [128, NKT, 65], BF16, "va")
        qts = rot(3, [96, 512], BF16, "qt")
        pts = rot(3, [128, 2, 512], BF16, "pt")
        osb = rot(2, [65, 512], F32, "osb")
        rvs = rot(2, [64, 512], F32, "rinv")
        ybs = rot(2, [64, 512], BF16, "yb")
        scale = 96.0 ** -0.5
        qi = 0
        for h in range(8):
            kt, ktr = kts.next()
            va, var_ = vas.next()
            load(kt[:, :], KT[h], ktr)
            VAh = VA[h].rearrange("(k p) e -> p k e", p=128)
            for k0 in range(0, NKT, 8):
                k1 = min(NKT, k0 + 8)
                load(va[:, k0:k1, :], VAh[:, k0:k1, :], var_)
            for ti, (t0, T, isc, ac0) in enumerate(tiles):
                if isc and l == L - 1:
                    continue
                nk = CTX // 128 if isc else NKT
                qt, qtr = qts.next()
                load(qt[:, :T], QT[h][:, t0:t0 + T], qtr)
                ob = 4 + (qi % 2)
                qi += 1
                npair = nk // 2

                def qk2(pi):
                    for u in range(2):
                        b = 2 * (pi % 2) + u
                        i = 2 * pi + u
                        mm(PS[b][:, :T], kt[:, i * 128:(i + 1) * 128], qt[:, :T], True, True, [ktr, qtr], PR[b])

                qk2(0)
                for pi in range(npair):
                    if pi + 1 < npair:
                        qk2(pi + 1)
                    pp = pi % 2
                    pt, ptr = pts.next()
                    s2 = PSALL[:, pp * 1024:(pp + 1) * 1024].rearrange("p (a t) -> p a t", t=512)
                    act(pt[:, :, :T], s2[:, :, :T], AF.Exp, [PR[2 * pp], PR[2 * pp + 1]], ptr, scale=scale)
                    for u in range(2):
                        i = 2 * pi + u
                        mm(PS[ob][0:65, :T], va[:, i, :], pt[:, u, :T], i == 0, i == nk - 1, [var_, ptr], PR[ob])
                o, orr = osb.next()
                cp("dve", o[0:65, :T], PS[ob][0:65, :T], [PR[ob]], orr)
                mm(PS[6][0:64, :T], cst[0:65, CS["sel65"]:CS["sel65"] + 64], o[0:65, :T], True, True, [cst_r, orr], PR[6])
                rv, rvr = rvs.next()
                sc.op("dve", lambda e, rv=rv, T=T: e.reciprocal(out=rv[0:64, :T], in_=PS[6][0:64, :T]), reads=[PR[6]], writes=[rvr])
                yb, ybr = ybs.next()
                tt("dve", yb[0:64, :T], o[0:64, :T], rv[0:64, :T], ALU.mult, [orr, rvr], ybr)
                store(YM[h * 64:(h + 1) * 64, t0:t0 + T], yb[0:64, :T], ybr)
        sc.barrier()

    def sweep(l, kind):
        A.reset()
        Qd, Kd, Gd, Vd, Yd = (GQ, GK, GG, GV, YG) if kind == 0 else (RQ, RK, RG, RV, YR)
        qs = 128.0 ** -0.5 if kind == 0 else 1.0
        ks = 1.0 if kind == 0 else 128.0 ** -0.5
        NH = 4
        St = [A.tile([128, 128], F32, "S") for _ in range(NH)]
        Sb = [A.tile([128, 128], BF16, "Sb") for _ in range(NH)]
        qts = [rot(2, [128, 512], BF16, "q") for _ in range(NH)]
        kts_ = [rot(2, [128, 512], BF16, "k") for _ in range(NH)]
        vts = [rot(2, [128, 4, 128], BF16, "v") for _ in range(NH)]
        qds = [rot(2, [128, 512], BF16, "qd") for _ in range(NH)]
        kis = [rot(2, [128, 512], BF16, "ki") for _ in range(NH)]
        if kind == 0:
            spt = [rot(2, [128, 4, 128], F32, "spt") for _ in range(NH)]
            Es = [rot(2, [128, 512], F32, "E") for _ in range(NH)]
            Eis = [rot(2, [128, 512], F32, "Ei") for _ in range(NH)]
        else:
            Etab = [[A.tile([128, 128], F32, "Et") for _ in range(2)] for _ in range(NH)]
            Eitab = [[A.tile([128, 128], F32, "Eit") for _ in range(2)] for _ in range(NH)]
            for h in range(NH):
                for d in range(2):
                    idx = cst[:, CS["idxF" if d == 0 else "idxB"]:CS["idxF" if d == 0 else "idxB"] + 128]
                    act(Etab[h][d][0][:], idx, AF.Exp, [cst_r, lg_r], Etab[h][d][1], scale=lgt[:, d * 4 + h:d * 4 + h + 1])
                    act(Eitab[h][d][0][:], idx, AF.Exp, [cst_r, nlg_r], Eitab[h][d][1], scale=nlgt[:, d * 4 + h:d * 4 + h + 1])
        ktas = [rot(2, [128, 512], BF16, "kta") for _ in range(NH)]
        stmp = [A.tile([128, 128], F32, "stmp") for _ in range(NH)]
        ams = rot(4, [128, 128], BF16, "am")
        ofs = [rot(2, [128, 512], F32, "of") for _ in range(NH)]
        sgs = [rot(2, [128, 512], BF16, "sg") for _ in range(NH)]
        ots = rot(8, [128, 512], F32, "ot")
        sqs = rot(2, [128, 1, 512], BF16, "sq")
        lns = rot(2, [128, 512], F32, "ln")
        rss = rot(2, [128, 512], F32, "rs")
        y1s = rot(2, [128, 512], F32, "y1")
        ybs = rot(2, [128, 512], BF16, "yb")
        cnt = [0]
        for d in range(2):
            for h in range(NH):
                memset("dve", St[h][0][:], 0.0, St[h][1])
                memset("pool", Sb[h][0][:], 0.0, Sb[h][1])
            order = [0] + (list(range(1, len(tiles))) if d == 0 else list(range(len(tiles) - 1, 0, -1)))
            mask = cst[:, CS["maskF" if d == 0 else "maskB"]:CS["maskF" if d == 0 else "maskB"] + 128]
            um = cst[:, CS["uF" if d == 0 else "uB"]:CS["uF" if d == 0 else "uB"] + 128]
            for ti in order:
                t0, T, isc, ac0 = tiles[ti]
                skip_out = isc and l == L - 1
                nbk = T // 128
                cur = []
                for h in range(NH):
                    q, qr = qts[h].next(); k, kr = kts_[h].next(); v, vr = vts[h].next()
                    load(q[:, :T], Qd[h * 128:(h + 1) * 128, t0:t0 + T], qr)
                    load(k[:, :T], Kd[h * 128:(h + 1) * 128, t0:t0 + T], kr)
                    load(v[:, 0:nbk, :], Vd[t0:t0 + T, h * 128:(h + 1) * 128].rearrange("(j p) n -> p j n", p=128), vr)
                    qd, qdr = qds[h].next(); ki, kir = kis[h].next()
                    if kind == 0:
                        sp_, spr = spt[h].next()
                        load(sp_[:, 0:nbk, :], SPL[d, t0:t0 + T, h * 128:(h + 1) * 128].rearrange("(j p) n -> p j n", p=128), spr)
                        cb = 0
                        for j in range(nbk):
                            mm(PS[cb][:, j * 128:(j + 1) * 128], sp_[:, j, :], um, True, True, [spr, cst_r], PR[cb])
                        E, Er = Es[h].next(); Ei, Eir = Eis[h].next()
                        act(E[:, :T], PS[cb][:, :T], AF.Exp, [PR[cb]], Er)
                        act(Ei[:, :T], PS[cb][:, :T], AF.Exp, [PR[cb]], Eir, scale=-1.0)
                        stt("dve", qd[:, :T], q[:, :T], qs, E[:, :T], ALU.mult, ALU.mult, [qr, Er], qdr)
                        stt("dve", ki[:, :T], k[:, :T], ks, Ei[:, :T], ALU.mult, ALU.mult, [kr, Eir], kir)
                        gsrc = (E, Er)
                    else:
                        E, Er = Etab[h][d]; Ei, Eir = Eitab[h][d]
                        stt("dve", qd[:, :T].rearrange("p (j n) -> p j n", n=128), q[:, :T].rearrange("p (j n) -> p j n", n=128), qs,
                            E[:].unsqueeze(1).to_broadcast([128, nbk, 128]), ALU.mult, ALU.mult, [qr, Er], qdr)
                        stt("dve", ki[:, :T].rearrange("p (j n) -> p j n", n=128), k[:, :T].rearrange("p (j n) -> p j n", n=128), ks,
                            Ei[:].unsqueeze(1).to_broadcast([128, nbk, 128]), ALU.mult, ALU.mult, [kr, Eir], kir)
                        gsrc = (E, Er)
                    for j in range(nbk):
                        sc.op("pe", lambda e, j=j, ki=ki: e.transpose(PSB[:, j * 128:(j + 1) * 128], ki[:, j * 128:(j + 1) * 128], ident_bf),
                              reads=[kir, cbf_r], writes=[PBR[0]])
                    kta, ktar = ktas[h].next()
                    cp("act", kta[:, :T], PSB[:, 0:T], [PBR[0]], ktar)
                    cur.append((q, qr, k, kr, v, vr, qd, qdr, ki, kir, gsrc, kta, ktar))
                ob = [None] * NH
                blocks = list(range(nbk)) if d == 0 else list(range(nbk - 1, -1, -1))
                for j in blocks:
                    c0 = j * 128
                    for h in range(NH):
                        q, qr, k, kr, v, vr, qd, qdr, ki, kir, (E, Er), kta, ktar = cur[h]
                        if kind == 0:
                            gcol = E[:, c0 + 127:c0 + 128] if d == 0 else E[:, c0:c0 + 1]
                        else:
                            gcol = E[:, 127:128] if d == 0 else E[:, 0:1]
                        pb = cnt[0] % 2
                        cnt[0] += 1
                        ab = 4 + pb
                        mm(PS[ab][:, 0:128], ki[:, c0:c0 + 128], qd[:, c0:c0 + 128], True, True, [kir, qdr], PR[ab])
                        am, amr = ams.next()
                        tt("dve", am[:], PS[ab][:, 0:128], mask, ALU.mult, [PR[ab], cst_r], amr)
                        obk = 2 + (h % 2)
                        mm(PS[obk][:, 0:128], v[:, j, :], am[:], True, False, [vr, amr], PR[obk])
                        mm(PS[obk][:, 0:128], Sb[h][0][:], qd[:, c0:c0 + 128], False, True, [Sb[h][1], qdr], PR[obk])
                        if not skip_out:
                            if ob[h] is None:
                                ob[h] = ots.next()
                            ot, otr = ob[h]
                            cp("act", ot[:, c0:c0 + 128], PS[obk][:, 0:128], [PR[obk]], otr)
                        ib = 1 if h % 2 == 0 else 6
                        mm(PS[ib][:, 0:128], kta[:, c0:c0 + 128], v[:, j, :], True, True, [ktar, vr], PR[ib])
                        tmp, tmpr = stmp[h]
                        tt("dve", tmp[:], St[h][0][:], PS[ib][:, 0:128], ALU.add, [St[h][1], PR[ib]], tmpr)
                        act(Sb[h][0][:], tmp[:], AF.Copy, [tmpr, Er], Sb[h][1], scale=gcol)
                        ts1("dve", St[h][0][:], tmp[:], gcol, ALU.mult, [tmpr, Er], St[h][1])
                if skip_out:
                    continue
                for h in range(NH):
                    ot, otr = ob[h]
                    key = (kind, h, ti)
                    if d == 0:
                        r_ = of_res.setdefault(key, Res(f"of{key}"))
                        store(OF[h * 128:(h + 1) * 128, t0:t0 + T], ot[:, :T], otr, writes=[r_])
                    else:
                        r_ = of_res[key]
                        of_, ofr = ofs[h].next()
                        load(of_[:, :T], OF[h * 128:(h + 1) * 128, t0:t0 + T], ofr, reads=[r_])
                        sg, sgr = sgs[h].next()
                        load(sg[:, :T], Gd[h * 128:(h + 1) * 128, t0:t0 + T], sgr)
                        tt("dve", ot[:, :T], ot[:, :T], of_[:, :T], ALU.add, [otr, ofr], otr)
                        sq, sqr = sqs.next(); lnv, lnr = lns.next(); rs, rsr = rss.next()
                        partnorm([ot[:, :T]], [otr], 128, 128, T, sq, sqr, 0, lnv, lnr, rs, rsr)
                        y1, y1r = y1s.next()
                        ncol = ppc("gon", 0) if kind == 0 else onec[:, 0:1]
                        stt("dve", y1[:, :T], ot[:, :T], ncol, rs[:, :T], ALU.mult, ALU.mult, [otr, rsr, pp_r, onec_r], y1r)
                        yb, ybr = ybs.next()
                        tt("pool", yb[:, :T], y1[:, :T], sg[:, :T], ALU.mult, [y1r, sgr], ybr)
                        store(Yd[h * 128:(h + 1) * 128, t0:t0 + T], yb[:, :T], ybr)
            sc.barrier()

    def merge_phase(l):
        A.reset()
        stage = rot(2, [128, 2048], F32, "stg")
        wbr, wbr_r = A.tile([128, 12, 1024], BF16, "wbr")
        load_w(wbr, wbr_r, w_br_d[l].rearrange("n k m -> (n k) m"), 12, 1024, stage)
        wo, wo_r = A.tile([128, 8, 1024], BF16, "wo")
        load_w(wo, wo_r, w_out_d[l], 8, 1024, stage)
        ys = [rot(2, [128, 4, 512], BF16, f"y{n}") for n in range(3)]
        gs = [rot(2, [128, 8, 512], BF16, f"g{n}") for n in range(3)]
        hts = rot(2, [128, 8, 512], F32, "h")
        m32 = rot(2, [128, 512], F32, "m32")
        tms = rot(3, [128, 512], F32, "tm")
        mbs = rot(2, [128, 8, 512], BF16, "mb")
        HTv = HT.rearrange("(c p) t -> p c t", p=128)
        bi = 0
        for ti, (t0, T, isc, ac0) in enumerate(tiles):
            if isc and l == L - 1:
                continue
            col = 1 if isc else 0
            yy = []
            gg = []
            for n, Yd in enumerate((YM, YG, YR)):
                y, yr = ys[n].next()
                load(y[:, :, :T], Yd.rearrange("(c p) t -> p c t", p=128)[:, :, t0:t0 + T], yr)
                g, gr = gs[n].next()
                load(g[:, :, :T], GT[n].rearrange("(c p) t -> p c t", p=128)[:, :, t0:t0 + T], gr)
                yy.append((y, yr))
                gg.append((g, gr))
            h, hr = hts.next()
            load(h[:, :, :T], HTv[:, :, t0:t0 + T], hr)
            mb, mbr = mbs.next()
            for c in range(8):
                m, mr = m32.next()
                for n in range(3):
                    b = bi % 3
                    bi += 1
                    for k in range(4):
                        mm(PS[b][:, :T], wbr[:, n * 4 + k, c * 128:(c + 1) * 128], yy[n][0][:, k, :T], k == 0, k == 3, [wbr_r, yy[n][1]], PR[b])
                    if n == 0:
                        tt("dve", m[:, :T], PS[b][:, :T], gg[0][0][:, c, :T], ALU.mult, [PR[b], gg[0][1]], mr)
                    else:
                        tm, tmr = tms.next()
                        tt("dve", tm[:, :T], PS[b][:, :T], gg[n][0][:, c, :T], ALU.mult, [PR[b], gg[n][1]], tmr)
                        if n == 1:
                            tt("pool", m[:, :T], m[:, :T], tm[:, :T], ALU.add, [mr, tmr], mr)
                        else:
                            tt("pool", mb[:, c, :T], m[:, :T], tm[:, :T], ALU.add, [mr, tmr], mbr)
            for c in range(8):
                b = 3 + (c % 2)
                for k in range(8):
                    mm(PS[b][:, :T], wo[:, k, c * 128:(c + 1) * 128], mb[:, k, :T], k == 0, k == 7, [wo_r, mbr], PR[b])
                stt("dve", h[:, c, :T], PS[b][:, :T], modt[:, 16 + c, col:col + 1], h[:, c, :T], ALU.mult, ALU.add, [PR[b], mod_r, hr], hr)
            store(HTv[:, :, t0:t0 + T], h[:, :, :T], hr)
        sc.barrier()

    def ffn_phase(l):
        A.reset()
        stage = rot(2, [128, 2048], F32, "stg")
        wfi, wfi_r = A.tile([128, 8, 2 * DFF], BF16, "wfi")
        wfo, wfo_r = A.tile([128, NJ, 1024], BF16, "wfo")
        load_w(wfi, wfi_r, w_fi_d[l], 8, 2 * DFF, stage)
        load_w(wfo, wfo_r, w_fo_d[l], NJ, 1024, stage)
        FT = 256
        ats = rot(2, [128, 8, FT + 2], BF16, "a2")
        hts = rot(2, [128, 8, FT], F32, "h")
        hid = rot(1, [128, NJ, FT], BF16, "hid")
        gsb = rot(2, [128, FT + 2], F32, "gsb")
        acc = rot(2, [128, FT], F32, "acc")
        gel = rot(2, [128, FT], F32, "gel")
        ATv = AT.rearrange("(c p) t -> p c t", p=128)
        HTv = HT.rearrange("(c p) t -> p c t", p=128)
        OTv = outT.rearrange("(c p) t -> p c t", p=128)
        last = (l == L - 1)
        nlat = S_ // FT
        for fi, (t0, T, isc, ac0) in enumerate(ftiles):
            if isc and last:
                continue
            col = 1 if isc else 0
            a, ar = ats.next()
            load(a[:, :, :], ATv[:, :, ac0 - 1:ac0 + T + 1], ar)
            first = (t0 == 0) or (t0 == CTX)
            lastt = (t0 + T == CTX) or (t0 + T == NT)
            if first:
                memset("pool", a[:, :, 0:1], 0.0, ar)
            if lastt:
                memset("pool", a[:, :, T + 1:T + 2], 0.0, ar)
            h, hr = hts.next()
            load(h[:, :, :], HTv[:, :, t0:t0 + T], hr)
            hd, hdr = hid.next()
            for j in range(NJ):
                gb = j % 2
                ub = 2 + (j % 2)
                for k in range(8):
                    mm(PS[gb][:, 0:T], wfi[:, k, j * 128:(j + 1) * 128], a[:, k, 1:T + 1], k == 0, k == 7, [wfi_r, ar], PR[gb])
                for k in range(8):
                    mm(PS[4][:, 0:2], wfi[:, k, j * 128:(j + 1) * 128], a[:, k, 0:T + 2:T + 1], k == 0, k == 7, [wfi_r, ar], PR[4])
                for k in range(8):
                    mm(PS[ub][:, 0:T], wfi[:, k, DFF + j * 128:DFF + (j + 1) * 128], a[:, k, 1:T + 1], k == 0, k == 7, [wfi_r, ar], PR[ub])
                g, gr = gsb.next()
                cp("act", g[:, 1:T + 1], PS[gb][:, 0:T], [PR[gb]], gr)
                cp("act", g[:, 0:T + 2:T + 1], PS[4][:, 0:2], [PR[4]], gr)
                ac, acr = acc.next()
                ts("dve", ac[:], g[:, 0:T], ppc("wdw", j), ppc("bdw", j), ALU.mult, ALU.add, [gr, pp_r], acr)
                stt("dve", ac[:], g[:, 1:T + 1], ppc("wdw", NJ + j), ac[:], ALU.mult, ALU.add, [gr, pp_r, acr], acr)
                stt("dve", ac[:], g[:, 2:T + 2], ppc("wdw", 2 * NJ + j), ac[:], ALU.mult, ALU.add, [gr, pp_r, acr], acr)
                ge, ger = gel.next()
                act(ge[:], ac[:], AF.Gelu_apprx_tanh, [acr], ger)
                tt("dve", hd[:, j, :], PS[ub][:, 0:T], ge[:], ALU.mult, [PR[ub], ger], hdr)
            for c in range(8):
                b = 5 + (c % 2)
                for j in range(NJ):
                    mm(PS[b][:, 0:T], wfo[:, j, c * 128:(c + 1) * 128], hd[:, j, :], j == 0, j == NJ - 1, [wfo_r, hdr], PR[b])
                stt("dve", h[:, c, :], PS[b][:, 0:T], modt[:, 40 + c, col:col + 1], h[:, c, :], ALU.mult, ALU.add, [PR[b], mod_r, hr], hr)
            if last:
                store(OTv[:, :, t0 - CTX:t0 - CTX + T], h[:, :, :], hr)
            else:
                store(HTv[:, :, t0:t0 + T], h[:, :, :], hr)
        sc.barrier()

    plist = []
    for l in range(L):
        plist += [(mod_phase, (l,)), (norm_phase, (l, 1)), (inproj_mla, (l,)), (inproj_lin, (l, 0)), (inproj_lin, (l, 1)),
                  (inproj_gates, (l,)), (attention, (l,)), (sweep, (l, 0)), (sweep, (l, 1)), (merge_phase, (l,)),
                  (norm_phase, (l, 2)), (ffn_phase, (l,))]
    for f, a in plist[:nphase]:
        f(*a)
    print(f"[build] sbuf peak {A.peak}", flush=True)
    sc.emit()
    return nc


def _pc(v, n):
    return np.ascontiguousarray(np.asarray(v, np.float32).reshape(n, 128).T)


def _consts():
    c = np.zeros((128, NCST), np.float32)
    c[:, CS["ones"]:CS["ones"] + 128] = 1.0
    c[:, CS["ident"]:CS["ident"] + 128] = np.eye(128, dtype=np.float32)
    p = np.zeros((128, 128), np.float32)
    for i in range(64):
        p[64 + i, i] = -1.0
        p[i, 64 + i] = 1.0
    c[:, CS["p128"]:CS["p128"] + 128] = p
    j = np.arange(128)[:, None]
    i = np.arange(128)[None, :]
    c[:, CS["maskF"]:CS["maskF"] + 128] = (j <= i)
    c[:, CS["maskB"]:CS["maskB"] + 128] = (j > i)
    c[:, CS["uF"]:CS["uF"] + 128] = (j <= i) * (-1.0 / 16.0)
    c[:, CS["uB"]:CS["uB"] + 128] = (j >= i) * (-1.0 / 16.0)
    c[:, CS["idxF"]:CS["idxF"] + 128] = (i + 1.0) * np.ones((128, 1))
    c[:, CS["idxB"]:CS["idxB"] + 128] = (128.0 - i) * np.ones((128, 1))
    p96 = np.zeros((128, 96), np.float32)
    for base in (64, 80):
        for t in range(8):
            p96[base + 8 + t, base + t] = -1.0
            p96[base + t, base + 8 + t] = 1.0
    c[:, CS["p96"]:CS["p96"] + 96] = p96
    es = np.zeros((128, 96), np.float32)
    for t in range(32):
        es[t, 64 + t] = 1.0
    c[:, CS["esel"]:CS["esel"] + 96] = es
    s65 = np.zeros((128, 64), np.float32)
    s65[64, :] = 1.0
    c[:, CS["sel65"]:CS["sel65"] + 64] = s65
    return c


def _tables(S_, CTX):
    NT = CTX + S_
    f32 = np.float32
    t = np.arange(S_)
    row = (t // 64).astype(f32)
    colp = (t % 64).astype(f32)
    inv = (f32(10000.0) ** (-(np.arange(8, dtype=f32) * f32(2.0) / f32(16)))).astype(f32)
    mc = np.ones((96, NT), f32)
    ms = np.zeros((96, NT), f32)
    ar = (row[:, None] * inv[None, :]).astype(f32)
    ac = (colp[:, None] * inv[None, :]).astype(f32)
    for i in range(8):
        for base, ang in ((64, ar), (80, ac)):
            mc[base + i, CTX:] = np.cos(ang[:, i]); mc[base + 8 + i, CTX:] = np.cos(ang[:, i])
            ms[base + i, CTX:] = np.sin(ang[:, i]); ms[base + 8 + i, CTX:] = np.sin(ang[:, i])
    pos = np.arange(NT).astype(f32)
    rinv = (f32(1.0) / (f32(10000.0) ** np.linspace(0.0, 1.0, 64, dtype=f32))).astype(f32)
    ang = (pos[:, None] * rinv[None, :]).astype(f32)
    rc = np.concatenate([np.cos(ang), np.cos(ang)], 1).T.astype(f32)
    rs_ = np.concatenate([np.sin(ang), np.sin(ang)], 1).T.astype(f32)
    return np.ascontiguousarray(mc), np.ascontiguousarray(ms), np.ascontiguousarray(rc), np.ascontiguousarray(rs_)


def _pack_pp(inp, L):
    pp = np.zeros((L, 128, NPP), np.float32)
    for l in range(L):
        def put(name, arr):
            arr = np.asarray(arr, np.float32)
            pp[l, :arr.shape[0], PP[name]:PP[name] + arr.shape[1]] = arr
        put("bada", _pc(inp["b_ada"][l], 48))
        put("n1w", _pc(inp["norm1_w"][l], 8))
        put("n2w", _pc(inp["norm2_w"][l], 8))
        put("bgate", _pc(np.asarray(inp["b_gate"][l]).reshape(-1), 24))
        put("qna", _pc(inp["mla_q_norm_a"][l], 2))
        put("kvna", _pc(inp["mla_kv_norm_a"][l], 1))
        put("qn", np.asarray(inp["mla_q_norm"][l]).reshape(96, 1))
        put("kn", np.asarray(inp["mla_k_norm"][l]).reshape(96, 1))
        put("gon", _pc(inp["gla_o_norm"][l], 1))
        put("wdw", _pc(np.asarray(inp["w_dw"][l]).reshape(-1), 66))
        put("bdw", _pc(inp["b_dw"][l], 22))
        put("retdec", np.broadcast_to(np.asarray(inp["ret_decay"][l]).reshape(1, 8), (128, 8)))
        put("bgk", np.broadcast_to(np.asarray(inp["gla_b_gk"][l]).reshape(1, 1024), (128, 1024)))
    return pp


_NC_CACHE = {}


def _run(inp, S_, CTX, L, ncores, debug=False, nphase=999):
    key = (S_, CTX, L, debug, nphase)
    if key not in _NC_CACHE:
        _NC_CACHE[key] = build(S_, CTX, L, debug, nphase)
    nc = _NC_CACHE[key]
    B = np.asarray(inp["x"]).shape[0]
    mc, ms, rc, rs_ = _tables(S_, CTX)
    shared = {
        "pp": _pack_pp(inp, L), "cst": _consts(), "mlaC": mc, "mlaS": ms, "retC": rc, "retS": rs_,
    }
    for k in ("w_ada", "w_in", "mla_w_qb", "mla_w_kvb", "gla_w_gk2", "w_branch", "w_out", "w_ffn_in", "w_ffn_out"):
        shared[k] = np.ascontiguousarray(np.asarray(inp[k], np.float32))
    cc = _pc(inp["c_ctx"], 8)
    in_maps = []
    for core in range(ncores):
        b = core % B
        m = dict(shared)
        m["xT"] = np.ascontiguousarray(np.asarray(inp["x"][b], np.float32).T)
        m["ctxT"] = np.ascontiguousarray(np.asarray(inp["ctx"][b], np.float32).T)
        m["cvec"] = np.ascontiguousarray(np.concatenate([_pc(inp["c"][b], 8), cc], 1))
        in_maps.append(m)
    res = run_bass_kernel_spmd(nc, in_maps, core_ids=list(range(ncores)))
    return res


def kernel(**inputs):
    x = np.asarray(inputs["x"])
    B, S_, _ = x.shape
    CTX = np.asarray(inputs["ctx"]).shape[1]
    L = np.asarray(inputs["w_ada"]).shape[0]
    res = _run(inputs, S_, CTX, L, 8)
    out = np.empty((B, S_, D), np.float32)
    for b in range(B):
        out[b] = res.results[b]["outT"].T
    return out
```

```python
import concourse.bass as bass
import concourse.mybir as mybir

SEM_LIMIT = 1000000000
DMA_K = 12


class Res:
    __slots__ = ("name", "lw", "rd_c", "rd_d", "excl")

    def __init__(self, name, excl=False):
        self.name = name
        self.excl = excl
        self.lw = None
        self.rd_c = {}
        self.rd_d = []


class Op:
    __slots__ = ("eng", "fn", "reads", "writes", "dma", "acc", "deps", "sig", "ev", "waits", "barrier")

    def __init__(self, eng, fn, reads, writes, dma, acc):
        self.eng = eng
        self.fn = fn
        self.reads = reads
        self.writes = writes
        self.dma = dma
        self.acc = acc
        self.deps = set()
        self.sig = False
        self.ev = None
        self.waits = []
        self.barrier = False


class Sched:
    ENGS = ("pe", "act", "dve", "pool", "sp")

    def __init__(self, nc):
        self.nc = nc
        self.ops = []

    def op(self, eng, fn, reads=(), writes=(), acc=False):
        self.ops.append(Op(eng, fn, list(reads), list(writes), False, acc))

    def dma(self, q, out, in_, reads=(), writes=()):
        self.ops.append(Op(q, lambda e, o=out, i=in_: e.dma_start(out=o, in_=i), list(reads), list(writes), True, False))

    def custom_dma(self, q, fn, reads=(), writes=()):
        self.ops.append(Op(q, fn, list(reads), list(writes), True, False))

    def barrier(self):
        o = Op(None, None, [], [], False, False)
        o.barrier = True
        self.ops.append(o)

    def _analyse(self):
        ops = self.ops
        last_c = {}
        last_d = {e: [] for e in self.ENGS}
        pend = {e: set() for e in self.ENGS}
        for i, op in enumerate(ops):
            if op.barrier:
                deps = set(last_c.values())
                for q in self.ENGS:
                    deps.update(last_d[q][-DMA_K:])
                for e in self.ENGS:
                    pend[e] |= deps
                continue
            d = op.deps
            if pend[op.eng]:
                d |= pend[op.eng]
                pend[op.eng] = set()
            for r in op.reads:
                if r.lw is not None:
                    d.add(r.lw)
                if r.excl:
                    for e2, j in r.rd_c.items():
                        if e2 != op.eng:
                            d.add(j)
            for w in op.writes:
                if w.lw is not None:
                    lwop = ops[w.lw]
                    if not (op.acc and op.eng == "pe" and lwop.eng == "pe" and not lwop.dma):
                        d.add(w.lw)
                d.update(w.rd_c.values())
                d.update(w.rd_d)
            d.discard(i)
            for r in op.reads:
                if op.dma:
                    r.rd_d.append(i)
                else:
                    r.rd_c[op.eng] = i
            for w in op.writes:
                w.lw = i
                w.rd_c = {}
                w.rd_d = []
            if op.dma:
                last_d[op.eng].append(i)
            else:
                last_c[op.eng] = i
            for j in d:
                ops[j].sig = True
        self.final_deps = set(last_c.values())
        for q in self.ENGS:
            self.final_deps.update(last_d[q][-DMA_K:])
        for j in self.final_deps:
            ops[j].sig = True

    def _assign(self):
        nc = self.nc
        ops = self.ops
        csem = {}
        ccnt = {}
        dsem = {e: [None] * DMA_K for e in self.ENGS}
        dcnt = {e: [0] * DMA_K for e in self.ENGS}
        dprev = {e: [None] * DMA_K for e in self.ENGS}
        dn = {e: 0 for e in self.ENGS}
        waited = {e: {} for e in self.ENGS}
        self.nsem = 0

        def newsem(tag):
            self.nsem += 1
            return nc.alloc_semaphore(name=f"s_{tag}_{self.nsem}")

        def add_wait(op, ev):
            sem, val = ev
            w = waited[op.eng]
            k = id(sem)
            if w.get(k, 0) >= val:
                return
            w[k] = val
            op.waits.append((sem, val))

        for i, op in enumerate(ops):
            if op.barrier:
                continue
            e = op.eng
            for j in sorted(op.deps):
                add_wait(op, ops[j].ev)
            if op.dma:
                k = dn[e] % DMA_K
                dn[e] += 1
                if dprev[e][k] is not None:
                    add_wait(op, dprev[e][k])
                if dsem[e][k] is None or dcnt[e][k] + 16 > SEM_LIMIT:
                    dsem[e][k] = newsem("d" + e)
                    dcnt[e][k] = 0
                dcnt[e][k] += 16
                op.ev = (dsem[e][k], dcnt[e][k])
                dprev[e][k] = op.ev
                op.sig = True
            elif op.sig:
                if e not in csem or ccnt[e] + 1 > SEM_LIMIT:
                    csem[e] = newsem("c" + e)
                    ccnt[e] = 0
                ccnt[e] += 1
                op.ev = (csem[e], ccnt[e])
        print("[sched] ccnt", ccnt, "dcnt max", {e: max(v) for e, v in dcnt.items()}, flush=True)
        self.final_waits = []
        fw = {}
        for j in sorted(self.final_deps):
            sem, val = ops[j].ev
            k = id(sem)
            if fw.get(k, (None, 0))[1] < val:
                fw[k] = (sem, val)
        self.final_waits = list(fw.values())

    def emit(self):
        self._analyse()
        self._assign()
        nc = self.nc
        ops = self.ops

        def run(e):
            def body(eng):
                for op in ops:
                    if op.barrier or op.eng != e:
                        continue
                    for sem, val in op.waits:
                        eng.wait_ge(sem, val)
                    ins = op.fn(eng)
                    if op.sig:
                        ins.then_inc(op.ev[0], 16 if op.dma else 1)
                if e == "sp":
                    for sem, val in self.final_waits:
                        eng.wait_ge(sem, val)
            return body

        with nc.Block() as block:
            block.tensor(run("pe"))
            block.scalar(run("act"))
            block.vector(run("dve"))
            block.gpsimd(run("pool"))
            block.sync(run("sp"))
        n = sum(1 for o in ops if not o.barrier)
        nw = sum(len(o.waits) for o in ops if not o.barrier)
        print(f"[sched] ops={n} waits={nw} sems={self.nsem}", flush=True)


class SbufAlloc:
    def __init__(self, nc, base=16640, limit=229376):
        self.nc = nc
        self.base = base
        self.off = base
        self.limit = limit
        self.n = 0
        self.peak = 0

    def reset(self, to=None):
        self.off = self.base if to is None else to

    def mark(self):
        return self.off

    def tile(self, shape, dtype, name="t"):
        esz = {mybir.dt.float32: 4, mybir.dt.bfloat16: 2}[dtype]
        nb = esz
        for s in shape[1:]:
            nb *= s
        nb = (nb + 63) // 64 * 64
        assert self.off + nb <= self.limit, f"SBUF overflow {name}: {self.off}+{nb}>{self.limit}"
        self.n += 1
        h = self.nc.alloc_sbuf_tensor_at(f"{name}_{self.n}", list(shape), dtype, offset=self.off)
        self.off += nb
        self.peak = max(self.peak, self.off)
        r = Res(f"{name}_{self.n}")
        return h, r


import os
import numpy as np
DBG = os.environ.get('KDBG', '')
from concourse.bass_utils import run_bass_kernel_spmd

F32 = mybir.dt.float32
BF16 = mybir.dt.bfloat16
ALU = mybir.AluOpType
AF = mybir.ActivationFunctionType

D = 1024
KC = 8
NIN = 7616
DFF = 2816
NJ = 22
EPS = 1e-6
C_MQ, C_MKV, C_MKR, C_GQ, C_GK, C_GV, C_GG, C_RF, C_RB, C_RQ, C_RK, C_RV, C_RG, C_G0 = (
    0, 256, 384, 416, 928, 1440, 1952, 2464, 2480, 2496, 3008, 3520, 4032, 4544)

PP = {}
_o = 0
for _n, _w in (("bada", 48), ("n1w", 8), ("n2w", 8), ("bgate", 24), ("qna", 2), ("kvna", 1), ("qn", 1), ("kn", 1),
               ("gon", 1), ("wdw", 66), ("bdw", 22), ("retdec", 8), ("bgk", 1024)):
    PP[_n] = _o
    _o += _w
NPP = _o
CS = {}
_o = 0
for _n, _w in (("ones", 128), ("ident", 128), ("p128", 128), ("maskF", 128), ("maskB", 128), ("uF", 128), ("uB", 128),
               ("idxF", 128), ("idxB", 128), ("p96", 96), ("esel", 96), ("sel65", 64)):
    CS[_n] = _o
    _o += _w
NCST = _o


def build(S_, CTX, L, debug=False, nphase=999):
    NT = CTX + S_
    NTP = NT + 4
    nlt = S_ // 512
    tiles = [(0, CTX, True, 1)] + [(CTX + 512 * i, 512, False, CTX + 3 + 512 * i) for i in range(nlt)]
    ftiles = [(0, 256, True, 1)] if CTX == 256 else [(i * 256, 256, True, 1 + i * 256) for i in range(CTX // 256)]
    ftiles = ftiles + [(CTX + 256 * i, 256, False, CTX + 3 + 256 * i) for i in range(S_ // 256)]
    NKT = NT // 128

    nc = bass.Bass("TRN2", target_bir_lowering=False)
    sc = Sched(nc)
    A = SbufAlloc(nc)

    def din(name, shape, dt=F32):
        return nc.dram_tensor(name, list(shape), dt, kind="ExternalInput").ap()

    skind = "ExternalOutput" if debug else "Internal"

    def dscr(name, shape, dt):
        return nc.dram_tensor(name, list(shape), dt, kind=skind).ap()

    xT = din("xT", [D, S_])
    ctxT = din("ctxT", [D, CTX])
    cvec_d = din("cvec", [128, 16])
    pp_d = din("pp", [L, 128, NPP])
    cst_d = din("cst", [128, NCST])
    mlaC_d = din("mlaC", [96, NT])
    mlaS_d = din("mlaS", [96, NT])
    retC_d = din("retC", [128, NT])
    retS_d = din("retS", [128, NT])
    w_ada_d = din("w_ada", [L, D, 6 * D])
    w_in_d = din("w_in", [L, D, NIN])
    w_qb_d = din("mla_w_qb", [L, 256, 768])
    w_kvb_d = din("mla_w_kvb", [L, 128, 1024])
    w_gk2_d = din("gla_w_gk2", [L, 2, 16, 512])
    w_br_d = din("w_branch", [L, 3, 512, D])
    w_out_d = din("w_out", [L, D, D])
    w_fi_d = din("w_ffn_in", [L, D, 2 * DFF])
    w_fo_d = din("w_ffn_out", [L, DFF, D])
    outT = nc.dram_tensor("outT", [D, S_], F32, kind="ExternalOutput").ap()

    HT = dscr("HT", [D, NT], F32)
    AT = dscr("AT", [D, NTP], BF16)
    QT = dscr("QT", [8, 96, NT], BF16)
    KT = dscr("KT", [8, 96, NT], BF16)
    VA = dscr("VA", [8, NT, 65], BF16)
    GQ = dscr("GQ", [512, NT], BF16)
    GK = dscr("GK", [512, NT], BF16)
    GG = dscr("GG", [512, NT], BF16)
    RQ = dscr("RQ", [512, NT], BF16)
    RK = dscr("RK", [512, NT], BF16)
    RG = dscr("RG", [512, NT], BF16)
    GV = dscr("GV", [NT, 512], BF16)
    RV = dscr("RV", [NT, 512], BF16)
    SPL = dscr("SPL", [2, NT, 512], F32)
    GT = dscr("GT", [3, D, NT], BF16)
    YM = dscr("YM", [512, NT], BF16)
    YG = dscr("YG", [512, NT], BF16)
    YR = dscr("YR", [512, NT], BF16)
    OF = dscr("OF", [512, NT], F32)
    of_res = {}

    PSALL = nc.alloc_psum_tensor("psall", [128, 7 * 512], F32)
    PS = [PSALL[:, i * 512:(i + 1) * 512] for i in range(7)]
    PR = [Res(f"ps{i}", True) for i in range(7)]
    PSB = nc.alloc_psum_tensor("psb", [128, 1024], BF16)
    _pb = Res("psb", True)
    PBR = [_pb, _pb]

    def mm(out, lhsT, rhs, start, stop, reads, wres):
        sc.op("pe", lambda e: e.matmul(out, lhsT=lhsT, rhs=rhs, start=start, stop=stop), reads=reads, writes=[wres], acc=True)

    def act(out, in_, func, reads, wres, bias=None, scale=None, eng="act"):
        kw = {}
        if bias is not None:
            kw["bias"] = bias
        if scale is not None:
            kw["scale"] = scale
        sc.op("act", lambda e: e.activation(out=out, in_=in_, func=func, **kw), reads=reads, writes=[wres])

    def cp(eng, out, in_, reads, wres):
        if eng == "act":
            sc.op("act", lambda e: e.copy(out=out, in_=in_), reads=reads, writes=[wres])
        else:
            sc.op(eng, lambda e: e.tensor_copy(out=out, in_=in_), reads=reads, writes=[wres])

    def tt(eng, out, in0, in1, op, reads, wres):
        sc.op(eng, lambda e: e.tensor_tensor(out=out, in0=in0, in1=in1, op=op), reads=reads, writes=[wres])

    def stt(eng, out, in0, scalar, in1, op0, op1, reads, wres):
        sc.op(eng, lambda e: e.scalar_tensor_tensor(out=out, in0=in0, scalar=scalar, in1=in1, op0=op0, op1=op1),
              reads=reads, writes=[wres])

    def ts(eng, out, in0, s1, s2, op0, op1, reads, wres):
        sc.op(eng, lambda e: e.tensor_scalar(out=out, in0=in0, scalar1=s1, scalar2=s2, op0=op0, op1=op1),
              reads=reads, writes=[wres])

    def ts1(eng, out, in0, s1, op0, reads, wres):
        sc.op(eng, lambda e: e.tensor_single_scalar(out=out, in_=in0, scalar=s1, op=op0), reads=reads, writes=[wres])

    def memset(eng, ap, val, wres):
        sc.op(eng, lambda e: e.memset(ap, val), writes=[wres])

    def load(out, in_, wres, reads=()):
        sc.dma("sp", out, in_, reads=reads, writes=[wres])

    def store(out, in_, rres, writes=()):
        sc.dma("pool", out, in_, reads=[rres], writes=writes)

    class Rot:
        def __init__(self, items):
            self.items = items
            self.i = 0

        def next(self):
            it = self.items[self.i % len(self.items)]
            self.i += 1
            return it

    def rot(n, shape, dt, name):
        return Rot([A.tile(shape, dt, name) for _ in range(n)])

    cast_i = [0]

    def load_w(dst, dst_res, src, kc, n, stage):
        rows = src.shape[0] // kc
        for k in range(kc):
            for n0 in range(0, n, 2048):
                w = min(2048, n - n0)
                st, sr = stage.next()
                load(st[:rows, :w], src[k * rows:(k + 1) * rows, n0:n0 + w], sr)
                eng = ("dve", "pool", "act")[cast_i[0] % 3]
                cast_i[0] += 1
                cp(eng, dst[:rows, k, n0:n0 + w], st[:rows, :w], [sr], dst_res)

    cst, cst_r = A.tile([128, NCST], F32, "cst")
    load(cst[:], cst_d, cst_r)
    cbf, cbf_r = A.tile([128, NCST], BF16, "cbf")
    cp("dve", cbf[:], cst[:], [cst_r], cbf_r)
    epst, eps_r = A.tile([128, 1], F32, "eps")
    memset("dve", epst[:], EPS, eps_r)
    onec, onec_r = A.tile([128, 1], F32, "onec")
    memset("dve", onec[:], 1.0, onec_r)
    ppt, pp_r = A.tile([128, NPP], F32, "pp")
    cvt, cv_r = A.tile([128, 16], F32, "cvec")
    cond2, cond_r = A.tile([128, 8, 2], F32, "cond2")
    modt, mod_r = A.tile([128, 48, 2], F32, "mod")
    g1t, g1_r = A.tile([128, 8, 2], F32, "g1")
    g2t, g2_r = A.tile([128, 8, 2], F32, "g2")
    lgt, lg_r = A.tile([128, 8], F32, "lg")
    nlgt, nlg_r = A.tile([128, 8], F32, "nlg")
    A.base = A.off

    def C_(name, rows=128, w=None, bf=True):
        o = CS[name]
        w = w if w is not None else (128 if name not in ("p96", "esel", "sel65") else (96 if name != "sel65" else 64))
        t = cbf if bf else cst
        return t[0:rows, o:o + w]

    ones_bf = C_("ones")
    ident_bf = C_("ident")
    CRES = [cst_r, cbf_r]

    def ppc(name, col, rows=128):
        return ppt[0:rows, PP[name] + col:PP[name] + col + 1]

    def partnorm(src_list, srcres, rows, nfeat, T, sqt, sq_r, ss_bank, lnt, ln_r, rst, rs_r):
        n = len(src_list)
        for i, s in enumerate(src_list):
            act(sqt[0:rows, i, :T], s, AF.Square, srcres, sq_r)
        for i in range(n):
            mm(PS[ss_bank][0:rows, :T], cbf[0:rows, CS["ones"]:CS["ones"] + rows], sqt[0:rows, i, :T], i == 0, i == n - 1,
               [sq_r, cbf_r], PR[ss_bank])
        act(lnt[0:rows, :T], PS[ss_bank][0:rows, :T], AF.Ln, [PR[ss_bank], eps_r], ln_r, bias=epst[0:rows, 0:1], scale=1.0 / nfeat)
        act(rst[0:rows, :T], lnt[0:rows, :T], AF.Exp, [ln_r], rs_r, scale=-0.5)

    sc.dma("sp", HT[:, 0:CTX], ctxT, reads=[], writes=[])
    for i in range(0, S_, 2048):
        w = min(2048, S_ - i)
        sc.dma("sp", HT[:, CTX + i:CTX + i + w], xT[:, i:i + w], reads=[], writes=[])
    load(cvt[:], cvec_d, cv_r)
    act(cond2[:, :, 0], cvt[:, 0:8], AF.Silu, [cv_r], cond_r)
    act(cond2[:, :, 1], cvt[:, 8:16], AF.Silu, [cv_r], cond_r)
    sc.barrier()

    def norm_phase(l, which):
        A.reset()
        hts = rot(2, [128, 8, 512], F32, "h")
        sqs = rot(2, [128, 8, 512], BF16, "sq")
        lns = rot(2, [128, 512], F32, "ln")
        rss = rot(2, [128, 512], F32, "rs")
        tms = rot(2, [128, 8, 512], F32, "tm")
        abs_ = rot(2, [128, 8, 514], BF16, "a")
        for ab_, abr_ in abs_.items:
            memset("pool", ab_[:, :, 0:1], 0.0, abr_)
        gt = g1t if which == 1 else g2t
        gr = g1_r if which == 1 else g2_r
        shv = 0 if which == 1 else 3
        HTv = HT.rearrange("(c p) t -> p c t", p=128)
        ATv = AT.rearrange("(c p) t -> p c t", p=128)
        for ti, (t0, T, isc, ac0) in enumerate(tiles):
            if isc and l == L - 1 and which == 2:
                continue
            col = 1 if isc else 0
            h, hr = hts.next()
            load(h[:, :, :T], HTv[:, :, t0:t0 + T], hr)
            sq, sqr = sqs.next()
            lnv, lnr = lns.next()
            rs, rsr = rss.next()
            bank = ti % 2
            partnorm([h[:, c, :T] for c in range(8)], [hr], 128, D, T, sq, sqr, bank, lnv, lnr, rs, rsr)
            tm, tmr = tms.next()
            tt("dve", tm[:, :, :T], h[:, :, :T], rs[:, :T].unsqueeze(1).to_broadcast([128, 8, T]), ALU.mult, [hr, rsr], tmr)
            ab, abr = abs_.next()
            first = (t0 == 0) or (t0 == CTX)
            lastt = (t0 + T == CTX) or (t0 + T == NT)
            for c in range(8):
                if c % 3 == 0:
                    act(ab[:, c, 1:1 + T], tm[:, c, :T], AF.Identity, [tmr, gr, mod_r], abr,
                        bias=modt[:, shv * 8 + c, col:col + 1], scale=gt[:, c, col:col + 1])
                else:
                    ts("dve" if c % 3 == 1 else "pool", ab[:, c, 1:1 + T], tm[:, c, :T], gt[:, c, col:col + 1],
                       modt[:, shv * 8 + c, col:col + 1], ALU.mult, ALU.add, [tmr, gr, mod_r], abr)
            if lastt:
                memset("pool", ab[:, :, T + 1:T + 2], 0.0, abr)
            lo = 0 if first else 1
            hi = T + 2 if lastt else T + 1
            store(ATv[:, :, ac0 - 1 + lo:ac0 - 1 + hi], ab[:, :, lo:hi], abr)
        sc.barrier()

    def mod_phase(l):
        A.reset()
        load(ppt[:], pp_d[l], pp_r)
        ws = rot(2, [128, 8, 1024], F32, "wada")
        wv = w_ada_d[l].rearrange("(c p) n -> p c n", p=128)
        for g in range(6):
            w, wr = ws.next()
            for k in range(0, 8, 2):
                load(w[:, k:k + 2, :], wv[:, k:k + 2, g * 1024:(g + 1) * 1024], wr)
            for nn in range(8):
                j = g * 8 + nn
                for k in range(8):
                    mm(PS[0][:, 2 * j:2 * j + 2], w[:, k, nn * 128:(nn + 1) * 128], cond2[:, k, :], k == 0, k == 7,
                       [wr, cond_r], PR[0])
        psv = PS[0][:, 0:96].rearrange("p (j t) -> p j t", t=2)
        for col in range(2):
            tt("dve", modt[:, :, col], psv[:, :, col], ppt[:, PP["bada"]:PP["bada"] + 48], ALU.add, [PR[0], pp_r], mod_r)
        for col in range(2):
            stt("dve", g1t[:, :, col], modt[:, 8:16, col], 1.0, ppt[:, PP["n1w"]:PP["n1w"] + 8], ALU.add, ALU.mult,
                [mod_r, pp_r], g1_r)
            stt("dve", g2t[:, :, col], modt[:, 32:40, col], 1.0, ppt[:, PP["n2w"]:PP["n2w"] + 8], ALU.add, ALU.mult,
                [mod_r, pp_r], g2_r)
        act(nlgt[:], ppt[:, PP["retdec"]:PP["retdec"] + 8], AF.Exp, [pp_r], nlg_r)
        ts1("dve", lgt[:], nlgt[:], -1.0, ALU.mult, [nlg_r], lg_r)
        sc.barrier()

    def qk_finish(ps_bank, T, normcol, tabC, tabS, tab_r, out_ap, tmp):
        sq, sqr, lnv, lnr, rs, rsr, xn, xnr, t1, t1r, t2, t2r, ob, obr, ssb, swb = tmp
        partnorm([PS[ps_bank][0:96, :T]], [PR[ps_bank]], 96, 96, T, sq, sqr, ssb, lnv, lnr, rs, rsr)
        stt("dve", xn[0:96, :T], PS[ps_bank][0:96, :T], normcol, rs[0:96, :T], ALU.mult, ALU.mult, [PR[ps_bank], rsr, pp_r], xnr)
        mm(PS[swb][0:96, :T], cbf[0:96, CS["p96"]:CS["p96"] + 96], xn[0:96, :T], True, True, [xnr, cbf_r], PR[swb])
        tt("pool", t1[0:96, :T], xn[0:96, :T], tabC, ALU.mult, [xnr, tab_r], t1r)
        tt("dve", t2[0:96, :T], PS[swb][0:96, :T], tabS, ALU.mult, [PR[swb], tab_r], t2r)
        tt("dve", ob[0:96, :T], t1[0:96, :T], t2[0:96, :T], ALU.add, [t1r, t2r], obr)
        store(out_ap, ob[0:96, :T], obr)

    def inproj_mla(l):
        A.reset()
        stage = rot(2, [128, 2048], F32, "stg")
        win, win_r = A.tile([128, 8, 416], BF16, "winA")
        load_w(win, win_r, w_in_d[l][:, 0:416], 8, 416, stage)
        wqb, wqb_r = A.tile([128, 2, 768], BF16, "wqb")
        load_w(wqb, wqb_r, w_qb_d[l], 2, 768, stage)
        wkf, wkf_r = A.tile([128, 1024], F32, "wkf")
        load(wkf[:], w_kvb_d[l], wkf_r)
        wkn, wkn_r = A.tile([128, 8, 96], BF16, "wkn")
        memset("dve", wkn[:], 0.0, wkn_r)
        wkfv = wkf[:].rearrange("p (h e) -> p h e", e=128)
        cp("dve", wkn[:, :, 0:64], wkfv[:, :, 0:64], [wkf_r], wkn_r)
        wkv, wkv_r = A.tile([128, 8, 64], BF16, "wkv")
        cp("dve", wkv[:], wkfv[:, :, 64:128], [wkf_r], wkv_r)
        ats = rot(2, [128, 8, 512], BF16, "a")
        cqs = rot(2, [128, 2, 512], F32, "cq")
        sq2 = rot(2, [128, 2, 512], BF16, "sq2")
        lns = rot(2, [128, 512], F32, "ln")
        rss = rot(2, [128, 512], F32, "rs")
        cqn = rot(2, [128, 2, 512], BF16, "cqn")
        ckv = rot(2, [128, 512], F32, "ckv")
        ckn = rot(2, [128, 512], BF16, "ckn")
        krs = rot(2, [32, 512], BF16, "kr")
        tCs = rot(2, [96, 512], F32, "tC")
        tSs = rot(2, [96, 512], F32, "tS")
        sqh = rot(3, [96, 1, 512], BF16, "sqh")
        lnh = rot(3, [96, 512], F32, "lnh")
        rsh = rot(3, [96, 512], F32, "rsh")
        xnh = rot(3, [96, 512], BF16, "xnh")
        t1h = rot(3, [96, 512], F32, "t1h")
        t2h = rot(3, [96, 512], F32, "t2h")
        obh = rot(3, [96, 512], BF16, "obh")
        vas = rot(2, [128, 8, 65], BF16, "va")
        for v, vr in vas.items:
            memset("pool", v[:], 1.0, vr)
        ATv = AT.rearrange("(c p) t -> p c t", p=128)
        VAv = VA.rearrange("h t e -> t h e")
        hcount = [0]

        def tmpset():
            i = hcount[0]
            hcount[0] += 1
            sq, sqr = sqh.next(); lnv, lnr = lnh.next(); rs, rsr = rsh.next(); xn, xnr = xnh.next()
            t1, t1r = t1h.next(); t2, t2r = t2h.next(); ob, obr = obh.next()
            return (sq, sqr, lnv, lnr, rs, rsr, xn, xnr, t1, t1r, t2, t2r, ob, obr, 3 + (i % 2), 5 + (i % 2))

        for ti, (t0, T, isc, ac0) in enumerate(tiles):
            a, ar = ats.next()
            load(a[:, :, :T], ATv[:, :, ac0:ac0 + T], ar)
            tC, tCr = tCs.next()
            tS, tSr = tSs.next()
            load(tC[:, :T], mlaC_d[:, t0:t0 + T], tCr)
            load(tS[:, :T], mlaS_d[:, t0:t0 + T], tCr)
            cq, cqr = cqs.next()
            for c2 in range(2):
                for k in range(8):
                    mm(PS[c2][:, :T], win[:, k, c2 * 128:(c2 + 1) * 128], a[:, k, :T], k == 0, k == 7, [win_r, ar], PR[c2])
                cp("act", cq[:, c2, :T], PS[c2][:, :T], [PR[c2]], cqr)
            sq, sqr = sq2.next(); lnv, lnr = lns.next(); rs, rsr = rss.next()
            partnorm([cq[:, 0, :T], cq[:, 1, :T]], [cqr], 128, 256, T, sq, sqr, 2, lnv, lnr, rs, rsr)
            cn, cnr = cqn.next()
            for c2 in range(2):
                stt("dve", cn[:, c2, :T], cq[:, c2, :T], ppc("qna", c2), rs[:, :T], ALU.mult, ALU.mult, [cqr, rsr, pp_r], cnr)
            for k in range(8):
                mm(PS[0][:, :T], win[:, k, 256:384], a[:, k, :T], k == 0, k == 7, [win_r, ar], PR[0])
            kv, kvr = ckv.next()
            cp("act", kv[:, :T], PS[0][:, :T], [PR[0]], kvr)
            sq, sqr = sq2.next(); lnv, lnr = lns.next(); rs, rsr = rss.next()
            partnorm([kv[:, :T]], [kvr], 128, 128, T, sq, sqr, 2, lnv, lnr, rs, rsr)
            kn, knr = ckn.next()
            stt("dve", kn[:, :T], kv[:, :T], ppc("kvna", 0), rs[:, :T], ALU.mult, ALU.mult, [kvr, rsr, pp_r], knr)
            for k in range(8):
                mm(PS[1][0:32, :T], win[:, k, 384:416], a[:, k, :T], k == 0, k == 7, [win_r, ar], PR[1])
            kr, krr = krs.next()
            cp("act", kr[0:32, :T], PS[1][0:32, :T], [PR[1]], krr)
            for h in range(8):
                b = h % 3
                for c2 in range(2):
                    mm(PS[b][0:96, :T], wqb[:, c2, h * 96:(h + 1) * 96], cn[:, c2, :T], c2 == 0, c2 == 1, [wqb_r, cnr], PR[b])
                qk_finish(b, T, ppc("qn", 0, 96), tC[0:96, :T], tS[0:96, :T], tCr, QT[h][:, t0:t0 + T], tmpset())
            for h in range(8):
                b = h % 3
                mm(PS[b][0:96, :T], wkn[:, h, :], kn[:, :T], True, False, [wkn_r, knr], PR[b])
                mm(PS[b][0:96, :T], cbf[0:32, CS["esel"]:CS["esel"] + 96], kr[0:32, :T], False, True, [cbf_r, krr], PR[b])
                qk_finish(b, T, ppc("kn", 0, 96), tC[0:96, :T], tS[0:96, :T], tCr, KT[h][:, t0:t0 + T], tmpset())
            for j in range(T // 128):
                b = j % 2
                mm(PS[b][:, 0:512], kn[:, j * 128:(j + 1) * 128], wkv[:].rearrange("p h e -> p (h e)"), True, True, [knr, wkv_r], PR[b])
                va, var_ = vas.next()
                cp("act", va[:, :, 0:64], PS[b][:, 0:512].rearrange("p (h e) -> p h e", e=64), [PR[b]], var_)
                store(VAv[t0 + j * 128:t0 + (j + 1) * 128], va[:], var_)
        sc.barrier()

    def inproj_lin(l, kind):
        A.reset()
        stage = rot(2, [128, 2048], F32, "stg")
        c0 = C_GQ if kind == 0 else C_RQ
        ncols = 2080 if kind == 0 else 2048
        win, win_r = A.tile([128, 8, ncols], BF16, "winB")
        load_w(win, win_r, w_in_d[l][:, c0:c0 + ncols], 8, ncols, stage)
        if kind == 0:
            w2, w2_r = A.tile([16, 2, 512], BF16, "w2")
            for d in range(2):
                st, sr = stage.next()
                load(st[0:16, 0:512], w_gk2_d[l, d], sr)
                cp("dve", w2[0:16, d, :], st[0:16, 0:512], [sr], w2_r)
        ats = rot(2, [128, 8, 512], BF16, "a")
        stq = rot(2, [128, 4, 512], BF16, "stq")
        stv = rot(2, [128, 4, 512], BF16, "stv")
        if kind == 0:
            rfs = rot(2, [16, 512], BF16, "rf")
            rbs = rot(2, [16, 512], BF16, "rb")
            xbs = rot(2, [128, 512], F32, "xb")
            ees = rot(2, [128, 512], F32, "ee")
            sps = rot(2, [128, 4, 512], F32, "sp")
        else:
            tCs = rot(2, [128, 512], F32, "tC")
            tSs = rot(2, [128, 512], F32, "tS")
            xbf = rot(3, [128, 512], BF16, "xbf")
            t1s = rot(3, [128, 512], F32, "t1")
            t2s = rot(3, [128, 512], F32, "t2")
        ATv = AT.rearrange("(c p) t -> p c t", p=128)
        Qd, Kd, Gd, Vd = (GQ, GK, GG, GV) if kind == 0 else (RQ, RK, RG, RV)
        pbank = [0]

        def nb():
            pbank[0] += 1
            return pbank[0] % 4

        cpi = [0]
        for ti, (t0, T, isc, ac0) in enumerate(tiles):
            a, ar = ats.next()
            load(a[:, :, :T], ATv[:, :, ac0:ac0 + T], ar)
            if kind == 1 and 'notab' not in DBG:
                tC, tCr = tCs.next()
                tS, tSr = tSs.next()
                load(tC[:, :T], retC_d[:, t0:t0 + T], tCr)
                load(tS[:, :T], retS_d[:, t0:t0 + T], tSr)
            for wi, (dst, coff) in enumerate(((Qd, 0), (Kd, 512), (Gd, 1536))):
                st, sr = stq.next()
                for h in range(4):
                    b = nb()
                    for k in range(8):
                        mm(PS[b][:, :T], win[:, k, coff + h * 128:coff + (h + 1) * 128], a[:, k, :T], k == 0, k == 7, [win_r, ar], PR[b])
                    if wi == 2:
                        act(st[:, h, :T], PS[b][:, :T], AF.Silu, [PR[b]], sr)
                    elif kind == 0 or 'norope' in DBG:
                        cpi[0] += 1
                        cp("act" if cpi[0] % 2 else "dve", st[:, h, :T], PS[b][:, :T], [PR[b]], sr)
                    else:
                        xb_, xbr = xbf.next()
                        cp("act", xb_[:, :T], PS[b][:, :T], [PR[b]], xbr)
                        b2 = 4 + (h % 2)
                        if 'nomm' in DBG:
                            b2 = b
                        else:
                            mm(PS[b2][:, :T], C_("p128"), xb_[:, :T], True, True, [xbr, cbf_r], PR[b2])
                        t1, t1r = t1s.next()
                        t2, t2r = t2s.next()
                        tt("dve", t1[:, :T], PS[b][:, :T], tC[:, :T], ALU.mult, [PR[b], tCr], t1r)
                        tt("dve", t2[:, :T], PS[b2][:, :T], tS[:, :T], ALU.mult, [PR[b2], tSr], t2r)
                        tt("dve" if 'dveadd' in DBG else "pool", st[:, h, :T], t1[:, :T], t2[:, :T], ALU.add, [t1r, t2r], sr)
                store(dst.rearrange("(h p) t -> p h t", p=128)[:, :, t0:t0 + T], st[:, :, :T], sr)
            sv, svr = stv.next()
            nbk = T // 128
            for j in range(nbk):
                b = nb()
                for k in range(8):
                    mm(PS[b][:, 0:512], a[:, k, j * 128:(j + 1) * 128], win[:, k, 1024:1536], k == 0, k == 7, [win_r, ar], PR[b])
                cpi[0] += 1
                cp("act" if cpi[0] % 2 else "dve", sv[:, j, :], PS[b][:, 0:512], [PR[b]], svr)
            store(Vd[t0:t0 + T, :].rearrange("(j p) n -> p j n", p=128), sv[:, 0:nbk, :], svr)
            if kind == 0:
                rf, rfr = rfs.next()
                rb, rbr = rbs.next()
                for (rt, rr, cc) in ((rf, rfr, 2048), (rb, rbr, 2064)):
                    b = nb()
                    for k in range(8):
                        mm(PS[b][0:16, :T], win[:, k, cc:cc + 16], a[:, k, :T], k == 0, k == 7, [win_r, ar], PR[b])
                    cp("act", rt[0:16, :T], PS[b][0:16, :T], [PR[b]], rr)
                for d, (rt, rr) in enumerate(((rf, rfr), (rb, rbr))):
                    sp_, spr = sps.next()
                    for j in range(nbk):
                        b = nb()
                        mm(PS[b][:, 0:512], rt[0:16, j * 128:(j + 1) * 128], w2[0:16, d, :], True, True, [rr, w2_r], PR[b])
                        xb_, xbr = xbs.next()
                        tt("dve", xb_[:], PS[b][:, 0:512], ppt[:, PP["bgk"] + d * 512:PP["bgk"] + (d + 1) * 512], ALU.add, [PR[b], pp_r], xbr)
                        ee, eer = ees.next()
                        act(ee[:], xb_[:], AF.Exp, [xbr], eer, scale=-1.0)
                        act(sp_[:, j, :], ee[:], AF.Ln, [eer], spr, bias=1.0)
                    store(SPL[d, t0:t0 + T, :].rearrange("(j p) n -> p j n", p=128), sp_[:, 0:nbk, :], spr)
        sc.barrier()

    def inproj_gates(l):
        A.reset()
        stage = rot(2, [128, 2048], F32, "stg")
        win, win_r = A.tile([128, 8, 3072], BF16, "winD")
        load_w(win, win_r, w_in_d[l][:, C_G0:C_G0 + 3072], 8, 3072, stage)
        ats = rot(2, [128, 8, 512], BF16, "a")
        sts = rot(3, [128, 8, 512], BF16, "stg8")
        ATv = AT.rearrange("(c p) t -> p c t", p=128)
        bi = 0
        for ti, (t0, T, isc, ac0) in enumerate(tiles):
            if isc and l == L - 1:
                continue
            a, ar = ats.next()
            load(a[:, :, :T], ATv[:, :, ac0:ac0 + T], ar)
            for n in range(3):
                st, sr = sts.next()
                for c in range(8):
                    b = bi % 4
                    bi += 1
                    for k in range(8):
                        mm(PS[b][:, :T], win[:, k, n * 1024 + c * 128:n * 1024 + (c + 1) * 128], a[:, k, :T], k == 0, k == 7, [win_r, ar], PR[b])
                    act(st[:, c, :T], PS[b][:, :T], AF.Sigmoid, [PR[b], pp_r], sr, bias=ppc("bgate", n * 8 + c))
                store(GT[n].rearrange("(c p) t -> p c t", p=128)[:, :, t0:t0 + T], st[:, :, :T], sr)
        sc.barrier()

    def attention(l):
        A.reset()
        kts = rot(2, [96, NT], BF16, "kt")
        vas = rot(2, [128, NKT, 65], BF16, "va")
        qts = rot(3, [96, 512], BF16, "qt")
        pts = rot(3, [128, 2, 512], BF16, "pt")
        osb = rot(2, [65, 512], F32, "osb")
        rvs = rot(2, [64, 512], F32, "rinv")
        ybs = rot(2, [64, 512], BF16, "yb")
        scale = 96.0 ** -0.5
        qi = 0
        for h in range(8):
            kt, ktr = kts.next()
            va, var_ = vas.next()
            load(kt[:, :], KT[h], ktr)
            VAh = VA[h].rearrange("(k p) e -> p k e", p=128)
            for k0 in range(0, NKT, 8):
                k1 = min(NKT, k0 + 8)
                load(va[:, k0:k1, :], VAh[:, k0:k1, :], var_)
            for ti, (t0, T, isc, ac0) in enumerate(tiles):
                if isc and l == L - 1:
                    continue
                nk = CTX // 128 if isc else NKT
                qt, qtr = qts.next()
                load(qt[:, :T], QT[h][:, t0:t0 + T], qtr)
                ob = 4 + (qi % 2)
                qi += 1
                npair = nk // 2

                def qk2(pi):
                    for u in range(2):
                        b = 2 * (pi % 2) + u
                        i = 2 * pi + u
                        mm(PS[b][:, :T], kt[:, i * 128:(i + 1) * 128], qt[:, :T], True, True, [ktr, qtr], PR[b])

                qk2(0)
                for pi in range(npair):
                    if pi + 1 < npair:
                        qk2(pi + 1)
                    pp = pi % 2
                    pt, ptr = pts.next()
                    s2 = PSALL[:, pp * 1024:(pp + 1) * 1024].rearrange("p (a t) -> p a t", t=512)
                    act(pt[:, :, :T], s2[:, :, :T], AF.Exp, [PR[2 * pp], PR[2 * pp + 1]], ptr, scale=scale)
                    for u in range(2):
                        i = 2 * pi + u
                        mm(PS[ob][0:65, :T], va[:, i, :], pt[:, u, :T], i == 0, i == nk - 1, [var_, ptr], PR[ob])
                o, orr = osb.next()
                cp("dve", o[0:65, :T], PS[ob][0:65, :T], [PR[ob]], orr)
                mm(PS[6][0:64, :T], cst[0:65, CS["sel65"]:CS["sel65"] + 64], o[0:65, :T], True, True, [cst_r, orr], PR[6])
                rv, rvr = rvs.next()
                sc.op("dve", lambda e, rv=rv, T=T: e.reciprocal(out=rv[0:64, :T], in_=PS[6][0:64, :T]), reads=[PR[6]], writes=[rvr])
                yb, ybr = ybs.next()
                tt("dve", yb[0:64, :T], o[0:64, :T], rv[0:64, :T], ALU.mult, [orr, rvr], ybr)
                store(YM[h * 64:(h + 1) * 64, t0:t0 + T], yb[0:64, :T], ybr)
        sc.barrier()

    def sweep(l, kind):
        A.reset()
        Qd, Kd, Gd, Vd, Yd = (GQ, GK, GG, GV, YG) if kind == 0 else (RQ, RK, RG, RV, YR)
        qs = 128.0 ** -0.5 if kind == 0 else 1.0
        ks = 1.0 if kind == 0 else 128.0 ** -0.5
        NH = 4
        St = [A.tile([128, 128], F32, "S") for _ in range(NH)]
        Sb = [A.tile([128, 128], BF16, "Sb") for _ in range(NH)]
        qts = [rot(2, [128, 512], BF16, "q") for _ in range(NH)]
        kts_ = [rot(2, [128, 512], BF16, "k") for _ in range(NH)]
        vts = [rot(2, [128, 4, 128], BF16, "v") for _ in range(NH)]
        qds = [rot(2, [128, 512], BF16, "qd") for _ in range(NH)]
        kis = [rot(2, [128, 512], BF16, "ki") for _ in range(NH)]
        if kind == 0:
            spt = [rot(2, [128, 4, 128], F32, "spt") for _ in range(NH)]
            Es = [rot(2, [128, 512], F32, "E") for _ in range(NH)]
            Eis = [rot(2, [128, 512], F32, "Ei") for _ in range(NH)]
        else:
            Etab = [[A.tile([128, 128], F32, "Et") for _ in range(2)] for _ in range(NH)]
            Eitab = [[A.tile([128, 128], F32, "Eit") for _ in range(2)] for _ in range(NH)]
            for h in range(NH):
                for d in range(2):
                    idx = cst[:, CS["idxF" if d == 0 else "idxB"]:CS["idxF" if d == 0 else "idxB"] + 128]
                    act(Etab[h][d][0][:], idx, AF.Exp, [cst_r, lg_r], Etab[h][d][1], scale=lgt[:, d * 4 + h:d * 4 + h + 1])
                    act(Eitab[h][d][0][:], idx, AF.Exp, [cst_r, nlg_r], Eitab[h][d][1], scale=nlgt[:, d * 4 + h:d * 4 + h + 1])
        ktas = [rot(2, [128, 512], BF16, "kta") for _ in range(NH)]
        stmp = [A.tile([128, 128], F32, "stmp") for _ in range(NH)]
        ams = rot(4, [128, 128], BF16, "am")
        ofs = [rot(2, [128, 512], F32, "of") for _ in range(NH)]
        sgs = [rot(2, [128, 512], BF16, "sg") for _ in range(NH)]
        ots = rot(8, [128, 512], F32, "ot")
        sqs = rot(2, [128, 1, 512], BF16, "sq")
        lns = rot(2, [128, 512], F32, "ln")
        rss = rot(2, [128, 512], F32, "rs")
        y1s = rot(2, [128, 512], F32, "y1")
        ybs = rot(2, [128, 512], BF16, "yb")
        cnt = [0]
        for d in range(2):
            for h in range(NH):
                memset("dve", St[h][0][:], 0.0, St[h][1])
                memset("pool", Sb[h][0][:], 0.0, Sb[h][1])
            order = [0] + (list(range(1, len(tiles))) if d == 0 else list(range(len(tiles) - 1, 0, -1)))
            mask = cst[:, CS["maskF" if d == 0 else "maskB"]:CS["maskF" if d == 0 else "maskB"] + 128]
            um = cst[:, CS["uF" if d == 0 else "uB"]:CS["uF" if d == 0 else "uB"] + 128]
            for ti in order:
                t0, T, isc, ac0 = tiles[ti]
                skip_out = isc and l == L - 1
                nbk = T // 128
                cur = []
                for h in range(NH):
                    q, qr = qts[h].next(); k, kr = kts_[h].next(); v, vr = vts[h].next()
                    load(q[:, :T], Qd[h * 128:(h + 1) * 128, t0:t0 + T], qr)
                    load(k[:, :T], Kd[h * 128:(h + 1) * 128, t0:t0 + T], kr)
                    load(v[:, 0:nbk, :], Vd[t0:t0 + T, h * 128:(h + 1) * 128].rearrange("(j p) n -> p j n", p=128), vr)
                    qd, qdr = qds[h].next(); ki, kir = kis[h].next()
                    if kind == 0:
                        sp_, spr = spt[h].next()
                        load(sp_[:, 0:nbk, :], SPL[d, t0:t0 + T, h * 128:(h + 1) * 128].rearrange("(j p) n -> p j n", p=128), spr)
                        cb = 0
                        for j in range(nbk):
                            mm(PS[cb][:, j * 128:(j + 1) * 128], sp_[:, j, :], um, True, True, [spr, cst_r], PR[cb])
                        E, Er = Es[h].next(); Ei, Eir = Eis[h].next()
                        act(E[:, :T], PS[cb][:, :T], AF.Exp, [PR[cb]], Er)
                        act(Ei[:, :T], PS[cb][:, :T], AF.Exp, [PR[cb]], Eir, scale=-1.0)
                        stt("dve", qd[:, :T], q[:, :T], qs, E[:, :T], ALU.mult, ALU.mult, [qr, Er], qdr)
                        stt("dve", ki[:, :T], k[:, :T], ks, Ei[:, :T], ALU.mult, ALU.mult, [kr, Eir], kir)
                        gsrc = (E, Er)
                    else:
                        E, Er = Etab[h][d]; Ei, Eir = Eitab[h][d]
                        stt("dve", qd[:, :T].rearrange("p (j n) -> p j n", n=128), q[:, :T].rearrange("p (j n) -> p j n", n=128), qs,
                            E[:].unsqueeze(1).to_broadcast([128, nbk, 128]), ALU.mult, ALU.mult, [qr, Er], qdr)
                        stt("dve", ki[:, :T].rearrange("p (j n) -> p j n", n=128), k[:, :T].rearrange("p (j n) -> p j n", n=128), ks,
                            Ei[:].unsqueeze(1).to_broadcast([128, nbk, 128]), ALU.mult, ALU.mult, [kr, Eir], kir)
                        gsrc = (E, Er)
                    for j in range(nbk):
                        sc.op("pe", lambda e, j=j, ki=ki: e.transpose(PSB[:, j * 128:(j + 1) * 128], ki[:, j * 128:(j + 1) * 128], ident_bf),
                              reads=[kir, cbf_r], writes=[PBR[0]])
                    kta, ktar = ktas[h].next()
                    cp("act", kta[:, :T], PSB[:, 0:T], [PBR[0]], ktar)
                    cur.append((q, qr, k, kr, v, vr, qd, qdr, ki, kir, gsrc, kta, ktar))
                ob = [None] * NH
                blocks = list(range(nbk)) if d == 0 else list(range(nbk - 1, -1, -1))
                for j in blocks:
                    c0 = j * 128
                    for h in range(NH):
                        q, qr, k, kr, v, vr, qd, qdr, ki, kir, (E, Er), kta, ktar = cur[h]
                        if kind == 0:
                            gcol = E[:, c0 + 127:c0 + 128] if d == 0 else E[:, c0:c0 + 1]
                        else:
                            gcol = E[:, 127:128] if d == 0 else E[:, 0:1]
                        pb = cnt[0] % 2
                        cnt[0] += 1
                        ab = 4 + pb
                        mm(PS[ab][:, 0:128], ki[:, c0:c0 + 128], qd[:, c0:c0 + 128], True, True, [kir, qdr], PR[ab])
                        am, amr = ams.next()
                        tt("dve", am[:], PS[ab][:, 0:128], mask, ALU.mult, [PR[ab], cst_r], amr)
                        obk = 2 + (h % 2)
                        mm(PS[obk][:, 0:128], v[:, j, :], am[:], True, False, [vr, amr], PR[obk])
                        mm(PS[obk][:, 0:128], Sb[h][0][:], qd[:, c0:c0 + 128], False, True, [Sb[h][1], qdr], PR[obk])
                        if not skip_out:
                            if ob[h] is None:
                                ob[h] = ots.next()
                            ot, otr = ob[h]
                            cp("act", ot[:, c0:c0 + 128], PS[obk][:, 0:128], [PR[obk]], otr)
                        ib = 1 if h % 2 == 0 else 6
                        mm(PS[ib][:, 0:128], kta[:, c0:c0 + 128], v[:, j, :], True, True, [ktar, vr], PR[ib])
                        tmp, tmpr = stmp[h]
                        tt("dve", tmp[:], St[h][0][:], PS[ib][:, 0:128], ALU.add, [St[h][1], PR[ib]], tmpr)
                        act(Sb[h][0][:], tmp[:], AF.Copy, [tmpr, Er], Sb[h][1], scale=gcol)
                        ts1("dve", St[h][0][:], tmp[:], gcol, ALU.mult, [tmpr, Er], St[h][1])
                if skip_out:
                    continue
                for h in range(NH):
                    ot, otr = ob[h]
                    key = (kind, h, ti)
                    if d == 0:
                        r_ = of_res.setdefault(key, Res(f"of{key}"))
                        store(OF[h * 128:(h + 1) * 128, t0:t0 + T], ot[:, :T], otr, writes=[r_])
                    else:
                        r_ = of_res[key]
                        of_, ofr = ofs[h].next()
                        load(of_[:, :T], OF[h * 128:(h + 1) * 128, t0:t0 + T], ofr, reads=[r_])
                        sg, sgr = sgs[h].next()
                        load(sg[:, :T], Gd[h * 128:(h + 1) * 128, t0:t0 + T], sgr)
                        tt("dve", ot[:, :T], ot[:, :T], of_[:, :T], ALU.add, [otr, ofr], otr)
                        sq, sqr = sqs.next(); lnv, lnr = lns.next(); rs, rsr = rss.next()
                        partnorm([ot[:, :T]], [otr], 128, 128, T, sq, sqr, 0, lnv, lnr, rs, rsr)
                        y1, y1r = y1s.next()
                        ncol = ppc("gon", 0) if kind == 0 else onec[:, 0:1]
                        stt("dve", y1[:, :T], ot[:, :T], ncol, rs[:, :T], ALU.mult, ALU.mult, [otr, rsr, pp_r, onec_r], y1r)
                        yb, ybr = ybs.next()
                        tt("pool", yb[:, :T], y1[:, :T], sg[:, :T], ALU.mult, [y1r, sgr], ybr)
                        store(Yd[h * 128:(h + 1) * 128, t0:t0 + T], yb[:, :T], ybr)
            sc.barrier()

    def merge_phase(l):
        A.reset()
        stage = rot(2, [128, 2048], F32, "stg")
        wbr, wbr_r = A.tile([128, 12, 1024], BF16, "wbr")
        load_w(wbr, wbr_r, w_br_d[l].rearrange("n k m -> (n k) m"), 12, 1024, stage)
        wo, wo_r = A.tile([128, 8, 1024], BF16, "wo")
        load_w(wo, wo_r, w_out_d[l], 8, 1024, stage)
        ys = [rot(2, [128, 4, 512], BF16, f"y{n}") for n in range(3)]
        gs = [rot(2, [128, 8, 512], BF16, f"g{n}") for n in range(3)]
        hts = rot(2, [128, 8, 512], F32, "h")
        m32 = rot(2, [128, 512], F32, "m32")
        tms = rot(3, [128, 512], F32, "tm")
        mbs = rot(2, [128, 8, 512], BF16, "mb")
        HTv = HT.rearrange("(c p) t -> p c t", p=128)
        bi = 0
        for ti, (t0, T, isc, ac0) in enumerate(tiles):
            if isc and l == L - 1:
                continue
            col = 1 if isc else 0
            yy = []
            gg = []
            for n, Yd in enumerate((YM, YG, YR)):
                y, yr = ys[n].next()
                load(y[:, :, :T], Yd.rearrange("(c p) t -> p c t", p=128)[:, :, t0:t0 + T], yr)
                g, gr = gs[n].next()
                load(g[:, :, :T], GT[n].rearrange("(c p) t -> p c t", p=128)[:, :, t0:t0 + T], gr)
                yy.append((y, yr))
                gg.append((g, gr))
            h, hr = hts.next()
            load(h[:, :, :T], HTv[:, :, t0:t0 + T], hr)
            mb, mbr = mbs.next()
            for c in range(8):
                m, mr = m32.next()
                for n in range(3):
                    b = bi % 3
                    bi += 1
                    for k in range(4):
                        mm(PS[b][:, :T], wbr[:, n * 4 + k, c * 128:(c + 1) * 128], yy[n][0][:, k, :T], k == 0, k == 3, [wbr_r, yy[n][1]], PR[b])
                    if n == 0:
                        tt("dve", m[:, :T], PS[b][:, :T], gg[0][0][:, c, :T], ALU.mult, [PR[b], gg[0][1]], mr)
                    else:
                        tm, tmr = tms.next()
                        tt("dve", tm[:, :T], PS[b][:, :T], gg[n][0][:, c, :T], ALU.mult, [PR[b], gg[n][1]], tmr)
                        if n == 1:
                            tt("pool", m[:, :T], m[:, :T], tm[:, :T], ALU.add, [mr, tmr], mr)
                        else:
                            tt("pool", mb[:, c, :T], m[:, :T], tm[:, :T], ALU.add, [mr, tmr], mbr)
            for c in range(8):
                b = 3 + (c % 2)
                for k in range(8):
                    mm(PS[b][:, :T], wo[:, k, c * 128:(c + 1) * 128], mb[:, k, :T], k == 0, k == 7, [wo_r, mbr], PR[b])
                stt("dve", h[:, c, :T], PS[b][:, :T], modt[:, 16 + c, col:col + 1], h[:, c, :T], ALU.mult, ALU.add, [PR[b], mod_r, hr], hr)
            store(HTv[:, :, t0:t0 + T], h[:, :, :T], hr)
        sc.barrier()

    def ffn_phase(l):
        A.reset()
        stage = rot(2, [128, 2048], F32, "stg")
        wfi, wfi_r = A.tile([128, 8, 2 * DFF], BF16, "wfi")
        wfo, wfo_r = A.tile([128, NJ, 1024], BF16, "wfo")
        load_w(wfi, wfi_r, w_fi_d[l], 8, 2 * DFF, stage)
        load_w(wfo, wfo_r, w_fo_d[l], NJ, 1024, stage)
        FT = 256
        ats = rot(2, [128, 8, FT + 2], BF16, "a2")
        hts = rot(2, [128, 8, FT], F32, "h")
        hid = rot(1, [128, NJ, FT], BF16, "hid")
        gsb = rot(2, [128, FT + 2], F32, "gsb")
        acc = rot(2, [128, FT], F32, "acc")
        gel = rot(2, [128, FT], F32, "gel")
        ATv = AT.rearrange("(c p) t -> p c t", p=128)
        HTv = HT.rearrange("(c p) t -> p c t", p=128)
        OTv = outT.rearrange("(c p) t -> p c t", p=128)
        last = (l == L - 1)
        nlat = S_ // FT
        for fi, (t0, T, isc, ac0) in enumerate(ftiles):
            if isc and last:
                continue
            col = 1 if isc else 0
            a, ar = ats.next()
            load(a[:, :, :], ATv[:, :, ac0 - 1:ac0 + T + 1], ar)
            first = (t0 == 0) or (t0 == CTX)
            lastt = (t0 + T == CTX) or (t0 + T == NT)
            if first:
                memset("pool", a[:, :, 0:1], 0.0, ar)
            if lastt:
                memset("pool", a[:, :, T + 1:T + 2], 0.0, ar)
            h, hr = hts.next()
            load(h[:, :, :], HTv[:, :, t0:t0 + T], hr)
            hd, hdr = hid.next()
            for j in range(NJ):
                gb = j % 2
                ub = 2 + (j % 2)
                for k in range(8):
                    mm(PS[gb][:, 0:T], wfi[:, k, j * 128:(j + 1) * 128], a[:, k, 1:T + 1], k == 0, k == 7, [wfi_r, ar], PR[gb])
                for k in range(8):
                    mm(PS[4][:, 0:2], wfi[:, k, j * 128:(j + 1) * 128], a[:, k, 0:T + 2:T + 1], k == 0, k == 7, [wfi_r, ar], PR[4])
                for k in range(8):
                    mm(PS[ub][:, 0:T], wfi[:, k, DFF + j * 128:DFF + (j + 1) * 128], a[:, k, 1:T + 1], k == 0, k == 7, [wfi_r, ar], PR[ub])
                g, gr = gsb.next()
                cp("act", g[:, 1:T + 1], PS[gb][:, 0:T], [PR[gb]], gr)
                cp("act", g[:, 0:T + 2:T + 1], PS[4][:, 0:2], [PR[4]], gr)
                ac, acr = acc.next()
                ts("dve", ac[:], g[:, 0:T], ppc("wdw", j), ppc("bdw", j), ALU.mult, ALU.add, [gr, pp_r], acr)
                stt("dve", ac[:], g[:, 1:T + 1], ppc("wdw", NJ + j), ac[:], ALU.mult, ALU.add, [gr, pp_r, acr], acr)
                stt("dve", ac[:], g[:, 2:T + 2], ppc("wdw", 2 * NJ + j), ac[:], ALU.mult, ALU.add, [gr, pp_r, acr], acr)
                ge, ger = gel.next()
                act(ge[:], ac[:], AF.Gelu_apprx_tanh, [acr], ger)
                tt("dve", hd[:, j, :], PS[ub][:, 0:T], ge[:], ALU.mult, [PR[ub], ger], hdr)
            for c in range(8):
                b = 5 + (c % 2)
                for j in range(NJ):
                    mm(PS[b][:, 0:T], wfo[:, j, c * 128:(c + 1) * 128], hd[:, j, :], j == 0, j == NJ - 1, [wfo_r, hdr], PR[b])
                stt("dve", h[:, c, :], PS[b][:, 0:T], modt[:, 40 + c, col:col + 1], h[:, c, :], ALU.mult, ALU.add, [PR[b], mod_r, hr], hr)
            if last:
                store(OTv[:, :, t0 - CTX:t0 - CTX + T], h[:, :, :], hr)
            else:
                store(HTv[:, :, t0:t0 + T], h[:, :, :], hr)
        sc.barrier()

    plist = []
    for l in range(L):
        plist += [(mod_phase, (l,)), (norm_phase, (l, 1)), (inproj_mla, (l,)), (inproj_lin, (l, 0)), (inproj_lin, (l, 1)),
                  (inproj_gates, (l,)), (attention, (l,)), (sweep, (l, 0)), (sweep, (l, 1)), (merge_phase, (l,)),
                  (norm_phase, (l, 2)), (ffn_phase, (l,))]
    for f, a in plist[:nphase]:
        f(*a)
    print(f"[build] sbuf peak {A.peak}", flush=True)
    sc.emit()
    return nc


def _pc(v, n):
    return np.ascontiguousarray(np.asarray(v, np.float32).reshape(n, 128).T)


def _consts():
    c = np.zeros((128, NCST), np.float32)
    c[:, CS["ones"]:CS["ones"] + 128] = 1.0
    c[:, CS["ident"]:CS["ident"] + 128] = np.eye(128, dtype=np.float32)
    p = np.zeros((128, 128), np.float32)
    for i in range(64):
        p[64 + i, i] = -1.0
        p[i, 64 + i] = 1.0
    c[:, CS["p128"]:CS["p128"] + 128] = p
    j = np.arange(128)[:, None]
    i = np.arange(128)[None, :]
    c[:, CS["maskF"]:CS["maskF"] + 128] = (j <= i)
    c[:, CS["maskB"]:CS["maskB"] + 128] = (j > i)
    c[:, CS["uF"]:CS["uF"] + 128] = (j <= i) * (-1.0 / 16.0)
    c[:, CS["uB"]:CS["uB"] + 128] = (j >= i) * (-1.0 / 16.0)
    c[:, CS["idxF"]:CS["idxF"] + 128] = (i + 1.0) * np.ones((128, 1))
    c[:, CS["idxB"]:CS["idxB"] + 128] = (128.0 - i) * np.ones((128, 1))
    p96 = np.zeros((128, 96), np.float32)
    for base in (64, 80):
        for t in range(8):
            p96[base + 8 + t, base + t] = -1.0
            p96[base + t, base + 8 + t] = 1.0
    c[:, CS["p96"]:CS["p96"] + 96] = p96
    es = np.zeros((128, 96), np.float32)
    for t in range(32):
        es[t, 64 + t] = 1.0
    c[:, CS["esel"]:CS["esel"] + 96] = es
    s65 = np.zeros((128, 64), np.float32)
    s65[64, :] = 1.0
    c[:, CS["sel65"]:CS["sel65"] + 64] = s65
    return c


def _tables(S_, CTX):
    NT = CTX + S_
    f32 = np.float32
    t = np.arange(S_)
    row = (t // 64).astype(f32)
    colp = (t % 64).astype(f32)
    inv = (f32(10000.0) ** (-(np.arange(8, dtype=f32) * f32(2.0) / f32(16)))).astype(f32)
    mc = np.ones((96, NT), f32)
    ms = np.zeros((96, NT), f32)
    ar = (row[:, None] * inv[None, :]).astype(f32)
    ac = (colp[:, None] * inv[None, :]).astype(f32)
    for i in range(8):
        for base, ang in ((64, ar), (80, ac)):
            mc[base + i, CTX:] = np.cos(ang[:, i]); mc[base + 8 + i, CTX:] = np.cos(ang[:, i])
            ms[base + i, CTX:] = np.sin(ang[:, i]); ms[base + 8 + i, CTX:] = np.sin(ang[:, i])
    pos = np.arange(NT).astype(f32)
    rinv = (f32(1.0) / (f32(10000.0) ** np.linspace(0.0, 1.0, 64, dtype=f32))).astype(f32)
    ang = (pos[:, None] * rinv[None, :]).astype(f32)
    rc = np.concatenate([np.cos(ang), np.cos(ang)], 1).T.astype(f32)
    rs_ = np.concatenate([np.sin(ang), np.sin(ang)], 1).T.astype(f32)
    return np.ascontiguousarray(mc), np.ascontiguousarray(ms), np.ascontiguousarray(rc), np.ascontiguousarray(rs_)


def _pack_pp(inp, L):
    pp = np.zeros((L, 128, NPP), np.float32)
    for l in range(L):
        def put(name, arr):
            arr = np.asarray(arr, np.float32)
            pp[l, :arr.shape[0], PP[name]:PP[name] + arr.shape[1]] = arr
        put("bada", _pc(inp["b_ada"][l], 48))
        put("n1w", _pc(inp["norm1_w"][l], 8))
        put("n2w", _pc(inp["norm2_w"][l], 8))
        put("bgate", _pc(np.asarray(inp["b_gate"][l]).reshape(-1), 24))
        put("qna", _pc(inp["mla_q_norm_a"][l], 2))
        put("kvna", _pc(inp["mla_kv_norm_a"][l], 1))
        put("qn", np.asarray(inp["mla_q_norm"][l]).reshape(96, 1))
        put("kn", np.asarray(inp["mla_k_norm"][l]).reshape(96, 1))
        put("gon", _pc(inp["gla_o_norm"][l], 1))
        put("wdw", _pc(np.asarray(inp["w_dw"][l]).reshape(-1), 66))
        put("bdw", _pc(inp["b_dw"][l], 22))
        put("retdec", np.broadcast_to(np.asarray(inp["ret_decay"][l]).reshape(1, 8), (128, 8)))
        put("bgk", np.broadcast_to(np.asarray(inp["gla_b_gk"][l]).reshape(1, 1024), (128, 1024)))
    return pp


_NC_CACHE = {}


def _run(inp, S_, CTX, L, ncores, debug=False, nphase=999):
    key = (S_, CTX, L, debug, nphase)
    if key not in _NC_CACHE:
        _NC_CACHE[key] = build(S_, CTX, L, debug, nphase)
    nc = _NC_CACHE[key]
    B = np.asarray(inp["x"]).shape[0]
    mc, ms, rc, rs_ = _tables(S_, CTX)
    shared = {
        "pp": _pack_pp(inp, L), "cst": _consts(), "mlaC": mc, "mlaS": ms, "retC": rc, "retS": rs_,
    }
    for k in ("w_ada", "w_in", "mla_w_qb", "mla_w_kvb", "gla_w_gk2", "w_branch", "w_out", "w_ffn_in", "w_ffn_out"):
        shared[k] = np.ascontiguousarray(np.asarray(inp[k], np.float32))
    cc = _pc(inp["c_ctx"], 8)
    in_maps = []
    for core in range(ncores):
        b = core % B
        m = dict(shared)
        m["xT"] = np.ascontiguousarray(np.asarray(inp["x"][b], np.float32).T)
        m["ctxT"] = np.ascontiguousarray(np.asarray(inp["ctx"][b], np.float32).T)
        m["cvec"] = np.ascontiguousarray(np.concatenate([_pc(inp["c"][b], 8), cc], 1))
        in_maps.append(m)
    res = run_bass_kernel_spmd(nc, in_maps, core_ids=list(range(ncores)))
    return res


def kernel(**inputs):
    x = np.asarray(inputs["x"])
    B, S_, _ = x.shape
    CTX = np.asarray(inputs["ctx"]).shape[1]
    L = np.asarray(inputs["w_ada"]).shape[0]
    res = _run(inputs, S_, CTX, L, 8)
    out = np.empty((B, S_, D), np.float32)
    for b in range(B):
        out[b] = res.results[b]["outT"].T
    return out
```

```python
import concourse.bass as bass
import concourse.mybir as mybir

SEM_LIMIT = 1000000000
DMA_K = 12


class Res:
    __slots__ = ("name", "lw", "rd_c", "rd_d", "excl")

    def __init__(self, name, excl=False):
        self.name = name
        self.excl = excl
        self.lw = None
        self.rd_c = {}
        self.rd_d = []


class Op:
    __slots__ = ("eng", "fn", "reads", "writes", "dma", "acc", "deps", "sig", "ev", "waits", "barrier")

    def __init__(self, eng, fn, reads, writes, dma, acc):
        self.eng = eng
        self.fn = fn
        self.reads = reads
        self.writes = writes
        self.dma = dma
        self.acc = acc
        self.deps = set()
        self.sig = False
        self.ev = None
        self.waits = []
        self.barrier = False


class Sched:
    ENGS = ("pe", "act", "dve", "pool", "sp")

    def __init__(self, nc):
        self.nc = nc
        self.ops = []

    def op(self, eng, fn, reads=(), writes=(), acc=False):
        self.ops.append(Op(eng, fn, list(reads), list(writes), False, acc))

    def dma(self, q, out, in_, reads=(), writes=()):
        self.ops.append(Op(q, lambda e, o=out, i=in_: e.dma_start(out=o, in_=i), list(reads), list(writes), True, False))

    def custom_dma(self, q, fn, reads=(), writes=()):
        self.ops.append(Op(q, fn, list(reads), list(writes), True, False))

    def barrier(self):
        o = Op(None, None, [], [], False, False)
        o.barrier = True
        self.ops.append(o)

    def _analyse(self):
        ops = self.ops
        last_c = {}
        last_d = {e: [] for e in self.ENGS}
        pend = {e: set() for e in self.ENGS}
        for i, op in enumerate(ops):
            if op.barrier:
                deps = set(last_c.values())
                for q in self.ENGS:
                    deps.update(last_d[q][-DMA_K:])
                for e in self.ENGS:
                    pend[e] |= deps
                continue
            d = op.deps
            if pend[op.eng]:
                d |= pend[op.eng]
                pend[op.eng] = set()
            for r in op.reads:
                if r.lw is not None:
                    d.add(r.lw)
                if r.excl:
                    for e2, j in r.rd_c.items():
                        if e2 != op.eng:
                            d.add(j)
            for w in op.writes:
                if w.lw is not None:
                    lwop = ops[w.lw]
                    if not (op.acc and op.eng == "pe" and lwop.eng == "pe" and not lwop.dma):
                        d.add(w.lw)
                d.update(w.rd_c.values())
                d.update(w.rd_d)
            d.discard(i)
            for r in op.reads:
                if op.dma:
                    r.rd_d.append(i)
                else:
                    r.rd_c[op.eng] = i
            for w in op.writes:
                w.lw = i
                w.rd_c = {}
                w.rd_d = []
            if op.dma:
                last_d[op.eng].append(i)
            else:
                last_c[op.eng] = i
            for j in d:
                ops[j].sig = True
        self.final_deps = set(last_c.values())
        for q in self.ENGS:
            self.final_deps.update(last_d[q][-DMA_K:])
        for j in self.final_deps:
            ops[j].sig = True

    def _assign(self):
        nc = self.nc
        ops = self.ops
        csem = {}
        ccnt = {}
        dsem = {e: [None] * DMA_K for e in self.ENGS}
        dcnt = {e: [0] * DMA_K for e in self.ENGS}
        dprev = {e: [None] * DMA_K for e in self.ENGS}
        dn = {e: 0 for e in self.ENGS}
        waited = {e: {} for e in self.ENGS}
        self.nsem = 0

        def newsem(tag):
            self.nsem += 1
            return nc.alloc_semaphore(name=f"s_{tag}_{self.nsem}")

        def add_wait(op, ev):
            sem, val = ev
            w = waited[op.eng]
            k = id(sem)
            if w.get(k, 0) >= val:
                return
            w[k] = val
            op.waits.append((sem, val))

        for i, op in enumerate(ops):
            if op.barrier:
                continue
            e = op.eng
            for j in sorted(op.deps):
                add_wait(op, ops[j].ev)
            if op.dma:
                k = dn[e] % DMA_K
                dn[e] += 1
                if dprev[e][k] is not None:
                    add_wait(op, dprev[e][k])
                if dsem[e][k] is None or dcnt[e][k] + 16 > SEM_LIMIT:
                    dsem[e][k] = newsem("d" + e)
                    dcnt[e][k] = 0
                dcnt[e][k] += 16
                op.ev = (dsem[e][k], dcnt[e][k])
                dprev[e][k] = op.ev
                op.sig = True
            elif op.sig:
                if e not in csem or ccnt[e] + 1 > SEM_LIMIT:
                    csem[e] = newsem("c" + e)
                    ccnt[e] = 0
                ccnt[e] += 1
                op.ev = (csem[e], ccnt[e])
        print("[sched] ccnt", ccnt, "dcnt max", {e: max(v) for e, v in dcnt.items()}, flush=True)
        self.final_waits = []
        fw = {}
        for j in sorted(self.final_deps):
            sem, val = ops[j].ev
            k = id(sem)
            if fw.get(k, (None, 0))[1] < val:
                fw[k] = (sem, val)
        self.final_waits = list(fw.values())

    def emit(self):
        self._analyse()
        self._assign()
        nc = self.nc
        ops = self.ops

        def run(e):
            def body(eng):
                for op in ops:
                    if op.barrier or op.eng != e:
                        continue
                    for sem, val in op.waits:
                        eng.wait_ge(sem, val)
                    ins = op.fn(eng)
                    if op.sig:
                        ins.then_inc(op.ev[0], 16 if op.dma else 1)
                if e == "sp":
                    for sem, val in self.final_waits:
                        eng.wait_ge(sem, val)
            return body

        with nc.Block() as block:
            block.tensor(run("pe"))
            block.scalar(run("act"))
            block.vector(run("dve"))
            block.gpsimd(run("pool"))
            block.sync(run("sp"))
        n = sum(1 for o in ops if not o.barrier)
        nw = sum(len(o.waits) for o in ops if not o.barrier)
        print(f"[sched] ops={n} waits={nw} sems={self.nsem}", flush=True)


class SbufAlloc:
    def __init__(self, nc, base=16640, limit=229376):
        self.nc = nc
        self.base = base
        self.off = base
        self.limit = limit
        self.n = 0
        self.peak = 0

    def reset(self, to=None):
        self.off = self.base if to is None else to

    def mark(self):
        return self.off

    def tile(self, shape, dtype, name="t"):
        esz = {mybir.dt.float32: 4, mybir.dt.bfloat16: 2}[dtype]
        nb = esz
        for s in shape[1:]:
            nb *= s
        nb = (nb + 63) // 64 * 64
        assert self.off + nb <= self.limit, f"SBUF overflow {name}: {self.off}+{nb}>{self.limit}"
        self.n += 1
        h = self.nc.alloc_sbuf_tensor_at(f"{name}_{self.n}", list(shape), dtype, offset=self.off)
        self.off += nb
        self.peak = max(self.peak, self.off)
        r = Res(f"{name}_{self.n}")
        return h, r


import os
import numpy as np
DBG = os.environ.get('KDBG', '')
from concourse.bass_utils import run_bass_kernel_spmd

F32 = mybir.dt.float32
BF16 = mybir.dt.bfloat16
ALU = mybir.AluOpType
AF = mybir.ActivationFunctionType

D = 1024
KC = 8
NIN = 7616
DFF = 2816
NJ = 22
EPS = 1e-6
C_MQ, C_MKV, C_MKR, C_GQ, C_GK, C_GV, C_GG, C_RF, C_RB, C_RQ, C_RK, C_RV, C_RG, C_G0 = (
    0, 256, 384, 416, 928, 1440, 1952, 2464, 2480, 2496, 3008, 3520, 4032, 4544)

PP = {}
_o = 0
for _n, _w in (("bada", 48), ("n1w", 8), ("n2w", 8), ("bgate", 24), ("qna", 2), ("kvna", 1), ("qn", 1), ("kn", 1),
               ("gon", 1), ("wdw", 66), ("bdw", 22), ("retdec", 8), ("bgk", 1024)):
    PP[_n] = _o
    _o += _w
NPP = _o
CS = {}
_o = 0
for _n, _w in (("ones", 128), ("ident", 128), ("p128", 128), ("maskF", 128), ("maskB", 128), ("uF", 128), ("uB", 128),
               ("idxF", 128), ("idxB", 128), ("p96", 96), ("esel", 96), ("sel65", 64)):
    CS[_n] = _o
    _o += _w
NCST = _o


def build(S_, CTX, L, debug=False, nphase=999):
    NT = CTX + S_
    NTP = NT + 4
    nlt = S_ // 512
    tiles = [(0, CTX, True, 1)] + [(CTX + 512 * i, 512, False, CTX + 3 + 512 * i) for i in range(nlt)]
    ftiles = [(0, 256, True, 1)] if CTX == 256 else [(i * 256, 256, True, 1 + i * 256) for i in range(CTX // 256)]
    ftiles = ftiles + [(CTX + 256 * i, 256, False, CTX + 3 + 256 * i) for i in range(S_ // 256)]
    NKT = NT // 128

    nc = bass.Bass("TRN2", target_bir_lowering=False)
    sc = Sched(nc)
    A = SbufAlloc(nc)

    def din(name, shape, dt=F32):
        return nc.dram_tensor(name, list(shape), dt, kind="ExternalInput").ap()

    skind = "ExternalOutput" if debug else "Internal"

    def dscr(name, shape, dt):
        return nc.dram_tensor(name, list(shape), dt, kind=skind).ap()

    xT = din("xT", [D, S_])
    ctxT = din("ctxT", [D, CTX])
    cvec_d = din("cvec", [128, 16])
    pp_d = din("pp", [L, 128, NPP])
    cst_d = din("cst", [128, NCST])
    mlaC_d = din("mlaC", [96, NT])
    mlaS_d = din("mlaS", [96, NT])
    retC_d = din("retC", [128, NT])
    retS_d = din("retS", [128, NT])
    w_ada_d = din("w_ada", [L, D, 6 * D])
    w_in_d = din("w_in", [L, D, NIN])
    w_qb_d = din("mla_w_qb", [L, 256, 768])
    w_kvb_d = din("mla_w_kvb", [L, 128, 1024])
    w_gk2_d = din("gla_w_gk2", [L, 2, 16, 512])
    w_br_d = din("w_branch", [L, 3, 512, D])
    w_out_d = din("w_out", [L, D, D])
    w_fi_d = din("w_ffn_in", [L, D, 2 * DFF])
    w_fo_d = din("w_ffn_out", [L, DFF, D])
    outT = nc.dram_tensor("outT", [D, S_], F32, kind="ExternalOutput").ap()

    HT = dscr("HT", [D, NT], F32)
    AT = dscr("AT", [D, NTP], BF16)
    QT = dscr("QT", [8, 96, NT], BF16)
    KT = dscr("KT", [8, 96, NT], BF16)
    VA = dscr("VA", [8, NT, 65], BF16)
    GQ = dscr("GQ", [512, NT], BF16)
    GK = dscr("GK", [512, NT], BF16)
    GG = dscr("GG", [512, NT], BF16)
    RQ = dscr("RQ", [512, NT], BF16)
    RK = dscr("RK", [512, NT], BF16)
    RG = dscr("RG", [512, NT], BF16)
    GV = dscr("GV", [NT, 512], BF16)
    RV = dscr("RV", [NT, 512], BF16)
    SPL = dscr("SPL", [2, NT, 512], F32)
    GT = dscr("GT", [3, D, NT], BF16)
    YM = dscr("YM", [512, NT], BF16)
    YG = dscr("YG", [512, NT], BF16)
    YR = dscr("YR", [512, NT], BF16)
    OF = dscr("OF", [512, NT], F32)
    of_res = {}

    PSALL = nc.alloc_psum_tensor("psall", [128, 7 * 512], F32)
    PS = [PSALL[:, i * 512:(i + 1) * 512] for i in range(7)]
    PR = [Res(f"ps{i}", True) for i in range(7)]
    PSB = nc.alloc_psum_tensor("psb", [128, 1024], BF16)
    _pb = Res("psb", True)
    PBR = [_pb, _pb]

    def mm(out, lhsT, rhs, start, stop, reads, wres):
        sc.op("pe", lambda e: e.matmul(out, lhsT=lhsT, rhs=rhs, start=start, stop=stop), reads=reads, writes=[wres], acc=True)

    def act(out, in_, func, reads, wres, bias=None, scale=None, eng="act"):
        kw = {}
        if bias is not None:
            kw["bias"] = bias
        if scale is not None:
            kw["scale"] = scale
        sc.op("act", lambda e: e.activation(out=out, in_=in_, func=func, **kw), reads=reads, writes=[wres])

    def cp(eng, out, in_, reads, wres):
        if eng == "act":
            sc.op("act", lambda e: e.copy(out=out, in_=in_), reads=reads, writes=[wres])
        else:
            sc.op(eng, lambda e: e.tensor_copy(out=out, in_=in_), reads=reads, writes=[wres])

    def tt(eng, out, in0, in1, op, reads, wres):
        sc.op(eng, lambda e: e.tensor_tensor(out=out, in0=in0, in1=in1, op=op), reads=reads, writes=[wres])

    def stt(eng, out, in0, scalar, in1, op0, op1, reads, wres):
        sc.op(eng, lambda e: e.scalar_tensor_tensor(out=out, in0=in0, scalar=scalar, in1=in1, op0=op0, op1=op1),
              reads=reads, writes=[wres])

    def ts(eng, out, in0, s1, s2, op0, op1, reads, wres):
        sc.op(eng, lambda e: e.tensor_scalar(out=out, in0=in0, scalar1=s1, scalar2=s2, op0=op0, op1=op1),
              reads=reads, writes=[wres])

    def ts1(eng, out, in0, s1, op0, reads, wres):
        sc.op(eng, lambda e: e.tensor_single_scalar(out=out, in_=in0, scalar=s1, op=op0), reads=reads, writes=[wres])

    def memset(eng, ap, val, wres):
        sc.op(eng, lambda e: e.memset(ap, val), writes=[wres])

    def load(out, in_, wres, reads=()):
        sc.dma("sp", out, in_, reads=reads, writes=[wres])

    def store(out, in_, rres, writes=()):
        sc.dma("pool", out, in_, reads=[rres], writes=writes)

    class Rot:
        def __init__(self, items):
            self.items = items
            self.i = 0

        def next(self):
            it = self.items[self.i % len(self.items)]
            self.i += 1
            return it

    def rot(n, shape, dt, name):
        return Rot([A.tile(shape, dt, name) for _ in range(n)])

    cast_i = [0]

    def load_w(dst, dst_res, src, kc, n, stage):
        rows = src.shape[0] // kc
        for k in range(kc):
            for n0 in range(0, n, 2048):
                w = min(2048, n - n0)
                st, sr = stage.next()
                load(st[:rows, :w], src[k * rows:(k + 1) * rows, n0:n0 + w], sr)
                eng = ("dve", "pool", "act")[cast_i[0] % 3]
                cast_i[0] += 1
                cp(eng, dst[:rows, k, n0:n0 + w], st[:rows, :w], [sr], dst_res)

    cst, cst_r = A.tile([128, NCST], F32, "cst")
    load(cst[:], cst_d, cst_r)
    cbf, cbf_r = A.tile([128, NCST], BF16, "cbf")
    cp("dve", cbf[:], cst[:], [cst_r], cbf_r)
    epst, eps_r = A.tile([128, 1], F32, "eps")
    memset("dve", epst[:], EPS, eps_r)
    onec, onec_r = A.tile([128, 1], F32, "onec")
    memset("dve", onec[:], 1.0, onec_r)
    ppt, pp_r = A.tile([128, NPP], F32, "pp")
    cvt, cv_r = A.tile([128, 16], F32, "cvec")
    cond2, cond_r = A.tile([128, 8, 2], F32, "cond2")
    modt, mod_r = A.tile([128, 48, 2], F32, "mod")
    g1t, g1_r = A.tile([128, 8, 2], F32, "g1")
    g2t, g2_r = A.tile([128, 8, 2], F32, "g2")
    lgt, lg_r = A.tile([128, 8], F32, "lg")
    nlgt, nlg_r = A.tile([128, 8], F32, "nlg")
    A.base = A.off

    def C_(name, rows=128, w=None, bf=True):
        o = CS[name]
        w = w if w is not None else (128 if name not in ("p96", "esel", "sel65") else (96 if name != "sel65" else 64))
        t = cbf if bf else cst
        return t[0:rows, o:o + w]

    ones_bf = C_("ones")
    ident_bf = C_("ident")
    CRES = [cst_r, cbf_r]

    def ppc(name, col, rows=128):
        return ppt[0:rows, PP[name] + col:PP[name] + col + 1]

    def partnorm(src_list, srcres, rows, nfeat, T, sqt, sq_r, ss_bank, lnt, ln_r, rst, rs_r):
        n = len(src_list)
        for i, s in enumerate(src_list):
            act(sqt[0:rows, i, :T], s, AF.Square, srcres, sq_r)
        for i in range(n):
            mm(PS[ss_bank][0:rows, :T], cbf[0:rows, CS["ones"]:CS["ones"] + rows], sqt[0:rows, i, :T], i == 0, i == n - 1,
               [sq_r, cbf_r], PR[ss_bank])
        act(lnt[0:rows, :T], PS[ss_bank][0:rows, :T], AF.Ln, [PR[ss_bank], eps_r], ln_r, bias=epst[0:rows, 0:1], scale=1.0 / nfeat)
        act(rst[0:rows, :T], lnt[0:rows, :T], AF.Exp, [ln_r], rs_r, scale=-0.5)

    sc.dma("sp", HT[:, 0:CTX], ctxT, reads=[], writes=[])
    for i in range(0, S_, 2048):
        w = min(2048, S_ - i)
        sc.dma("sp", HT[:, CTX + i:CTX + i + w], xT[:, i:i + w], reads=[], writes=[])
    load(cvt[:], cvec_d, cv_r)
    act(cond2[:, :, 0], cvt[:, 0:8], AF.Silu, [cv_r], cond_r)
    act(cond2[:, :, 1], cvt[:, 8:16], AF.Silu, [cv_r], cond_r)
    sc.barrier()

    def norm_phase(l, which):
        A.reset()
        hts = rot(2, [128, 8, 512], F32, "h")
        sqs = rot(2, [128, 8, 512], BF16, "sq")
        lns = rot(2, [128, 512], F32, "ln")
        rss = rot(2, [128, 512], F32, "rs")
        tms = rot(2, [128, 8, 512], F32, "tm")
        abs_ = rot(2, [128, 8, 514], BF16, "a")
        for ab_, abr_ in abs_.items:
            memset("pool", ab_[:, :, 0:1], 0.0, abr_)
        gt = g1t if which == 1 else g2t
        gr = g1_r if which == 1 else g2_r
        shv = 0 if which == 1 else 3
        HTv = HT.rearrange("(c p) t -> p c t", p=128)
        ATv = AT.rearrange("(c p) t -> p c t", p=128)
        for ti, (t0, T, isc, ac0) in enumerate(tiles):
            if isc and l == L - 1 and which == 2:
                continue
            col = 1 if isc else 0
            h, hr = hts.next()
            load(h[:, :, :T], HTv[:, :, t0:t0 + T], hr)
            sq, sqr = sqs.next()
            lnv, lnr = lns.next()
            rs, rsr = rss.next()
            bank = ti % 2
            partnorm([h[:, c, :T] for c in range(8)], [hr], 128, D, T, sq, sqr, bank, lnv, lnr, rs, rsr)
            tm, tmr = tms.next()
            tt("dve", tm[:, :, :T], h[:, :, :T], rs[:, :T].unsqueeze(1).to_broadcast([128, 8, T]), ALU.mult, [hr, rsr], tmr)
            ab, abr = abs_.next()
            first = (t0 == 0) or (t0 == CTX)
            lastt = (t0 + T == CTX) or (t0 + T == NT)
            for c in range(8):
                if c % 3 == 0:
                    act(ab[:, c, 1:1 + T], tm[:, c, :T], AF.Identity, [tmr, gr, mod_r], abr,
                        bias=modt[:, shv * 8 + c, col:col + 1], scale=gt[:, c, col:col + 1])
                else:
                    ts("dve" if c % 3 == 1 else "pool", ab[:, c, 1:1 + T], tm[:, c, :T], gt[:, c, col:col + 1],
                       modt[:, shv * 8 + c, col:col + 1], ALU.mult, ALU.add, [tmr, gr, mod_r], abr)
            if lastt:
                memset("pool", ab[:, :, T + 1:T + 2], 0.0, abr)
            lo = 0 if first else 1
            hi = T + 2 if lastt else T + 1
            store(ATv[:, :, ac0 - 1 + lo:ac0 - 1 + hi], ab[:, :, lo:hi], abr)
        sc.barrier()

    def mod_phase(l):
        A.reset()
        load(ppt[:], pp_d[l], pp_r)
        ws = rot(2, [128, 8, 1024], F32, "wada")
        wv = w_ada_d[l].rearrange("(c p) n -> p c n", p=128)
        for g in range(6):
            w, wr = ws.next()
            for k in range(0, 8, 2):
                load(w[:, k:k + 2, :], wv[:, k:k + 2, g * 1024:(g + 1) * 1024], wr)
            for nn in range(8):
                j = g * 8 + nn
                for k in range(8):
                    mm(PS[0][:, 2 * j:2 * j + 2], w[:, k, nn * 128:(nn + 1) * 128], cond2[:, k, :], k == 0, k == 7,
                       [wr, cond_r], PR[0])
        psv = PS[0][:, 0:96].rearrange("p (j t) -> p j t", t=2)
        for col in range(2):
            tt("dve", modt[:, :, col], psv[:, :, col], ppt[:, PP["bada"]:PP["bada"] + 48], ALU.add, [PR[0], pp_r], mod_r)
        for col in range(2):
            stt("dve", g1t[:, :, col], modt[:, 8:16, col], 1.0, ppt[:, PP["n1w"]:PP["n1w"] + 8], ALU.add, ALU.mult,
                [mod_r, pp_r], g1_r)
            stt("dve", g2t[:, :, col], modt[:, 32:40, col], 1.0, ppt[:, PP["n2w"]:PP["n2w"] + 8], ALU.add, ALU.mult,
                [mod_r, pp_r], g2_r)
        act(nlgt[:], ppt[:, PP["retdec"]:PP["retdec"] + 8], AF.Exp, [pp_r], nlg_r)
        ts1("dve", lgt[:], nlgt[:], -1.0, ALU.mult, [nlg_r], lg_r)
        sc.barrier()

    def qk_finish(ps_bank, T, normcol, tabC, tabS, tab_r, out_ap, tmp):
        sq, sqr, lnv, lnr, rs, rsr, xn, xnr, t1, t1r, t2, t2r, ob, obr, ssb, swb = tmp
        partnorm([PS[ps_bank][0:96, :T]], [PR[ps_bank]], 96, 96, T, sq, sqr, ssb, lnv, lnr, rs, rsr)
        stt("dve", xn[0:96, :T], PS[ps_bank][0:96, :T], normcol, rs[0:96, :T], ALU.mult, ALU.mult, [PR[ps_bank], rsr, pp_r], xnr)
        mm(PS[swb][0:96, :T], cbf[0:96, CS["p96"]:CS["p96"] + 96], xn[0:96, :T], True, True, [xnr, cbf_r], PR[swb])
        tt("pool", t1[0:96, :T], xn[0:96, :T], tabC, ALU.mult, [xnr, tab_r], t1r)
        tt("dve", t2[0:96, :T], PS[swb][0:96, :T], tabS, ALU.mult, [PR[swb], tab_r], t2r)
        tt("dve", ob[0:96, :T], t1[0:96, :T], t2[0:96, :T], ALU.add, [t1r, t2r], obr)
        store(out_ap, ob[0:96, :T], obr)

    def inproj_mla(l):
        A.reset()
        stage = rot(2, [128, 2048], F32, "stg")
        win, win_r = A.tile([128, 8, 416], BF16, "winA")
        load_w(win, win_r, w_in_d[l][:, 0:416], 8, 416, stage)
        wqb, wqb_r = A.tile([128, 2, 768], BF16, "wqb")
        load_w(wqb, wqb_r, w_qb_d[l], 2, 768, stage)
        wkf, wkf_r = A.tile([128, 1024], F32, "wkf")
        load(wkf[:], w_kvb_d[l], wkf_r)
        wkn, wkn_r = A.tile([128, 8, 96], BF16, "wkn")
        memset("dve", wkn[:], 0.0, wkn_r)
        wkfv = wkf[:].rearrange("p (h e) -> p h e", e=128)
        cp("dve", wkn[:, :, 0:64], wkfv[:, :, 0:64], [wkf_r], wkn_r)
        wkv, wkv_r = A.tile([128, 8, 64], BF16, "wkv")
        cp("dve", wkv[:], wkfv[:, :, 64:128], [wkf_r], wkv_r)
        ats = rot(2, [128, 8, 512], BF16, "a")
        cqs = rot(2, [128, 2, 512], F32, "cq")
        sq2 = rot(2, [128, 2, 512], BF16, "sq2")
        lns = rot(2, [128, 512], F32, "ln")
        rss = rot(2, [128, 512], F32, "rs")
        cqn = rot(2, [128, 2, 512], BF16, "cqn")
        ckv = rot(2, [128, 512], F32, "ckv")
        ckn = rot(2, [128, 512], BF16, "ckn")
        krs = rot(2, [32, 512], BF16, "kr")
        tCs = rot(2, [96, 512], F32, "tC")
        tSs = rot(2, [96, 512], F32, "tS")
        sqh = rot(3, [96, 1, 512], BF16, "sqh")
        lnh = rot(3, [96, 512], F32, "lnh")
        rsh = rot(3, [96, 512], F32, "rsh")
        xnh = rot(3, [96, 512], BF16, "xnh")
        t1h = rot(3, [96, 512], F32, "t1h")
        t2h = rot(3, [96, 512], F32, "t2h")
        obh = rot(3, [96, 512], BF16, "obh")
        vas = rot(2, [128, 8, 65], BF16, "va")
        for v, vr in vas.items:
            memset("pool", v[:], 1.0, vr)
        ATv = AT.rearrange("(c p) t -> p c t", p=128)
        VAv = VA.rearrange("h t e -> t h e")
        hcount = [0]

        def tmpset():
            i = hcount[0]
            hcount[0] += 1
            sq, sqr = sqh.next(); lnv, lnr = lnh.next(); rs, rsr = rsh.next(); xn, xnr = xnh.next()
            t1, t1r = t1h.next(); t2, t2r = t2h.next(); ob, obr = obh.next()
            return (sq, sqr, lnv, lnr, rs, rsr, xn, xnr, t1, t1r, t2, t2r, ob, obr, 3 + (i % 2), 5 + (i % 2))

        for ti, (t0, T, isc, ac0) in enumerate(tiles):
            a, ar = ats.next()
            load(a[:, :, :T], ATv[:, :, ac0:ac0 + T], ar)
            tC, tCr = tCs.next()
            tS, tSr = tSs.next()
            load(tC[:, :T], mlaC_d[:, t0:t0 + T], tCr)
            load(tS[:, :T], mlaS_d[:, t0:t0 + T], tCr)
            cq, cqr = cqs.next()
            for c2 in range(2):
                for k in range(8):
                    mm(PS[c2][:, :T], win[:, k, c2 * 128:(c2 + 1) * 128], a[:, k, :T], k == 0, k == 7, [win_r, ar], PR[c2])
                cp("act", cq[:, c2, :T], PS[c2][:, :T], [PR[c2]], cqr)
            sq, sqr = sq2.next(); lnv, lnr = lns.next(); rs, rsr = rss.next()
            partnorm([cq[:, 0, :T], cq[:, 1, :T]], [cqr], 128, 256, T, sq, sqr, 2, lnv, lnr, rs, rsr)
            cn, cnr = cqn.next()
            for c2 in range(2):
                stt("dve", cn[:, c2, :T], cq[:, c2, :T], ppc("qna", c2), rs[:, :T], ALU.mult, ALU.mult, [cqr, rsr, pp_r], cnr)
            for k in range(8):
                mm(PS[0][:, :T], win[:, k, 256:384], a[:, k, :T], k == 0, k == 7, [win_r, ar], PR[0])
            kv, kvr = ckv.next()
            cp("act", kv[:, :T], PS[0][:, :T], [PR[0]], kvr)
            sq, sqr = sq2.next(); lnv, lnr = lns.next(); rs, rsr = rss.next()
            partnorm([kv[:, :T]], [kvr], 128, 128, T, sq, sqr, 2, lnv, lnr, rs, rsr)
            kn, knr = ckn.next()
            stt("dve", kn[:, :T], kv[:, :T], ppc("kvna", 0), rs[:, :T], ALU.mult, ALU.mult, [kvr, rsr, pp_r], knr)
            for k in range(8):
                mm(PS[1][0:32, :T], win[:, k, 384:416], a[:, k, :T], k == 0, k == 7, [win_r, ar], PR[1])
            kr, krr = krs.next()
            cp("act", kr[0:32, :T], PS[1][0:32, :T], [PR[1]], krr)
            for h in range(8):
                b = h % 3
                for c2 in range(2):
                    mm(PS[b][0:96, :T], wqb[:, c2, h * 96:(h + 1) * 96], cn[:, c2, :T], c2 == 0, c2 == 1, [wqb_r, cnr], PR[b])
                qk_finish(b, T, ppc("qn", 0, 96), tC[0:96, :T], tS[0:96, :T], tCr, QT[h][:, t0:t0 + T], tmpset())
            for h in range(8):
                b = h % 3
                mm(PS[b][0:96, :T], wkn[:, h, :], kn[:, :T], True, False, [wkn_r, knr], PR[b])
                mm(PS[b][0:96, :T], cbf[0:32, CS["esel"]:CS["esel"] + 96], kr[0:32, :T], False, True, [cbf_r, krr], PR[b])
                qk_finish(b, T, ppc("kn", 0, 96), tC[0:96, :T], tS[0:96, :T], tCr, KT[h][:, t0:t0 + T], tmpset())
            for j in range(T // 128):
                b = j % 2
                mm(PS[b][:, 0:512], kn[:, j * 128:(j + 1) * 128], wkv[:].rearrange("p h e -> p (h e)"), True, True, [knr, wkv_r], PR[b])
                va, var_ = vas.next()
                cp("act", va[:, :, 0:64], PS[b][:, 0:512].rearrange("p (h e) -> p h e", e=64), [PR[b]], var_)
                store(VAv[t0 + j * 128:t0 + (j + 1) * 128], va[:], var_)
        sc.barrier()

    def inproj_lin(l, kind):
        A.reset()
        stage = rot(2, [128, 2048], F32, "stg")
        c0 = C_GQ if kind == 0 else C_RQ
        ncols = 2080 if kind == 0 else 2048
        win, win_r = A.tile([128, 8, ncols], BF16, "winB")
        load_w(win, win_r, w_in_d[l][:, c0:c0 + ncols], 8, ncols, stage)
        if kind == 0:
            w2, w2_r = A.tile([16, 2, 512], BF16, "w2")
            for d in range(2):
                st, sr = stage.next()
                load(st[0:16, 0:512], w_gk2_d[l, d], sr)
                cp("dve", w2[0:16, d, :], st[0:16, 0:512], [sr], w2_r)
        ats = rot(2, [128, 8, 512], BF16, "a")
        stq = rot(2, [128, 4, 512], BF16, "stq")
        stv = rot(2, [128, 4, 512], BF16, "stv")
        if kind == 0:
            rfs = rot(2, [16, 512], BF16, "rf")
            rbs = rot(2, [16, 512], BF16, "rb")
            xbs = rot(2, [128, 512], F32, "xb")
            ees = rot(2, [128, 512], F32, "ee")
            sps = rot(2, [128, 4, 512], F32, "sp")
        else:
            tCs = rot(2, [128, 512], F32, "tC")
            tSs = rot(2, [128, 512], F32, "tS")
            xbf = rot(3, [128, 512], BF16, "xbf")
            t1s = rot(3, [128, 512], F32, "t1")
            t2s = rot(3, [128, 512], F32, "t2")
        ATv = AT.rearrange("(c p) t -> p c t", p=128)
        Qd, Kd, Gd, Vd = (GQ, GK, GG, GV) if kind == 0 else (RQ, RK, RG, RV)
        pbank = [0]

        def nb():
            pbank[0] += 1
            return pbank[0] % 4

        cpi = [0]
        for ti, (t0, T, isc, ac0) in enumerate(tiles):
            a, ar = ats.next()
            load(a[:, :, :T], ATv[:, :, ac0:ac0 + T], ar)
            if kind == 1 and 'notab' not in DBG:
                tC, tCr = tCs.next()
                tS, tSr = tSs.next()
                load(tC[:, :T], retC_d[:, t0:t0 + T], tCr)
                load(tS[:, :T], retS_d[:, t0:t0 + T], tSr)
            for wi, (dst, coff) in enumerate(((Qd, 0), (Kd, 512), (Gd, 1536))):
                st, sr = stq.next()
                for h in range(4):
                    b = nb()
                    for k in range(8):
                        mm(PS[b][:, :T], win[:, k, coff + h * 128:coff + (h + 1) * 128], a[:, k, :T], k == 0, k == 7, [win_r, ar], PR[b])
                    if wi == 2:
                        act(st[:, h, :T], PS[b][:, :T], AF.Silu, [PR[b]], sr)
                    elif kind == 0 or 'norope' in DBG:
                        cpi[0] += 1
                        cp("act" if cpi[0] % 2 else "dve", st[:, h, :T], PS[b][:, :T], [PR[b]], sr)
                    else:
                        xb_, xbr = xbf.next()
                        cp("act", xb_[:, :T], PS[b][:, :T], [PR[b]], xbr)
                        b2 = 4 + (h % 2)
                        if 'nomm' in DBG:
                            b2 = b
                        else:
                            mm(PS[b2][:, :T], C_("p128"), xb_[:, :T], True, True, [xbr, cbf_r], PR[b2])
                        t1, t1r = t1s.next()
                        t2, t2r = t2s.next()
                        tt("dve", t1[:, :T], PS[b][:, :T], tC[:, :T], ALU.mult, [PR[b], tCr], t1r)
                        tt("dve", t2[:, :T], PS[b2][:, :T], tS[:, :T], ALU.mult, [PR[b2], tSr], t2r)
                        tt("dve" if 'dveadd' in DBG else "pool", st[:, h, :T], t1[:, :T], t2[:, :T], ALU.add, [t1r, t2r], sr)
                store(dst.rearrange("(h p) t -> p h t", p=128)[:, :, t0:t0 + T], st[:, :, :T], sr)
            sv, svr = stv.next()
            nbk = T // 128
            for j in range(nbk):
                b = nb()
                for k in range(8):
                    mm(PS[b][:, 0:512], a[:, k, j * 128:(j + 1) * 128], win[:, k, 1024:1536], k == 0, k == 7, [win_r, ar], PR[b])
                cpi[0] += 1
                cp("act" if cpi[0] % 2 else "dve", sv[:, j, :], PS[b][:, 0:512], [PR[b]], svr)
            store(Vd[t0:t0 + T, :].rearrange("(j p) n -> p j n", p=128), sv[:, 0:nbk, :], svr)
            if kind == 0:
                rf, rfr = rfs.next()
                rb, rbr = rbs.next()
                for (rt, rr, cc) in ((rf, rfr, 2048), (rb, rbr, 2064)):
                    b = nb()
                    for k in range(8):
                        mm(PS[b][0:16, :T], win[:, k, cc:cc + 16], a[:, k, :T], k == 0, k == 7, [win_r, ar], PR[b])
                    cp("act", rt[0:16, :T], PS[b][0:16, :T], [PR[b]], rr)
                for d, (rt, rr) in enumerate(((rf, rfr), (rb, rbr))):
                    sp_, spr = sps.next()
                    for j in range(nbk):
                        b = nb()
                        mm(PS[b][:, 0:512], rt[0:16, j * 128:(j + 1) * 128], w2[0:16, d, :], True, True, [rr, w2_r], PR[b])
                        xb_, xbr = xbs.next()
                        tt("dve", xb_[:], PS[b][:, 0:512], ppt[:, PP["bgk"] + d * 512:PP["bgk"] + (d + 1) * 512], ALU.add, [PR[b], pp_r], xbr)
                        ee, eer = ees.next()
                        act(ee[:], xb_[:], AF.Exp, [xbr], eer, scale=-1.0)
                        act(sp_[:, j, :], ee[:], AF.Ln, [eer], spr, bias=1.0)
                    store(SPL[d, t0:t0 + T, :].rearrange("(j p) n -> p j n", p=128), sp_[:, 0:nbk, :], spr)
        sc.barrier()

    def inproj_gates(l):
        A.reset()
        stage = rot(2, [128, 2048], F32, "stg")
        win, win_r = A.tile([128, 8, 3072], BF16, "winD")
        load_w(win, win_r, w_in_d[l][:, C_G0:C_G0 + 3072], 8, 3072, stage)
        ats = rot(2, [128, 8, 512], BF16, "a")
        sts = rot(3, [128, 8, 512], BF16, "stg8")
        ATv = AT.rearrange("(c p) t -> p c t", p=128)
        bi = 0
        for ti, (t0, T, isc, ac0) in enumerate(tiles):
            if isc and l == L - 1:
                continue
            a, ar = ats.next()
            load(a[:, :, :T], ATv[:, :, ac0:ac0 + T], ar)
            for n in range(3):
                st, sr = sts.next()
                for c in range(8):
                    b = bi % 4
                    bi += 1
                    for k in range(8):
                        mm(PS[b][:, :T], win[:, k, n * 1024 + c * 128:n * 1024 + (c + 1) * 128], a[:, k, :T], k == 0, k == 7, [win_r, ar], PR[b])
                    act(st[:, c, :T], PS[b][:, :T], AF.Sigmoid, [PR[b], pp_r], sr, bias=ppc("bgate", n * 8 + c))
                store(GT[n].rearrange("(c p) t -> p c t", p=128)[:, :, t0:t0 + T], st[:, :, :T], sr)
        sc.barrier()

    def attention(l):
        A.reset()
        kts = rot(2, [96, NT], BF16, "kt")
        vas = rot(2, [128, NKT, 65], BF16, "va")
        qts = rot(3, [96, 512], BF16, "qt")
        pts = rot(3, [128, 2, 512], BF16, "pt")
        osb = rot(2, [65, 512], F32, "osb")
        rvs = rot(2, [64, 512], F32, "rinv")
        ybs = rot(2, [64, 512], BF16, "yb")
        scale = 96.0 ** -0.5
        qi = 0
        for h in range(8):
            kt, ktr = kts.next()
            va, var_ = vas.next()
            load(kt[:, :], KT[h], ktr)
            VAh = VA[h].rearrange("(k p) e -> p k e", p=128)
            for k0 in range(0, NKT, 8):
                k1 = min(NKT, k0 + 8)
                load(va[:, k0:k1, :], VAh[:, k0:k1, :], var_)
            for ti, (t0, T, isc, ac0) in enumerate(tiles):
                if isc and l == L - 1:
                    continue
                nk = CTX // 128 if isc else NKT
                qt, qtr = qts.next()
                load(qt[:, :T], QT[h][:, t0:t0 + T], qtr)
                ob = 4 + (qi % 2)
                qi += 1
                npair = nk // 2

                def qk2(pi):
                    for u in range(2):
                        b = 2 * (pi % 2) + u
                        i = 2 * pi + u
                        mm(PS[b][:, :T], kt[:, i * 128:(i + 1) * 128], qt[:, :T], True, True, [ktr, qtr], PR[b])

                qk2(0)
                for pi in range(npair):
                    if pi + 1 < npair:
                        qk2(pi + 1)
                    pp = pi % 2
                    pt, ptr = pts.next()
                    s2 = PSALL[:, pp * 1024:(pp + 1) * 1024].rearrange("p (a t) -> p a t", t=512)
                    act(pt[:, :, :T], s2[:, :, :T], AF.Exp, [PR[2 * pp], PR[2 * pp + 1]], ptr, scale=scale)
                    for u in range(2):
                        i = 2 * pi + u
                        mm(PS[ob][0:65, :T], va[:, i, :], pt[:, u, :T], i == 0, i == nk - 1, [var_, ptr], PR[ob])
                o, orr = osb.next()
                cp("dve", o[0:65, :T], PS[ob][0:65, :T], [PR[ob]], orr)
                mm(PS[6][0:64, :T], cst[0:65, CS["sel65"]:CS["sel65"] + 64], o[0:65, :T], True, True, [cst_r, orr], PR[6])
                rv, rvr = rvs.next()
                sc.op("dve", lambda e, rv=rv, T=T: e.reciprocal(out=rv[0:64, :T], in_=PS[6][0:64, :T]), reads=[PR[6]], writes=[rvr])
                yb, ybr = ybs.next()
                tt("dve", yb[0:64, :T], o[0:64, :T], rv[0:64, :T], ALU.mult, [orr, rvr], ybr)
                store(YM[h * 64:(h + 1) * 64, t0:t0 + T], yb[0:64, :T], ybr)
        sc.barrier()

    def sweep(l, kind):
        A.reset()
        Qd, Kd, Gd, Vd, Yd = (GQ, GK, GG, GV, YG) if kind == 0 else (RQ, RK, RG, RV, YR)
        qs = 128.0 ** -0.5 if kind == 0 else 1.0
        ks = 1.0 if kind == 0 else 128.0 ** -0.5
        NH = 4
        St = [A.tile([128, 128], F32, "S") for _ in range(NH)]
        Sb2 = [[A.tile([128, 128], BF16, "Sb") for _ in range(2)] for _ in range(NH)]
        qts = [rot(2, [128, 512], BF16, "q") for _ in range(NH)]
        kts_ = [rot(2, [128, 512], BF16, "k") for _ in range(NH)]
        vts = [rot(2, [128, 4, 128], BF16, "v") for _ in range(NH)]
        qds = [rot(2, [128, 512], BF16, "qd") for _ in range(NH)]
        kis = [rot(2, [128, 512], BF16, "ki") for _ in range(NH)]
        if kind == 0:
            spt = [rot(2, [128, 4, 128], F32, "spt") for _ in range(NH)]
            Es = [rot(2, [128, 512], F32, "E") for _ in range(NH)]
            Eis = [rot(2, [128, 512], F32, "Ei") for _ in range(NH)]
        else:
            Etab = [[A.tile([128, 128], F32, "Et") for _ in range(2)] for _ in range(NH)]
            Eitab = [[A.tile([128, 128], F32, "Eit") for _ in range(2)] for _ in range(NH)]
            for h in range(NH):
                for d in range(2):
                    idx = cst[:, CS["idxF" if d == 0 else "idxB"]:CS["idxF" if d == 0 else "idxB"] + 128]
                    act(Etab[h][d][0][:], idx, AF.Exp, [cst_r, lg_r], Etab[h][d][1], scale=lgt[:, d * 4 + h:d * 4 + h + 1])
                    act(Eitab[h][d][0][:], idx, AF.Exp, [cst_r, nlg_r], Eitab[h][d][1], scale=nlgt[:, d * 4 + h:d * 4 + h + 1])
        ktas = [rot(2, [128, 512], BF16, "kta") for _ in range(NH)]
        stmp = [A.tile([128, 128], F32, "stmp") for _ in range(NH)]
        ams = rot(4, [128, 128], BF16, "am")
        ofs = [rot(2, [128, 512], F32, "of") for _ in range(NH)]
        sgs = [rot(2, [128, 512], BF16, "sg") for _ in range(NH)]
        ots = rot(8, [128, 512], F32, "ot")
        sqs = rot(2, [128, 1, 512], BF16, "sq")
        lns = rot(2, [128, 512], F32, "ln")
        rss = rot(2, [128, 512], F32, "rs")
        y1s = rot(2, [128, 512], F32, "y1")
        ybs = rot(2, [128, 512], BF16, "yb")
        cnt = [0]
        for d in range(2):
            sbi = [0] * NH
            for h in range(NH):
                memset("dve", St[h][0][:], 0.0, St[h][1])
                memset("pool", Sb2[h][0][0][:], 0.0, Sb2[h][0][1])
            order = [0] + (list(range(1, len(tiles))) if d == 0 else list(range(len(tiles) - 1, 0, -1)))
            mask = cst[:, CS["maskF" if d == 0 else "maskB"]:CS["maskF" if d == 0 else "maskB"] + 128]
            um = cst[:, CS["uF" if d == 0 else "uB"]:CS["uF" if d == 0 else "uB"] + 128]
            for ti in order:
                t0, T, isc, ac0 = tiles[ti]
                skip_out = isc and l == L - 1
                nbk = T // 128
                cur = []
                for h in range(NH):
                    q, qr = qts[h].next(); k, kr = kts_[h].next(); v, vr = vts[h].next()
                    load(q[:, :T], Qd[h * 128:(h + 1) * 128, t0:t0 + T], qr)
                    load(k[:, :T], Kd[h * 128:(h + 1) * 128, t0:t0 + T], kr)
                    load(v[:, 0:nbk, :], Vd[t0:t0 + T, h * 128:(h + 1) * 128].rearrange("(j p) n -> p j n", p=128), vr)
                    qd, qdr = qds[h].next(); ki, kir = kis[h].next()
                    if kind == 0:
                        sp_, spr = spt[h].next()
                        load(sp_[:, 0:nbk, :], SPL[d, t0:t0 + T, h * 128:(h + 1) * 128].rearrange("(j p) n -> p j n", p=128), spr)
                        cb = 0
                        for j in range(nbk):
                            mm(PS[cb][:, j * 128:(j + 1) * 128], sp_[:, j, :], um, True, True, [spr, cst_r], PR[cb])
                        E, Er = Es[h].next(); Ei, Eir = Eis[h].next()
                        act(E[:, :T], PS[cb][:, :T], AF.Exp, [PR[cb]], Er)
                        act(Ei[:, :T], PS[cb][:, :T], AF.Exp, [PR[cb]], Eir, scale=-1.0)
                        stt("dve", qd[:, :T], q[:, :T], qs, E[:, :T], ALU.mult, ALU.mult, [qr, Er], qdr)
                        stt("dve", ki[:, :T], k[:, :T], ks, Ei[:, :T], ALU.mult, ALU.mult, [kr, Eir], kir)
                        gsrc = (E, Er)
                    else:
                        E, Er = Etab[h][d]; Ei, Eir = Eitab[h][d]
                        stt("dve", qd[:, :T].rearrange("p (j n) -> p j n", n=128), q[:, :T].rearrange("p (j n) -> p j n", n=128), qs,
                            E[:].unsqueeze(1).to_broadcast([128, nbk, 128]), ALU.mult, ALU.mult, [qr, Er], qdr)
                        stt("dve", ki[:, :T].rearrange("p (j n) -> p j n", n=128), k[:, :T].rearrange("p (j n) -> p j n", n=128), ks,
                            Ei[:].unsqueeze(1).to_broadcast([128, nbk, 128]), ALU.mult, ALU.mult, [kr, Eir], kir)
                        gsrc = (E, Er)
                    for j in range(nbk):
                        sc.op("pe", lambda e, j=j, ki=ki: e.transpose(PSB[:, j * 128:(j + 1) * 128], ki[:, j * 128:(j + 1) * 128], ident_bf),
                              reads=[kir, cbf_r], writes=[PBR[0]])
                    kta, ktar = ktas[h].next()
                    cp("act", kta[:, :T], PSB[:, 0:T], [PBR[0]], ktar)
                    cur.append((q, qr, k, kr, v, vr, qd, qdr, ki, kir, gsrc, kta, ktar))
                ob = [None] * NH
                blocks = list(range(nbk)) if d == 0 else list(range(nbk - 1, -1, -1))
                for j in blocks:
                    c0 = j * 128
                    for h in range(NH):
                        q, qr, k, kr, v, vr, qd, qdr, ki, kir, (E, Er), kta, ktar = cur[h]
                        if kind == 0:
                            gcol = E[:, c0 + 127:c0 + 128] if d == 0 else E[:, c0:c0 + 1]
                        else:
                            gcol = E[:, 127:128] if d == 0 else E[:, 0:1]
                        cur_b = sbi[h]
                        nxt_b = 1 - cur_b
                        ib = 1 if h % 2 == 0 else 6
                        mm(PS[ib][:, 0:128], kta[:, c0:c0 + 128], v[:, j, :], True, True, [ktar, vr], PR[ib])
                        tmp, tmpr = stmp[h]
                        tt("dve", tmp[:], St[h][0][:], PS[ib][:, 0:128], ALU.add, [St[h][1], PR[ib]], tmpr)
                        act(Sb2[h][nxt_b][0][:], tmp[:], AF.Copy, [tmpr, Er], Sb2[h][nxt_b][1], scale=gcol)
                        ts1("dve", St[h][0][:], tmp[:], gcol, ALU.mult, [tmpr, Er], St[h][1])
                        pb = cnt[0] % 2
                        cnt[0] += 1
                        ab = 4 + pb
                        mm(PS[ab][:, 0:128], ki[:, c0:c0 + 128], qd[:, c0:c0 + 128], True, True, [kir, qdr], PR[ab])
                        am, amr = ams.next()
                        tt("dve", am[:], PS[ab][:, 0:128], mask, ALU.mult, [PR[ab], cst_r], amr)
                        obk = 2 + (h % 2)
                        mm(PS[obk][:, 0:128], v[:, j, :], am[:], True, False, [vr, amr], PR[obk])
                        mm(PS[obk][:, 0:128], Sb2[h][cur_b][0][:], qd[:, c0:c0 + 128], False, True, [Sb2[h][cur_b][1], qdr], PR[obk])
                        sbi[h] = nxt_b
                        if not skip_out:
                            if ob[h] is None:
                                ob[h] = ots.next()
                            ot, otr = ob[h]
                            cp("act", ot[:, c0:c0 + 128], PS[obk][:, 0:128], [PR[obk]], otr)
                if skip_out:
                    continue
                for h in range(NH):
                    ot, otr = ob[h]
                    key = (kind, h, ti)
                    if d == 0:
                        r_ = of_res.setdefault(key, Res(f"of{key}"))
                        store(OF[h * 128:(h + 1) * 128, t0:t0 + T], ot[:, :T], otr, writes=[r_])
                    else:
                        r_ = of_res[key]
                        of_, ofr = ofs[h].next()
                        load(of_[:, :T], OF[h * 128:(h + 1) * 128, t0:t0 + T], ofr, reads=[r_])
                        sg, sgr = sgs[h].next()
                        load(sg[:, :T], Gd[h * 128:(h + 1) * 128, t0:t0 + T], sgr)
                        tt("dve", ot[:, :T], ot[:, :T], of_[:, :T], ALU.add, [otr, ofr], otr)
                        sq, sqr = sqs.next(); lnv, lnr = lns.next(); rs, rsr = rss.next()
                        partnorm([ot[:, :T]], [otr], 128, 128, T, sq, sqr, 0, lnv, lnr, rs, rsr)
                        y1, y1r = y1s.next()
                        ncol = ppc("gon", 0) if kind == 0 else onec[:, 0:1]
                        stt("dve", y1[:, :T], ot[:, :T], ncol, rs[:, :T], ALU.mult, ALU.mult, [otr, rsr, pp_r, onec_r], y1r)
                        yb, ybr = ybs.next()
                        tt("pool", yb[:, :T], y1[:, :T], sg[:, :T], ALU.mult, [y1r, sgr], ybr)
                        store(Yd[h * 128:(h + 1) * 128, t0:t0 + T], yb[:, :T], ybr)
            sc.barrier()

    def merge_phase(l):
        A.reset()
        stage = rot(2, [128, 2048], F32, "stg")
        wbr, wbr_r = A.tile([128, 12, 1024], BF16, "wbr")
        load_w(wbr, wbr_r, w_br_d[l].rearrange("n k m -> (n k) m"), 12, 1024, stage)
        wo, wo_r = A.tile([128, 8, 1024], BF16, "wo")
        load_w(wo, wo_r, w_out_d[l], 8, 1024, stage)
        ys = [rot(2, [128, 4, 512], BF16, f"y{n}") for n in range(3)]
        gs = [rot(2, [128, 8, 512], BF16, f"g{n}") for n in range(3)]
        hts = rot(2, [128, 8, 512], F32, "h")
        m32 = rot(2, [128, 512], F32, "m32")
        tms = rot(3, [128, 512], F32, "tm")
        mbs = rot(2, [128, 8, 512], BF16, "mb")
        HTv = HT.rearrange("(c p) t -> p c t", p=128)
        bi = 0
        for ti, (t0, T, isc, ac0) in enumerate(tiles):
            if isc and l == L - 1:
                continue
            col = 1 if isc else 0
            yy = []
            gg = []
            for n, Yd in enumerate((YM, YG, YR)):
                y, yr = ys[n].next()
                load(y[:, :, :T], Yd.rearrange("(c p) t -> p c t", p=128)[:, :, t0:t0 + T], yr)
                g, gr = gs[n].next()
                load(g[:, :, :T], GT[n].rearrange("(c p) t -> p c t", p=128)[:, :, t0:t0 + T], gr)
                yy.append((y, yr))
                gg.append((g, gr))
            h, hr = hts.next()
            load(h[:, :, :T], HTv[:, :, t0:t0 + T], hr)
            mb, mbr = mbs.next()
            for c in range(8):
                m, mr = m32.next()
                for n in range(3):
                    b = bi % 3
                    bi += 1
                    for k in range(4):
                        mm(PS[b][:, :T], wbr[:, n * 4 + k, c * 128:(c + 1) * 128], yy[n][0][:, k, :T], k == 0, k == 3, [wbr_r, yy[n][1]], PR[b])
                    if n == 0:
                        tt("dve", m[:, :T], PS[b][:, :T], gg[0][0][:, c, :T], ALU.mult, [PR[b], gg[0][1]], mr)
                    else:
                        tm, tmr = tms.next()
                        tt("dve", tm[:, :T], PS[b][:, :T], gg[n][0][:, c, :T], ALU.mult, [PR[b], gg[n][1]], tmr)
                        if n == 1:
                            tt("pool", m[:, :T], m[:, :T], tm[:, :T], ALU.add, [mr, tmr], mr)
                        else:
                            tt("pool", mb[:, c, :T], m[:, :T], tm[:, :T], ALU.add, [mr, tmr], mbr)
            for c in range(8):
                b = 3 + (c % 2)
                for k in range(8):
                    mm(PS[b][:, :T], wo[:, k, c * 128:(c + 1) * 128], mb[:, k, :T], k == 0, k == 7, [wo_r, mbr], PR[b])
                stt("dve", h[:, c, :T], PS[b][:, :T], modt[:, 16 + c, col:col + 1], h[:, c, :T], ALU.mult, ALU.add, [PR[b], mod_r, hr], hr)
            store(HTv[:, :, t0:t0 + T], h[:, :, :T], hr)
        sc.barrier()

    def ffn_phase(l):
        A.reset()
        stage = rot(2, [128, 2048], F32, "stg")
        wfi, wfi_r = A.tile([128, 8, 2 * DFF], BF16, "wfi")
        wfo, wfo_r = A.tile([128, NJ, 1024], BF16, "wfo")
        load_w(wfi, wfi_r, w_fi_d[l], 8, 2 * DFF, stage)
        load_w(wfo, wfo_r, w_fo_d[l], NJ, 1024, stage)
        FT = 256
        ats = rot(2, [128, 8, FT + 2], BF16, "a2")
        hts = rot(2, [128, 8, FT], F32, "h")
        hid = rot(1, [128, NJ, FT], BF16, "hid")
        gsb = rot(2, [128, FT + 2], F32, "gsb")
        acc = rot(2, [128, FT], F32, "acc")
        gel = rot(2, [128, FT], F32, "gel")
        ATv = AT.rearrange("(c p) t -> p c t", p=128)
        HTv = HT.rearrange("(c p) t -> p c t", p=128)
        OTv = outT.rearrange("(c p) t -> p c t", p=128)
        last = (l == L - 1)
        nlat = S_ // FT
        for fi, (t0, T, isc, ac0) in enumerate(ftiles):
            if isc and last:
                continue
            col = 1 if isc else 0
            a, ar = ats.next()
            load(a[:, :, :], ATv[:, :, ac0 - 1:ac0 + T + 1], ar)
            first = (t0 == 0) or (t0 == CTX)
            lastt = (t0 + T == CTX) or (t0 + T == NT)
            if first:
                memset("pool", a[:, :, 0:1], 0.0, ar)
            if lastt:
                memset("pool", a[:, :, T + 1:T + 2], 0.0, ar)
            h, hr = hts.next()
            load(h[:, :, :], HTv[:, :, t0:t0 + T], hr)
            hd, hdr = hid.next()
            for j in range(NJ):
                gb = j % 2
                ub = 2 + (j % 2)
                for k in range(8):
                    mm(PS[gb][:, 0:T], wfi[:, k, j * 128:(j + 1) * 128], a[:, k, 1:T + 1], k == 0, k == 7, [wfi_r, ar], PR[gb])
                for k in range(8):
                    mm(PS[4][:, 0:2], wfi[:, k, j * 128:(j + 1) * 128], a[:, k, 0:T + 2:T + 1], k == 0, k == 7, [wfi_r, ar], PR[4])
                for k in range(8):
                    mm(PS[ub][:, 0:T], wfi[:, k, DFF + j * 128:DFF + (j + 1) * 128], a[:, k, 1:T + 1], k == 0, k == 7, [wfi_r, ar], PR[ub])
                g, gr = gsb.next()
                cp("act", g[:, 1:T + 1], PS[gb][:, 0:T], [PR[gb]], gr)
                cp("act", g[:, 0:T + 2:T + 1], PS[4][:, 0:2], [PR[4]], gr)
                ac, acr = acc.next()
                ts("dve", ac[:], g[:, 0:T], ppc("wdw", j), ppc("bdw", j), ALU.mult, ALU.add, [gr, pp_r], acr)
                stt("dve", ac[:], g[:, 1:T + 1], ppc("wdw", NJ + j), ac[:], ALU.mult, ALU.add, [gr, pp_r, acr], acr)
                stt("dve", ac[:], g[:, 2:T + 2], ppc("wdw", 2 * NJ + j), ac[:], ALU.mult, ALU.add, [gr, pp_r, acr], acr)
                ge, ger = gel.next()
                act(ge[:], ac[:], AF.Gelu_apprx_tanh, [acr], ger)
                tt("dve", hd[:, j, :], PS[ub][:, 0:T], ge[:], ALU.mult, [PR[ub], ger], hdr)
            for c in range(8):
                b = 5 + (c % 2)
                for j in range(NJ):
                    mm(PS[b][:, 0:T], wfo[:, j, c * 128:(c + 1) * 128], hd[:, j, :], j == 0, j == NJ - 1, [wfo_r, hdr], PR[b])
                stt("dve", h[:, c, :], PS[b][:, 0:T], modt[:, 40 + c, col:col + 1], h[:, c, :], ALU.mult, ALU.add, [PR[b], mod_r, hr], hr)
            if last:
                store(OTv[:, :, t0 - CTX:t0 - CTX + T], h[:, :, :], hr)
            else:
                store(HTv[:, :, t0:t0 + T], h[:, :, :], hr)
        sc.barrier()

    plist = []
    for l in range(L):
        plist += [(mod_phase, (l,)), (norm_phase, (l, 1)), (inproj_mla, (l,)), (inproj_lin, (l, 0)), (inproj_lin, (l, 1)),
                  (inproj_gates, (l,)), (attention, (l,)), (sweep, (l, 0)), (sweep, (l, 1)), (merge_phase, (l,)),
                  (norm_phase, (l, 2)), (ffn_phase, (l,))]
    for f, a in plist[:nphase]:
        f(*a)
    print(f"[build] sbuf peak {A.peak}", flush=True)
    sc.emit()
    return nc


def _pc(v, n):
    return np.ascontiguousarray(np.asarray(v, np.float32).reshape(n, 128).T)


def _consts():
    c = np.zeros((128, NCST), np.float32)
    c[:, CS["ones"]:CS["ones"] + 128] = 1.0
    c[:, CS["ident"]:CS["ident"] + 128] = np.eye(128, dtype=np.float32)
    p = np.zeros((128, 128), np.float32)
    for i in range(64):
        p[64 + i, i] = -1.0
        p[i, 64 + i] = 1.0
    c[:, CS["p128"]:CS["p128"] + 128] = p
    j = np.arange(128)[:, None]
    i = np.arange(128)[None, :]
    c[:, CS["maskF"]:CS["maskF"] + 128] = (j <= i)
    c[:, CS["maskB"]:CS["maskB"] + 128] = (j > i)
    c[:, CS["uF"]:CS["uF"] + 128] = (j <= i) * (-1.0 / 16.0)
    c[:, CS["uB"]:CS["uB"] + 128] = (j >= i) * (-1.0 / 16.0)
    c[:, CS["idxF"]:CS["idxF"] + 128] = (i + 1.0) * np.ones((128, 1))
    c[:, CS["idxB"]:CS["idxB"] + 128] = (128.0 - i) * np.ones((128, 1))
    p96 = np.zeros((128, 96), np.float32)
    for base in (64, 80):
        for t in range(8):
            p96[base + 8 + t, base + t] = -1.0
            p96[base + t, base + 8 + t] = 1.0
    c[:, CS["p96"]:CS["p96"] + 96] = p96
    es = np.zeros((128, 96), np.float32)
    for t in range(32):
        es[t, 64 + t] = 1.0
    c[:, CS["esel"]:CS["esel"] + 96] = es
    s65 = np.zeros((128, 64), np.float32)
    s65[64, :] = 1.0
    c[:, CS["sel65"]:CS["sel65"] + 64] = s65
    return c


def _tables(S_, CTX):
    NT = CTX + S_
    f32 = np.float32
    t = np.arange(S_)
    row = (t // 64).astype(f32)
    colp = (t % 64).astype(f32)
    inv = (f32(10000.0) ** (-(np.arange(8, dtype=f32) * f32(2.0) / f32(16)))).astype(f32)
    mc = np.ones((96, NT), f32)
    ms = np.zeros((96, NT), f32)
    ar = (row[:, None] * inv[None, :]).astype(f32)
    ac = (colp[:, None] * inv[None, :]).astype(f32)
    for i in range(8):
        for base, ang in ((64, ar), (80, ac)):
            mc[base + i, CTX:] = np.cos(ang[:, i]); mc[base + 8 + i, CTX:] = np.cos(ang[:, i])
            ms[base + i, CTX:] = np.sin(ang[:, i]); ms[base + 8 + i, CTX:] = np.sin(ang[:, i])
    pos = np.arange(NT).astype(f32)
    rinv = (f32(1.0) / (f32(10000.0) ** np.linspace(0.0, 1.0, 64, dtype=f32))).astype(f32)
    ang = (pos[:, None] * rinv[None, :]).astype(f32)
    rc = np.concatenate([np.cos(ang), np.cos(ang)], 1).T.astype(f32)
    rs_ = np.concatenate([np.sin(ang), np.sin(ang)], 1).T.astype(f32)
    return np.ascontiguousarray(mc), np.ascontiguousarray(ms), np.ascontiguousarray(rc), np.ascontiguousarray(rs_)


def _pack_pp(inp, L):
    pp = np.zeros((L, 128, NPP), np.float32)
    for l in range(L):
        def put(name, arr):
            arr = np.asarray(arr, np.float32)
            pp[l, :arr.shape[0], PP[name]:PP[name] + arr.shape[1]] = arr
        put("bada", _pc(inp["b_ada"][l], 48))
        put("n1w", _pc(inp["norm1_w"][l], 8))
        put("n2w", _pc(inp["norm2_w"][l], 8))
        put("bgate", _pc(np.asarray(inp["b_gate"][l]).reshape(-1), 24))
        put("qna", _pc(inp["mla_q_norm_a"][l], 2))
        put("kvna", _pc(inp["mla_kv_norm_a"][l], 1))
        put("qn", np.asarray(inp["mla_q_norm"][l]).reshape(96, 1))
        put("kn", np.asarray(inp["mla_k_norm"][l]).reshape(96, 1))
        put("gon", _pc(inp["gla_o_norm"][l], 1))
        put("wdw", _pc(np.asarray(inp["w_dw"][l]).reshape(-1), 66))
        put("bdw", _pc(inp["b_dw"][l], 22))
        put("retdec", np.broadcast_to(np.asarray(inp["ret_decay"][l]).reshape(1, 8), (128, 8)))
        put("bgk", np.broadcast_to(np.asarray(inp["gla_b_gk"][l]).reshape(1, 1024), (128, 1024)))
    return pp


_NC_CACHE = {}


def _run(inp, S_, CTX, L, ncores, debug=False, nphase=999):
    key = (S_, CTX, L, debug, nphase)
    if key not in _NC_CACHE:
        _NC_CACHE[key] = build(S_, CTX, L, debug, nphase)
    nc = _NC_CACHE[key]
    B = np.asarray(inp["x"]).shape[0]
    mc, ms, rc, rs_ = _tables(S_, CTX)
    shared = {
        "pp": _pack_pp(inp, L), "cst": _consts(), "mlaC": mc, "mlaS": ms, "retC": rc, "retS": rs_,
    }
    for k in ("w_ada", "w_in", "mla_w_qb", "mla_w_kvb", "gla_w_gk2", "w_branch", "w_out", "w_ffn_in", "w_ffn_out"):
        shared[k] = np.ascontiguousarray(np.asarray(inp[k], np.float32))
    cc = _pc(inp["c_ctx"], 8)
    in_maps = []
    for core in range(ncores):
        b = core % B
        m = dict(shared)
        m["xT"] = np.ascontiguousarray(np.asarray(inp["x"][b], np.float32).T)
        m["ctxT"] = np.ascontiguousarray(np.asarray(inp["ctx"][b], np.float32).T)
        m["cvec"] = np.ascontiguousarray(np.concatenate([_pc(inp["c"][b], 8), cc], 1))
        in_maps.append(m)
    res = run_bass_kernel_spmd(nc, in_maps, core_ids=list(range(ncores)))
    return res


def kernel(**inputs):
    x = np.asarray(inputs["x"])
    B, S_, _ = x.shape
    CTX = np.asarray(inputs["ctx"]).shape[1]
    L = np.asarray(inputs["w_ada"]).shape[0]
    res = _run(inputs, S_, CTX, L, 8)
    out = np.empty((B, S_, D), np.float32)
    for b in range(B):
        out[b] = res.results[b]["outT"].T
    return out
```
